# Optimizing a Trainium2 kernel written in Bass

```python
import math
import jax, jax.numpy as jnp
from jax import lax
import numpy as np

D_MODEL = 1024
BATCH = 32
SEQ = 2048
DEPTH = 4
DEC_BATCH = 16
DEC_SEQ = 32
PAST_LEN = 1024

CHUNK = 64
N_MIXERS = 3
N_GDN = (DEPTH + 2) // 3
N_MLA = (DEPTH + 1) // 3
N_SWA = DEPTH // 3
EPS = 1e-6

GDN_HEADS = 8
GDN_DK = 128
GDN_DV = 128
GDN_CONV = 4
GDN_QKV = GDN_HEADS * (2 * GDN_DK + GDN_DV)
GDN_IN = GDN_QKV + GDN_HEADS * GDN_DV + 2 * GDN_HEADS

MLA_HEADS = 16
MLA_Q_LORA = 384
MLA_KV_LORA = 256
MLA_NOPE = 64
MLA_ROPE = 32
MLA_V = 64
MLA_IN = MLA_Q_LORA + MLA_KV_LORA + MLA_ROPE
ROPE_THETA = 10000.0
Q_BLOCK = 128

SWA_HEADS = 16
SWA_KV_HEADS = 4
SWA_GROUP = SWA_HEADS // SWA_KV_HEADS
SWA_HD = 64
WINDOW = 128
WIN_CHUNKS = WINDOW // CHUNK
SWA_IN = (SWA_HEADS + 2 * SWA_KV_HEADS) * SWA_HD

REL_BUCKETS = 32
REL_MAX_DIST = 128

D_FF = 2816
FFN_CONV = 3

kernel_name = 'hybrid_streaming_encoder_step'

F32 = jnp.float32


def rmsnorm(x, g):
    xf = x.astype(F32)
    y = xf * lax.rsqrt(jnp.mean(xf * xf, axis=-1, keepdims=True) + EPS)
    return (y * g.astype(F32)).astype(x.dtype)


def l2norm(x):
    xf = x.astype(F32)
    return xf * lax.rsqrt(jnp.sum(xf * xf, axis=-1, keepdims=True) + EPS)


def causal_dwconv(x, hist, w):
    K = w.shape[0]
    T = x.shape[1]
    xp = jnp.concatenate([hist.astype(x.dtype), x], axis=1)
    y = xp[:, 0:T] * w[0]
    for j in range(1, K):
        y = y + xp[:, j:j + T] * w[j]
    return y, xp[:, -(K - 1):]


def rope(x, pos):
    half = x.shape[-1] // 2
    inv = ROPE_THETA ** (-jnp.arange(half, dtype=F32) / half)
    ang = pos.astype(F32)[:, None] * inv[None, :]
    if x.ndim == 4:
        ang = ang[:, None, :]
    cos, sin = jnp.cos(ang), jnp.sin(ang)
    xf = x.astype(F32)
    x1, x2 = xf[..., :half], xf[..., half:]
    return jnp.concatenate([x1 * cos - x2 * sin, x1 * sin + x2 * cos], axis=-1).astype(x.dtype)


def t5_bias(table, q_pos, k_pos):
    n = q_pos[:, None] - k_pos[None, :]
    half = REL_BUCKETS // 2
    exact = half // 2
    side = jnp.where(n < 0, half, 0)
    n = jnp.abs(n)
    log_b = exact + (jnp.log(jnp.maximum(n, 1).astype(F32) / exact)
                     / math.log(REL_MAX_DIST / exact) * (half - exact)).astype(jnp.int32)
    bucket = side + jnp.where(n < exact, n, jnp.minimum(log_b, half - 1))
    bias = table[bucket].astype(F32).transpose(2, 0, 1)
    return bias.reshape(SWA_KV_HEADS, SWA_GROUP, q_pos.shape[0], k_pos.shape[0])


def gated_delta_rule(q, k, v, g, beta, S0, chunk):
    B, T, H, dk = q.shape
    dv = v.shape[-1]
    nc = T // chunk

    def blk4(a):
        return a.astype(F32).reshape(B, nc, chunk, H, a.shape[-1]).transpose(1, 0, 3, 2, 4)

    def blk3(a):
        return a.astype(F32).reshape(B, nc, chunk, H).transpose(1, 0, 3, 2)

    qc, kc, vc = blk4(q) * (dk ** -0.5), blk4(k), blk4(v)
    gc = jnp.cumsum(blk3(g), axis=-1)
    bc = blk3(beta)
    idx = jnp.arange(chunk)
    causal = idx[:, None] >= idx[None, :]
    decay = jnp.exp(jnp.where(causal, gc[..., :, None] - gc[..., None, :], -jnp.inf))
    kb = kc * bc[..., None]
    A = jnp.where(idx[:, None] > idx[None, :],
                  jnp.einsum('nbhid,nbhjd->nbhij', kb, kc) * decay, 0.0)
    eye = jnp.eye(chunk, dtype=F32)
    rhs = jnp.concatenate([vc * bc[..., None], kb * jnp.exp(gc)[..., None]], axis=-1)
    sol = lax.linalg.triangular_solve(eye + A, rhs, left_side=True, lower=True, unit_diagonal=True)
    w_v, w_k = sol[..., :dv], sol[..., dv:]
    qk = jnp.einsum('nbhid,nbhjd->nbhij', qc, kc) * decay

    def step(S, inp):
        qi, ki, wv, wk, qki, gi = inp
        u = wv - jnp.einsum('bhik,bhkv->bhiv', wk, S)
        o = (jnp.einsum('bhik,bhkv->bhiv', qi * jnp.exp(gi)[..., None], S)
             + jnp.einsum('bhij,bhjv->bhiv', qki, u))
        gl = gi[..., -1:]
        S = (S * jnp.exp(gl)[..., None]
             + jnp.einsum('bhik,bhiv->bhkv', ki * jnp.exp(gl - gi)[..., None], u))
        return S, o

    S, o = lax.scan(step, S0, (qc, kc, w_v, w_k, qk, gc))
    o = o.transpose(1, 0, 3, 2, 4).reshape(B, T, H, dv)
    return o, S


def gdn_mixer(h, hist, S0, w_in, conv_w, a_log, dt_bias, o_norm, w_o, chunk):
    B, T, _ = h.shape
    H, dk, dv = GDN_HEADS, GDN_DK, GDN_DV
    proj = h @ w_in
    qkv = proj[..., :GDN_QKV]
    z = proj[..., GDN_QKV:GDN_QKV + H * dv].reshape(B, T, H, dv)
    a = proj[..., GDN_QKV + H * dv:GDN_QKV + H * dv + H]
    b = proj[..., GDN_QKV + H * dv + H:]
    conv, new_hist = causal_dwconv(qkv, hist, conv_w)
    conv = jax.nn.silu(conv)
    q = l2norm(conv[..., :H * dk].reshape(B, T, H, dk))
    k = l2norm(conv[..., H * dk:2 * H * dk].reshape(B, T, H, dk))
    v = conv[..., 2 * H * dk:].reshape(B, T, H, dv)
    g = -jnp.exp(a_log.astype(F32)) * jax.nn.softplus(a.astype(F32) + dt_bias.astype(F32))
    beta = jax.nn.sigmoid(b.astype(F32))
    o, S = gated_delta_rule(q, k, v, g, beta, S0.astype(F32), chunk)
    o = rmsnorm(o, o_norm) * jax.nn.silu(z.astype(F32))
    out = o.reshape(B, T, H * dv).astype(h.dtype) @ w_o
    return out, new_hist, S.astype(h.dtype)


def mla_project(h, pos, w_in, q_norm, kv_norm, w_q_up):
    B, T, _ = h.shape
    proj = h @ w_in
    cq = rmsnorm(proj[..., :MLA_Q_LORA], q_norm)
    ckv = rmsnorm(proj[..., MLA_Q_LORA:MLA_Q_LORA + MLA_KV_LORA], kv_norm)
    kr = rope(proj[..., MLA_Q_LORA + MLA_KV_LORA:], pos)
    q = (cq @ w_q_up).reshape(B, T, MLA_HEADS, MLA_NOPE + MLA_ROPE)
    return q[..., :MLA_NOPE], rope(q[..., MLA_NOPE:], pos), ckv, kr


def mla_attend(qn, qr, q_pos, ckv, kr, k_pos, w_kv_up, w_o):
    B, Tq = qn.shape[:2]
    Tk = ckv.shape[1]
    kv = (ckv @ w_kv_up).reshape(B, Tk, MLA_HEADS, MLA_NOPE + MLA_V)
    k_nope, v = kv[..., :MLA_NOPE], kv[..., MLA_NOPE:]
    scale = (MLA_NOPE + MLA_ROPE) ** -0.5
    k_chunk = k_pos // CHUNK

    def block(args):
        qnb, qrb, qpb = args
        s = (jnp.einsum('bqhd,bkhd->bhqk', qnb, k_nope)
             + jnp.einsum('bqhr,bkr->bhqk', qrb, kr)).astype(F32) * scale
        mask = k_chunk[None, :] <= (qpb // CHUNK)[:, None]
        p = jax.nn.softmax(jnp.where(mask, s, -jnp.inf), axis=-1)
        return jnp.einsum('bhqk,bkhd->bqhd', p.astype(v.dtype), v)

    qb = min(Q_BLOCK, Tq)
    nb = Tq // qb
    qn_b = qn.reshape(B, nb, qb, MLA_HEADS, MLA_NOPE).transpose(1, 0, 2, 3, 4)
    qr_b = qr.reshape(B, nb, qb, MLA_HEADS, MLA_ROPE).transpose(1, 0, 2, 3, 4)
    o = lax.map(block, (qn_b, qr_b, q_pos.reshape(nb, qb)))
    o = o.transpose(1, 0, 2, 3, 4).reshape(B, Tq, MLA_HEADS * MLA_V)
    return o @ w_o


def swa_project(h, w_in):
    B, T, _ = h.shape
    proj = h @ w_in
    nq, nk = SWA_HEADS * SWA_HD, SWA_KV_HEADS * SWA_HD
    q = proj[..., :nq].reshape(B, T, SWA_KV_HEADS, SWA_GROUP, SWA_HD)
    k = proj[..., nq:nq + nk].reshape(B, T, SWA_KV_HEADS, SWA_HD)
    v = proj[..., nq + nk:].reshape(B, T, SWA_KV_HEADS, SWA_HD)
    return q, k, v


def sink_attend(q, k, v, bias, mask, sinks):
    s = jnp.einsum('bnqhgd,bnshd->bnhgqs', q, k).astype(F32) * (SWA_HD ** -0.5) + bias
    s = jnp.where(mask[:, None, None], s, -jnp.inf)
    sink = sinks.astype(F32).reshape(SWA_KV_HEADS, SWA_GROUP, 1, 1)
    m = jnp.maximum(jnp.max(s, axis=-1, keepdims=True), sink)
    p = jnp.exp(s - m)
    p = p / (jnp.sum(p, axis=-1, keepdims=True) + jnp.exp(sink - m))
    return jnp.einsum('bnhgqs,bnshd->bnqhgd', p.astype(v.dtype), v)


def band(a):
    B, T = a.shape[:2]
    nc = T // CHUNK
    ac = a.reshape(B, nc, CHUNK, SWA_KV_HEADS, SWA_HD)
    ap = jnp.pad(ac, ((0, 0), (WIN_CHUNKS, 0), (0, 0), (0, 0), (0, 0)))
    return jnp.concatenate([ap[:, j:j + nc] for j in range(WIN_CHUNKS + 1)], axis=2)


def swa_prompt(q, k, v, rel_bias, sinks):
    B, T = q.shape[:2]
    nc = T // CHUNK
    span = (WIN_CHUNKS + 1) * CHUNK
    qb = q.reshape(B, nc, CHUNK, SWA_KV_HEADS, SWA_GROUP, SWA_HD)
    q_rel = WIN_CHUNKS * CHUNK + jnp.arange(CHUNK)
    k_rel = jnp.arange(span)
    bias = t5_bias(rel_bias, q_rel, k_rel)
    k_chunk = jnp.arange(nc)[:, None] - WIN_CHUNKS + (k_rel // CHUNK)[None, :]
    mask = jnp.broadcast_to((k_chunk >= 0)[:, None, :], (nc, CHUNK, span))
    o = sink_attend(qb, band(k), band(v), bias, mask, sinks)
    return o.reshape(B, T, SWA_HEADS * SWA_HD)


def swa_sample(q, q_pos, k_all, v_all, k_pos, rel_bias, sinks):
    B, T = q.shape[:2]
    d = (q_pos // CHUNK)[:, None] - (k_pos // CHUNK)[None, :]
    mask = ((d >= 0) & (d <= WIN_CHUNKS))[None]
    bias = t5_bias(rel_bias, q_pos, k_pos)
    o = sink_attend(q[:, None], k_all[:, None], v_all[:, None], bias, mask, sinks)
    return o.reshape(B, T, SWA_HEADS * SWA_HD)


def conv_ffn(h, hist, w_in, conv_w, conv_b, w_out):
    proj = h @ w_in
    gate, up = proj[..., :D_FF], proj[..., D_FF:]
    gate_c, new_hist = causal_dwconv(gate, hist, conv_w)
    return (jax.nn.silu(gate_c + conv_b) * up) @ w_out, new_hist


def trunk(x, c, prompt, past_len, st, p):
    B, T, _ = x.shape
    pos = jnp.arange(T, dtype=jnp.int32) + (0 if prompt else past_len)
    new = dict(gdn_conv=[], gdn_S=[], mla_latent=[], mla_krope=[], swa_k=[], swa_v=[], ffn_conv=[])
    cs = jax.nn.silu(c)
    for layer in range(DEPTH):
        kind, slot = layer % N_MIXERS, layer // N_MIXERS
        mod = cs @ p['ada_w'][layer] + p['ada_b'][layer]
        sh1, sc1, g1, sh2, sc2, g2 = jnp.split(mod[:, None, :], 6, axis=-1)
        h = rmsnorm(x, p['norm1'][layer]) * (1 + sc1) + sh1
        if kind == 0:
            mix, conv_h, S = gdn_mixer(h, st['gdn_conv'][slot], st['gdn_S'][slot], p['gdn_w_in'][slot],
                                       p['gdn_conv_w'][slot], p['gdn_a_log'][slot], p['gdn_dt_bias'][slot],
                                       p['gdn_o_norm'][slot], p['gdn_w_o'][slot], CHUNK if prompt else T)
            new['gdn_conv'].append(conv_h)
            new['gdn_S'].append(S)
        elif kind == 1:
            qn, qr, ckv, kr = mla_project(h, pos, p['mla_w_in'][slot], p['mla_q_norm'][slot],
                                          p['mla_kv_norm'][slot], p['mla_w_q_up'][slot])
            if prompt:
                ckv_all, kr_all, k_pos = ckv, kr, pos
            else:
                ckv_all = jnp.concatenate([st['mla_latent'][slot].astype(ckv.dtype), ckv], axis=1)
                kr_all = jnp.concatenate([st['mla_krope'][slot].astype(kr.dtype), kr], axis=1)
                k_pos = jnp.arange(past_len + T, dtype=jnp.int32)
            mix = mla_attend(qn, qr, pos, ckv_all, kr_all, k_pos, p['mla_w_kv_up'][slot], p['mla_w_o'][slot])
            new['mla_latent'].append(ckv)
            new['mla_krope'].append(kr)
        else:
            q, k, v = swa_project(h, p['swa_w_in'][slot])
            if prompt:
                heads = swa_prompt(q, k, v, p['rel_bias'], p['swa_sinks'][slot])
                win = min(WINDOW, T)
                new_k, new_v = k[:, -win:], v[:, -win:]
            else:
                win = st['swa_k'].shape[2]
                k_all = jnp.concatenate([st['swa_k'][slot].astype(k.dtype), k], axis=1)
                v_all = jnp.concatenate([st['swa_v'][slot].astype(v.dtype), v], axis=1)
                k_pos = jnp.concatenate([past_len - win + jnp.arange(win, dtype=jnp.int32), pos])
                heads = swa_sample(q, pos, k_all, v_all, k_pos, p['rel_bias'], p['swa_sinks'][slot])
                new_k, new_v = k_all[:, -win:], v_all[:, -win:]
            mix = heads @ p['swa_w_o'][slot]
            new['swa_k'].append(new_k)
            new['swa_v'].append(new_v)
        x = x + g1 * mix
        h = rmsnorm(x, p['norm2'][layer]) * (1 + sc2) + sh2
        f, f_hist = conv_ffn(h, st['ffn_conv'][layer], p['ffn_w_in'][layer], p['ffn_conv_w'][layer],
                             p['ffn_conv_b'][layer], p['ffn_w_out'][layer])
        new['ffn_conv'].append(f_hist)
        x = x + g2 * f
    y = rmsnorm(x, p['final_norm'])
    return y, {name: jnp.stack(rows) for name, rows in new.items()}


def setup_inputs(seed: int = 0) -> dict:
    key = jax.random.key(seed)
    ks = iter(jax.random.split(key, 48))

    def nrm(shape, scale=1.0):
        return jax.random.normal(next(ks), shape, F32) * scale

    def gain(shape):
        return 1.0 + nrm(shape, 0.02)

    win = min(WINDOW, PAST_LEN)
    a_log = jnp.log(jax.random.uniform(next(ks), (N_GDN, GDN_HEADS), F32, minval=1.0, maxval=16.0))
    dt = jnp.exp(jax.random.uniform(next(ks), (N_GDN, GDN_HEADS), F32,
                                    minval=math.log(1e-3), maxval=math.log(1e-1)))
    dt_bias = dt + jnp.log(-jnp.expm1(-dt))
    return {
        'x_prompt': nrm((BATCH, SEQ, D_MODEL)),
        'x_sample': nrm((DEC_BATCH, DEC_SEQ, D_MODEL)),
        'c_prompt': nrm((BATCH, D_MODEL)),
        'c_sample': nrm((DEC_BATCH, D_MODEL)),
        'state_gdn_conv': nrm((N_GDN, DEC_BATCH, GDN_CONV - 1, GDN_QKV)),
        'state_gdn_S': nrm((N_GDN, DEC_BATCH, GDN_HEADS, GDN_DK, GDN_DV), 0.1),
        'cache_mla_latent': nrm((N_MLA, DEC_BATCH, PAST_LEN, MLA_KV_LORA)),
        'cache_mla_krope': nrm((N_MLA, DEC_BATCH, PAST_LEN, MLA_ROPE)),
        'cache_swa_k': nrm((N_SWA, DEC_BATCH, win, SWA_KV_HEADS, SWA_HD)),
        'cache_swa_v': nrm((N_SWA, DEC_BATCH, win, SWA_KV_HEADS, SWA_HD)),
        'state_ffn_conv': nrm((DEPTH, DEC_BATCH, FFN_CONV - 1, D_FF)),
        'ada_w': nrm((DEPTH, D_MODEL, 6 * D_MODEL), 0.02),
        'ada_b': nrm((DEPTH, 6 * D_MODEL), 0.02),
        'norm1': gain((DEPTH, D_MODEL)),
        'norm2': gain((DEPTH, D_MODEL)),
        'final_norm': gain((D_MODEL,)),
        'gdn_w_in': nrm((N_GDN, D_MODEL, GDN_IN), D_MODEL ** -0.5),
        'gdn_conv_w': nrm((N_GDN, GDN_CONV, GDN_QKV), GDN_CONV ** -0.5),
        'gdn_a_log': a_log,
        'gdn_dt_bias': dt_bias,
        'gdn_o_norm': gain((N_GDN, GDN_DV)),
        'gdn_w_o': nrm((N_GDN, GDN_HEADS * GDN_DV, D_MODEL), (GDN_HEADS * GDN_DV) ** -0.5),
        'mla_w_in': nrm((N_MLA, D_MODEL, MLA_IN), D_MODEL ** -0.5),
        'mla_q_norm': gain((N_MLA, MLA_Q_LORA)),
        'mla_kv_norm': gain((N_MLA, MLA_KV_LORA)),
        'mla_w_q_up': nrm((N_MLA, MLA_Q_LORA, MLA_HEADS * (MLA_NOPE + MLA_ROPE)), MLA_Q_LORA ** -0.5),
        'mla_w_kv_up': nrm((N_MLA, MLA_KV_LORA, MLA_HEADS * (MLA_NOPE + MLA_V)), MLA_KV_LORA ** -0.5),
        'mla_w_o': nrm((N_MLA, MLA_HEADS * MLA_V, D_MODEL), (MLA_HEADS * MLA_V) ** -0.5),
        'swa_w_in': nrm((N_SWA, D_MODEL, SWA_IN), D_MODEL ** -0.5),
        'swa_sinks': nrm((N_SWA, SWA_HEADS)),
        'swa_w_o': nrm((N_SWA, SWA_HEADS * SWA_HD, D_MODEL), (SWA_HEADS * SWA_HD) ** -0.5),
        'rel_bias': nrm((REL_BUCKETS, SWA_HEADS), 0.5),
        'ffn_w_in': nrm((DEPTH, D_MODEL, 2 * D_FF), D_MODEL ** -0.5),
        'ffn_conv_w': nrm((DEPTH, FFN_CONV, D_FF), FFN_CONV ** -0.5),
        'ffn_conv_b': nrm((DEPTH, D_FF), 0.02),
        'ffn_w_out': nrm((DEPTH, D_FF, D_MODEL), D_FF ** -0.5),
    }


def reference(x_prompt, x_sample, c_prompt, c_sample, state_gdn_conv, state_gdn_S, cache_mla_latent,
              cache_mla_krope, cache_swa_k, cache_swa_v, state_ffn_conv, ada_w, ada_b, norm1, norm2,
              final_norm, gdn_w_in, gdn_conv_w, gdn_a_log, gdn_dt_bias, gdn_o_norm, gdn_w_o, mla_w_in,
              mla_q_norm, mla_kv_norm, mla_w_q_up, mla_w_kv_up, mla_w_o, swa_w_in, swa_sinks, swa_w_o,
              rel_bias, ffn_w_in, ffn_conv_w, ffn_conv_b, ffn_w_out):
    p = dict(ada_w=ada_w, ada_b=ada_b, norm1=norm1, norm2=norm2, final_norm=final_norm,
             gdn_w_in=gdn_w_in, gdn_conv_w=gdn_conv_w, gdn_a_log=gdn_a_log, gdn_dt_bias=gdn_dt_bias,
             gdn_o_norm=gdn_o_norm, gdn_w_o=gdn_w_o, mla_w_in=mla_w_in, mla_q_norm=mla_q_norm,
             mla_kv_norm=mla_kv_norm, mla_w_q_up=mla_w_q_up, mla_w_kv_up=mla_w_kv_up, mla_w_o=mla_w_o,
             swa_w_in=swa_w_in, swa_sinks=swa_sinks, swa_w_o=swa_w_o, rel_bias=rel_bias,
             ffn_w_in=ffn_w_in, ffn_conv_w=ffn_conv_w, ffn_conv_b=ffn_conv_b, ffn_w_out=ffn_w_out)
    bp = x_prompt.shape[0]
    st_prompt = dict(gdn_conv=jnp.zeros((N_GDN, bp, GDN_CONV - 1, GDN_QKV), x_prompt.dtype),
                     gdn_S=jnp.zeros((N_GDN, bp, GDN_HEADS, GDN_DK, GDN_DV), F32),
                     ffn_conv=jnp.zeros((DEPTH, bp, FFN_CONV - 1, D_FF), x_prompt.dtype))
    st_sample = dict(gdn_conv=state_gdn_conv, gdn_S=state_gdn_S, mla_latent=cache_mla_latent,
                     mla_krope=cache_mla_krope, swa_k=cache_swa_k, swa_v=cache_swa_v,
                     ffn_conv=state_ffn_conv)
    past_len = cache_mla_latent.shape[2]
    y_prompt, sp = trunk(x_prompt, c_prompt, True, 0, st_prompt, p)
    y_sample, ss = trunk(x_sample, c_sample, False, past_len, st_sample, p)
    return (y_prompt, y_sample,
            sp['gdn_conv'], ss['gdn_conv'],
            sp['gdn_S'], ss['gdn_S'],
            sp['mla_latent'], ss['mla_latent'],
            sp['mla_krope'], ss['mla_krope'],
            sp['swa_k'], ss['swa_k'],
            sp['swa_v'], ss['swa_v'],
            sp['ffn_conv'], ss['ffn_conv'])
```

```python
import math
import numpy as np
import concourse.bass as bass
import concourse.mybir as mybir
from concourse.bass_utils import run_bass_kernel_spmd

F32 = mybir.dt.float32
BF16 = mybir.dt.bfloat16
AF = mybir.ActivationFunctionType
ALU = mybir.AluOpType

NCORES = 8
D = 1024
KC = 8
DEPTH = 4
TP = 2048
TS = 32
PAST = 1024
NPS = 4
NSS = 2
NSEQ = NPS + NSS
EPS = 1e-6
DFF = 2816
NFC = 22
NEG = -30000.0
NLAYERS_DBG = [4]
ARENA_USE = {}
DBG = {'mixer': True, 'ffn': True, 'gdn_stop': 99}
SEQS_DBG = [0, 1, 2, 3, 4, 5]

COMPUTE = ('pe', 'act', 'dve', 'pool')
DMAQ = ('sp', 'gq')
NDSEM = 8


class Op:
    __slots__ = ('stream', 'idx', 'fn', 'is_dma', 'q', 'deps', 'signal', 'sig', 'dseq')

    def __init__(self):
        self.signal = False
        self.sig = None
        self.deps = []


class Prog:
    def __init__(self, nc):
        self.nc = nc
        self.streams = {s: [] for s in ('pe', 'act', 'dve', 'pool', 'sp')}
        self.last_w = {}
        self.readers = {}
        self.known = {s: {} for s in self.streams}
        self.known_dma = {s: set() for s in self.streams}
        self.dma_count = {q: 0 for q in DMAQ}
        self.dma_ops = {q: [] for q in DMAQ}

    @staticmethod
    def _stream_of(eng):
        return 'pool' if eng == 'gq' else eng

    muted = False

    def op(self, eng, fn, reads=(), writes=()):
        if self.muted:
            return None
        o = Op()
        o.is_dma = eng in DMAQ
        o.q = eng if o.is_dma else None
        o.stream = self._stream_of(eng)
        o.fn = fn
        st = self.streams[o.stream]
        o.idx = len(st)
        cand = []
        comp = not o.is_dma
        for key in reads:
            w = self.last_w.get(key)
            if w is not None:
                if comp and (not w.is_dma) and w.stream == 'pe' and o.stream == 'pe':
                    continue
                cand.append(w)
            if isinstance(key, str) and key.startswith('ps') and key[2:].isdigit():
                for r in self.readers.get(key, ()):
                    if r.stream != o.stream:
                        cand.append(r)
        for key in writes:
            w = self.last_w.get(key)
            if w is not None:
                if not (comp and (not w.is_dma) and w.stream == o.stream and o.stream == 'pe'):
                    cand.append(w)
            for r in self.readers.get(key, ()):
                if comp and (not r.is_dma) and r.stream == o.stream and o.stream == 'pe':
                    continue
                cand.append(r)
        if o.is_dma:
            n = self.dma_count[o.q]
            o.dseq = n
            if n >= NDSEM:
                cand.append(self.dma_ops[o.q][n - NDSEM])
            self.dma_count[o.q] = n + 1
            self.dma_ops[o.q].append(o)
            o.signal = True
        best = {}
        dl = []
        kd = self.known_dma[o.stream]
        kn = self.known[o.stream]
        for d in cand:
            if d is o:
                continue
            if d.is_dma:
                if id(d) not in kd:
                    kd.add(id(d))
                    dl.append(d)
            else:
                b = best.get(d.stream)
                if b is None or d.idx > b.idx:
                    best[d.stream] = d
        for s, d in best.items():
            if kn.get(s, -1) < d.idx:
                kn[s] = d.idx
                d.signal = True
                dl.append(d)
        o.deps = dl
        for key in reads:
            self.readers.setdefault(key, []).append(o)
        for key in writes:
            self.last_w[key] = o
            self.readers[key] = []
        st.append(o)
        return o

    def barrier(self):
        streams = COMPUTE
        lasts = []
        for s in streams:
            for o in reversed(self.streams[s]):
                if (not o.is_dma) and o.fn is not None:
                    lasts.append(o)
                    break
        dmas = list(self.dma_ops['gq'][-NDSEM:])
        for s in streams:
            o = Op()
            o.is_dma = False
            o.q = None
            o.stream = s
            o.fn = None
            o.idx = len(self.streams[s])
            for l in lasts:
                if l.stream == s and s == 'pe':
                    continue
                if self.known[s].get(l.stream, -1) < l.idx:
                    self.known[s][l.stream] = l.idx
                    l.signal = True
                    o.deps.append(l)
            for d in dmas:
                if id(d) not in self.known_dma[s]:
                    self.known_dma[s].add(id(d))
                    o.deps.append(d)
            self.streams[s].append(o)

    def emit(self):
        nc = self.nc
        ctxs = [nc.semaphore('sem_' + s) for s in COMPUTE]
        for q in DMAQ:
            ctxs += [nc.semaphore('dsem_%s_%d' % (q, i)) for i in range(NDSEM)]
        handles = [c.__enter__() for c in ctxs]
        hs = {s: handles[i] for i, s in enumerate(COMPUTE)}
        dh = {}
        i = len(COMPUTE)
        for q in DMAQ:
            dh[q] = handles[i:i + NDSEM]
            i += NDSEM
        for s in COMPUTE:
            cnt = 0
            for o in self.streams[s]:
                if o.is_dma or o.fn is None:
                    continue
                if o.signal:
                    cnt += 1
                    o.sig = (hs[s], cnt, 1)
        for q in DMAQ:
            for o in self.dma_ops[q]:
                o.sig = (dh[q][o.dseq % NDSEM], 16 * (o.dseq // NDSEM + 1), 16)
        with nc.Block() as block:
            def run(s, e):
                for o in self.streams[s]:
                    for d in o.deps:
                        e.wait_ge(d.sig[0], d.sig[1])
                    if o.fn is None:
                        continue
                    ins = o.fn()
                    if o.signal:
                        ins.then_inc(o.sig[0], o.sig[2])

            @block.tensor
            def _(e):
                run('pe', e)

            @block.scalar
            def _(e):
                run('act', e)

            @block.vector
            def _(e):
                run('dve', e)

            @block.gpsimd
            def _(e):
                run('pool', e)

            @block.sync
            def _(e):
                run('sp', e)
        for c in reversed(ctxs):
            c.__exit__(None, None, None)


class Tl:
    __slots__ = ('ap', 'key')

    def __init__(self, ap, key):
        self.ap = ap
        self.key = key

    def __getitem__(self, idx):
        return self.ap[idx]


def _keys(lst):
    out = []
    for x in lst:
        if x is None:
            continue
        out.append(x.key if isinstance(x, Tl) else x)
    return out


def _t5_bucket(n):
    n = np.asarray(n, dtype=np.int64)
    half, exact = 16, 8
    side = np.where(n < 0, half, 0)
    na = np.abs(n)
    val = (np.log(np.maximum(na, 1).astype(np.float64) / exact) / math.log(128 / exact) * (half - exact))
    r = np.round(val)
    val = np.where(np.abs(val - r) < 1e-5, r, val)
    log_b = exact + np.floor(val).astype(np.int64)
    log_b = np.where(val < 0, exact + np.ceil(val).astype(np.int64), log_b)
    return side + np.where(na < exact, na, np.minimum(log_b, half - 1))


def _rope_tables(pos):
    half = 16
    inv = (10000.0 ** (-(np.arange(half, dtype=np.float32)) / np.float32(half))).astype(np.float32)
    ang = (pos.astype(np.float32)[None, :] * inv[:, None]).astype(np.float32)
    c = np.cos(ang.astype(np.float64)).astype(np.float32)
    s = np.sin(ang.astype(np.float64)).astype(np.float32)
    cos = np.concatenate([c, c], 0)
    sin = np.concatenate([-s, s], 0)
    return np.ascontiguousarray(cos), np.ascontiguousarray(sin)


def _gdn_masks(ub, c):
    idx = np.arange(ub)
    same = (idx[:, None] // c) == (idx[None, :] // c)
    tri = (same & (idx[:, None] <= idx[None, :])).astype(np.float32)
    blk = same.astype(np.float32)
    mA = np.where(same & (idx[:, None] > idx[None, :]), 0.0, NEG).astype(np.float32)
    mT = np.where(same & (idx[None, :] >= idx[:, None]), 0.0, NEG).astype(np.float32)
    return tri, blk, mA, mT


def build_program():
    nc = bass.Bass("TRN2", target_bir_lowering=False)
    P = Prog(nc)

    def din(name, shape):
        return nc.dram_tensor(name, list(shape), F32, kind="ExternalInput").ap()

    def dout(name, shape):
        return nc.dram_tensor(name, list(shape), F32, kind="ExternalOutput").ap()

    I = {}
    I['xp'] = din('xp', [NPS, TP, D])
    I['xs'] = din('xs', [NSS, TS, D])
    I['c6'] = din('c6', [NSEQ, D])
    I['st_gconv'] = din('st_gconv', [2, NSS, 3, 3072])
    I['st_gS'] = din('st_gS', [2, NSS, 8, 128, 128])
    I['c_lat'] = din('c_lat', [NSS, PAST, 256])
    I['c_kr'] = din('c_kr', [NSS, PAST, 32])
    I['c_sk'] = din('c_sk', [NSS, 128, 256])
    I['c_sv'] = din('c_sv', [NSS, 128, 256])
    I['st_fconv'] = din('st_fconv', [DEPTH, NSS, 2, DFF])
    I['ada_w'] = din('ada_w', [DEPTH, D, 6 * D])
    I['ada_b'] = din('ada_b', [DEPTH, 6 * D])
    I['norm1'] = din('norm1', [DEPTH, D])
    I['norm2'] = din('norm2', [DEPTH, D])
    I['final_norm'] = din('final_norm', [D])
    I['gdn_w_in'] = din('gdn_w_in', [2, D, 4112])
    I['gdn_conv_w'] = din('gdn_conv_w', [2, 4, 3072])
    I['gdn_a_log'] = din('gdn_a_log', [2, 8])
    I['gdn_dt_bias'] = din('gdn_dt_bias', [2, 8])
    I['gdn_o_norm'] = din('gdn_o_norm', [2, 128])
    I['gdn_w_o'] = din('gdn_w_o', [2, D, D])
    I['mla_w_in'] = din('mla_w_in', [1, D, 672])
    I['mla_q_norm'] = din('mla_q_norm', [1, 384])
    I['mla_kv_norm'] = din('mla_kv_norm', [1, 256])
    I['mla_w_q_up'] = din('mla_w_q_up', [1, 384, 1536])
    I['mla_w_kv_up'] = din('mla_w_kv_up', [1, 256, 2048])
    I['mla_w_o'] = din('mla_w_o', [1, D, D])
    I['swa_w_in'] = din('swa_w_in', [1, D, 1536])
    I['swa_sinks'] = din('swa_sinks', [1, 16])
    I['swa_w_o'] = din('swa_w_o', [1, D, D])
    I['ffn_w_in'] = din('ffn_w_in', [DEPTH, D, 2 * DFF])
    I['ffn_conv_w'] = din('ffn_conv_w', [DEPTH, 3, DFF])
    I['ffn_conv_b'] = din('ffn_conv_b', [DEPTH, DFF])
    I['ffn_w_out'] = din('ffn_w_out', [DEPTH, DFF, D])
    I['k_ident'] = din('k_ident', [128, 128])
    for nm, ub in (('p', 128), ('s', 32)):
        for t in ('tri', 'blk', 'mA', 'mT'):
            I['k_%s_%s' % (t, nm)] = din('k_%s_%s' % (t, nm), [ub, ub])
    I['k_cos_p'] = din('k_cos_p', [32, TP])
    I['k_sin_p'] = din('k_sin_p', [32, TP])
    I['k_cos_s'] = din('k_cos_s', [32, TS])
    I['k_sin_s'] = din('k_sin_s', [32, TS])
    I['k_ba_p'] = din('k_ba_p', [128, 4, 256])
    I['k_ba1_p'] = din('k_ba1_p', [128, 4, 256])
    I['k_bb_p'] = din('k_bb_p', [64, 4, 256])
    I['k_ba_s'] = din('k_ba_s', [128, 4, 128])
    I['k_bb_s'] = din('k_bb_s', [32, 4, 128])

    O = {}
    O['y_p'] = dout('y_p', [NPS, TP, D])
    O['y_s'] = dout('y_s', [NSS, TS, D])
    O['gconv_p'] = dout('gconv_p', [2, NPS, 3, 3072])
    O['gconv_s'] = dout('gconv_s', [2, NSS, 3, 3072])
    O['gS_p'] = dout('gS_p', [2, NPS, 8, 128, 128])
    O['gS_s'] = dout('gS_s', [2, NSS, 8, 128, 128])
    O['lat_p'] = dout('lat_p', [NPS, TP, 256])
    O['lat_s'] = dout('lat_s', [NSS, TS, 256])
    O['kr_p'] = dout('kr_p', [NPS, TP, 32])
    O['kr_s'] = dout('kr_s', [NSS, TS, 32])
    O['sk_p'] = dout('sk_p', [NPS, 128, 256])
    O['sk_s'] = dout('sk_s', [NSS, 128, 256])
    O['sv_p'] = dout('sv_p', [NPS, 128, 256])
    O['sv_s'] = dout('sv_s', [NSS, 128, 256])
    O['fconv_p'] = dout('fconv_p', [DEPTH, NPS, 2, DFF])
    O['fconv_s'] = dout('fconv_s', [DEPTH, NSS, 2, DFF])

    import contextlib
    es = contextlib.ExitStack()
    uid = [0]

    def sb(name, cols, dt=F32, parts=128):
        t = es.enter_context(nc.sbuf_tensor(name, [parts, cols], dt))
        return t

    xT_t = sb('xT', KC * TP, F32)
    xT = Tl(xT_t[:, :].rearrange("p (k t) -> p k t", k=KC), 'xT')
    hT_t = sb('hT', KC * TP, BF16)
    hT = Tl(hT_t[:, :].rearrange("p (k t) -> p k t", k=KC), 'hT')
    WCOLS = 2048
    wst = [Tl(sb('wst%d' % i, WCOLS, F32)[:, :], 'wst%d' % i) for i in range(2)]
    wbf = [Tl(sb('wbf%d' % i, WCOLS, BF16)[:, :], 'wbf%d' % i) for i in range(3)]
    wctr = [0, 0]
    identF = Tl(sb('identF', 128, F32)[:, :], 'identF')
    identB = Tl(sb('identB', 128, BF16)[:, :], 'identB')
    onesB = Tl(sb('onesB', 128, BF16)[:, :], 'onesB')
    onesF = Tl(sb('onesF', 128, F32)[:, :], 'onesF')
    cvec = Tl(sb('cvec', 8, F32)[:, :], 'cvec')
    modT_t = sb('modT', DEPTH * 48 * NSEQ, F32)
    modT = Tl(modT_t[:, :].rearrange("p (l j s) -> p l j s", l=DEPTH, j=48), 'modT')
    am_t = sb('amul', 2 * DEPTH * KC * NSEQ, F32)
    amul = Tl(am_t[:, :].rearrange("p (w l k s) -> p w l k s", w=2, l=DEPTH, k=KC), 'amul')
    nrm_t = sb('nrmw', (2 * DEPTH + 1) * KC, F32)
    nrmw = Tl(nrm_t[:, :].rearrange("p (w k) -> p w k", k=KC), 'nrmw')
    bar_t = Tl(sb('bar', 4, F32)[:, :], 'BAR')
    ARENA = 19000
    arena_t = sb('arena', ARENA, F32)
    apos = [0]
    aphase = [0]

    def aalloc(name, cols, dt=F32, parts=128, shape=None):
        nf = cols if dt == F32 else (cols + 1) // 2
        nf = (nf + 7) // 8 * 8
        assert apos[0] + nf <= ARENA, (name, apos[0], nf)
        ap = arena_t[0:parts, apos[0]:apos[0] + nf]
        apos[0] += nf
        if dt != F32:
            ap = ap.bitcast(dt)
        ap = ap[:, 0:cols]
        if shape is not None:
            ap = ap.rearrange(shape[0], **shape[1])
        return Tl(ap, '%s@%d' % (name, aphase[0]))

    def phase_end():
        P.barrier()
        P.op('dve', lambda: nc.vector.memset(bar_t[:, 0:1], 0.0), writes=['BAR'])
        apos[0] = 0
        aphase[0] += 1

    psb = [Tl(es.enter_context(nc.psum_tensor('ps%d' % i, [128, 512], F32))[:, :], 'ps%d' % i) for i in range(8)]
    psctr = [0]

    psrot = [6]

    def psum():
        t = psb[psctr[0] % psrot[0]]
        psctr[0] += 1
        return t

    ACC0, ACC1 = psb[6], psb[7]
    ACCP = [(psb[4], psb[5]), (psb[6], psb[7])]
    cast_engs = ['act']
    cast_ctr = [0]

    def wcast(bv, sv, s_t, b_t):
        e = cast_engs[cast_ctr[0] % len(cast_engs)]
        cast_ctr[0] += 1
        if e == 'pool':
            P.op('pool', lambda: nc.gpsimd.tensor_copy(out=bv, in_=sv), _keys([s_t]), _keys([b_t]))
        elif e == 'dve':
            P.op('dve', lambda: nc.vector.tensor_copy(out=bv, in_=sv), _keys([s_t]), _keys([b_t]))
        else:
            P.op('act', lambda: nc.scalar.copy(out=bv, in_=sv), _keys([s_t]), _keys([b_t]))

    def mm(out, lhsT, rhs, start=True, stop=True, r=(), w=()):
        return P.op('pe', lambda: nc.tensor.matmul(out, lhsT, rhs, start=start, stop=stop), _keys(r), _keys(w))

    def tr(out, in_, ident, r=(), w=()):
        return P.op('pe', lambda: nc.tensor.transpose(out, in_, ident), _keys(r), _keys(w))

    def act(out, in_, func, bias=None, scale=1.0, r=(), w=()):
        if bias is None:
            return P.op('act', lambda: nc.scalar.activation(out=out, in_=in_, func=func, scale=scale),
                        _keys(r), _keys(w))
        return P.op('act', lambda: nc.scalar.activation(out=out, in_=in_, func=func, bias=bias, scale=scale),
                    _keys(r), _keys(w))

    def tt(out, in0, in1, op, r=(), w=(), eng='dve'):
        e = nc.vector if eng == 'dve' else nc.gpsimd
        return P.op(eng, lambda: e.tensor_tensor(out=out, in0=in0, in1=in1, op=op), _keys(r), _keys(w))

    def ts(out, in0, s1, op0, s2=None, op1=None, r=(), w=(), eng='dve'):
        e = nc.vector if eng == 'dve' else nc.gpsimd
        if op1 is None:
            return P.op(eng, lambda: e.tensor_scalar(out=out, in0=in0, scalar1=s1, scalar2=None, op0=op0),
                        _keys(r), _keys(w))
        return P.op(eng, lambda: e.tensor_scalar(out=out, in0=in0, scalar1=s1, scalar2=s2, op0=op0, op1=op1),
                    _keys(r), _keys(w))

    def stt(out, in0, scalar, in1, op0, op1, r=(), w=()):
        return P.op('dve', lambda: nc.vector.scalar_tensor_tensor(out=out, in0=in0, scalar=scalar, in1=in1,
                                                                  op0=op0, op1=op1), _keys(r), _keys(w))

    def cp(out, in_, r=(), w=(), eng='dve'):
        if eng == 'act':
            return P.op('act', lambda: nc.scalar.copy(out=out, in_=in_), _keys(r), _keys(w))
        e = nc.vector if eng == 'dve' else nc.gpsimd
        return P.op(eng, lambda: e.tensor_copy(out=out, in_=in_), _keys(r), _keys(w))

    def recip(out, in_, r=(), w=()):
        return P.op('dve', lambda: nc.vector.reciprocal(out=out, in_=in_), _keys(r), _keys(w))

    def mset(ap, val, w=(), eng='dve'):
        e = nc.vector if eng == 'dve' else nc.gpsimd
        return P.op(eng, lambda: e.memset(ap, val), (), _keys(w))

    def ld(out, in_, w=(), r=(), slow=False):
        return P.op('sp', lambda: nc.sync.dma_start(out=out, in_=in_, allow_slow_non_contiguous=slow),
                    _keys(r), _keys(w))

    def stq(out, in_, r=(), w=(), slow=False):
        return P.op('gq', lambda: nc.gpsimd.dma_start(out=out, in_=in_, allow_slow_non_contiguous=slow),
                    _keys(r), _keys(w))

    def wload(dram_ap, shape_str, **dims):
        n = 1
        for s in dram_ap.shape[1:]:
            n *= s
        assert n <= WCOLS, dram_ap.shape
        s_t = wst[wctr[0] % 2]
        wctr[0] += 1
        b_t = wbf[wctr[1] % 3]
        wctr[1] += 1
        parts = dram_ap.shape[0]
        sv = s_t.ap[0:parts, 0:n]
        bv = b_t.ap[0:parts, 0:n]
        if shape_str is not None:
            svv = sv.rearrange(shape_str, **dims)
            bvv = bv.rearrange(shape_str, **dims)
        else:
            svv, bvv = sv, bv
        ld(svv, dram_ap, w=[s_t])
        wcast(bv, sv, s_t, b_t)
        return Tl(bvv, b_t.key)

    def rsqrt_from(out_sb, in_ps, mean_scale, rk, wk, tmp):
        act(tmp, in_ps, AF.Ln, bias=cvec[0:in_ps.shape[0], 0:1], scale=mean_scale, r=rk + [cvec], w=wk[1:])
        act(out_sb, tmp, AF.Exp, scale=-0.5, r=wk[1:], w=wk[0:1])

    ld(identF.ap, I['k_ident'], w=[identF])
    cp(identB.ap, identF.ap, r=[identF], w=[identB])
    mset(onesB.ap, 1.0, w=[onesB])
    mset(onesF.ap, 1.0, w=[onesF])
    mset(cvec[:, 0:1], EPS, w=[cvec])
    mset(cvec[:, 1:2], 1.0, w=[cvec])
    mset(cvec[:, 2:3], 0.0, w=[cvec])
    mset(bar_t[:, 0:1], 0.0, w=['BAR'])
    ld(nrmw[:, 0:4, :], I['norm1'].rearrange("l (k p) -> p l k", p=128), w=[nrmw], slow=True)
    ld(nrmw[:, 4:8, :], I['norm2'].rearrange("l (k p) -> p l k", p=128), w=[nrmw], slow=True)
    ld(nrmw[:, 8, :], I['final_norm'].rearrange("(k p) -> p k", p=128), w=[nrmw], slow=True)
    csT = aalloc('csT', KC * NSEQ, F32, shape=("p (k s) -> p k s", dict(k=KC)))
    adab = aalloc('adab', DEPTH * 48, F32, shape=("p (l j) -> p l j", dict(l=DEPTH)))
    for k in range(KC):
        ld(csT[:, k, :], I['c6'][:, k * 128:(k + 1) * 128].rearrange("s p -> p s"), w=[csT], slow=True)
    for l in range(DEPTH):
        ld(adab[:, l, :], I['ada_b'][l].rearrange("(j p) -> p j", p=128), w=[adab], slow=True)
    act(csT.ap, csT.ap, AF.Silu, r=[csT], w=[csT])
    for l in range(DEPTH):
        for ct in range(24):
            s_t = wst[wctr[0] % 2]
            wctr[0] += 1
            sv = s_t.ap[:, 0:2048].rearrange("p (k c) -> p k c", k=KC)
            ld(sv, I['ada_w'][l][:, ct * 256:(ct + 1) * 256].rearrange("(k p) c -> p k c", p=128), w=[s_t])
            for fc in range(2):
                ps = psum()
                for k in range(KC):
                    mm(ps[:, 0:NSEQ], sv[:, k, fc * 128:(fc + 1) * 128], csT[:, k, :], start=(k == 0),
                       stop=(k == KC - 1), r=[s_t, csT], w=[ps])
                j = ct * 2 + fc
                ts(modT[:, l, j, :], ps[:, 0:NSEQ], adab[:, l, j:j + 1], ALU.add, r=[ps, adab], w=[modT])
    for l in range(DEPTH):
        for wi, j0 in ((0, 8), (1, 32)):
            for k in range(KC):
                ts(amul[:, wi, l, k, :], modT[:, l, j0 + k, :], 1.0, ALU.add, nrmw[:, wi * 4 + l, k:k + 1], ALU.mult,
                   r=[modT, nrmw], w=[amul])
    phase_end()

    def run_seq(si):
        prompt = si < NPS
        li = si if prompt else si - NPS
        T = TP if prompt else TS
        TT = min(512, T)
        nTT = T // TT
        nm = 'p' if prompt else 's'
        xin_d = I['xp'][li] if prompt else I['xs'][li]
        tsz = min(128, T)
        cast_engs[:] = ['act'] if prompt else ['act', 'dve']
        nblk = T // tsz

        def mod(l, j, k):
            return modT[:, l, j * 8 + k, si:si + 1]

        xin = [aalloc('xin%d' % i, D, F32) for i in range(2)]
        for b in range(nblk):
            xi = xin[b % 2]
            ld(xi[0:tsz, :], xin_d[b * tsz:(b + 1) * tsz, :], w=[xi], r=['BAR'])
            for half in range(2):
                ps = psum()
                for q in range(4):
                    k = half * 4 + q
                    tr(ps[:, q * 128:q * 128 + tsz], xi[0:tsz, k * 128:(k + 1) * 128], identF[0:tsz, 0:tsz],
                       r=[xi, identF], w=[ps])
                cp(xT[:, half * 4:half * 4 + 4, b * tsz:(b + 1) * tsz],
                   ps.ap.rearrange("p (q t) -> p q t", q=4)[:, :, 0:tsz], r=[ps], w=[xT],
                   eng=('dve' if half == 0 else 'act'))
        phase_end()

        def norm_mod(l, which):
            psrot[0] = 8
            sq = [aalloc('nsq%d' % i, KC * TT, BF16, shape=("p (k t) -> p k t", dict(k=KC))) for i in range(2)]
            rs = [aalloc('nrs%d' % i, TT, F32) for i in range(2)]
            tmp = [aalloc('ntmp%d' % i, TT, F32) for i in range(2)]
            t2 = [aalloc('nt2%d' % i, TT, F32) for i in range(3)]
            c = 0
            for j in range(nTT):
                sl = slice(j * TT, (j + 1) * TT)
                s_ = sq[j % 2]
                sk_ = [Tl(s_[:, k, :], (s_.key, k)) for k in range(KC)]
                for k in range(KC):
                    if k % 2 == 0:
                        act(sk_[k].ap, xT[:, k, sl], AF.Square, r=[xT], w=[sk_[k]])
                    else:
                        tt(sk_[k].ap, xT[:, k, sl], xT[:, k, sl], ALU.mult, r=[xT], w=[sk_[k]])
                ps = psum()
                for k in range(KC):
                    mm(ps[:, 0:TT], onesB.ap, sk_[k].ap, start=(k == 0), stop=(k == KC - 1), r=[onesB, sk_[k]], w=[ps])
                r_ = rs[j % 2]
                tm = tmp[j % 2]
                rsqrt_from(r_.ap, ps[:, 0:TT], 1.0 / D, [ps], [r_, tm], tm.ap)
                for k in range(KC):
                    u = t2[c % 3]
                    c += 1
                    stt(u.ap, xT[:, k, sl], amul[:, which, l, k, si:si + 1], r_.ap, ALU.mult, ALU.mult,
                        r=[xT, amul, r_], w=[u])
                    shift = mod(l, 0 if which == 0 else 3, k)
                    act(hT[:, k, sl], u.ap, AF.Identity, bias=shift, scale=1.0, r=[u, modT], w=[hT])
            phase_end()

        def resid_add(ps, l, gj, k, sl):
            stt(xT[:, k, sl], ps[:, 0:TT], mod(l, gj, k), xT[:, k, sl], ALU.mult, ALU.add, r=[ps, modT, xT], w=[xT])

        def gdn(l, slot):
            psrot[0] = 8
            UB = min(128, T)
            C = 64 if prompt else 32
            cpu = UB // C
            nun = T // UB
            upt = TT // UB
            W = I['gdn_w_in'][slot]
            tri = aalloc('tri', UB, F32, parts=UB)
            blk = aalloc('blk', UB, F32, parts=UB)
            mA = aalloc('mA', UB, F32, parts=UB)
            mT = aalloc('mT', UB, F32, parts=UB)
            for t_, n_ in ((tri, 'tri'), (blk, 'blk'), (mA, 'mA'), (mT, 'mT')):
                ld(t_.ap, I['k_%s_%s' % (n_, nm)], w=[t_], r=['BAR'])
            cw = aalloc('gcw', 24 * 4, F32, shape=("p (c j) -> p c j", dict(c=24)))
            for j in range(4):
                ld(cw[:, :, j], I['gdn_conv_w'][slot][j].rearrange("(c p) -> p c", p=128), w=[cw], r=['BAR'], slow=True)
            onw = aalloc('onw', 1, F32)
            ld(onw.ap, I['gdn_o_norm'][slot].rearrange("(p o) -> p o", o=1), w=[onw], r=['BAR'], slow=True)
            alb = aalloc('alb', 16, F32)
            ld(alb[:, 0:8], I['gdn_a_log'][slot].partition_broadcast(128), w=[alb], r=['BAR'])
            ld(alb[:, 8:16], I['gdn_dt_bias'][slot].partition_broadcast(128), w=[alb], r=['BAR'])
            act(alb[:, 0:8], alb[:, 0:8], AF.Exp, r=[alb], w=[alb])
            ts(alb[:, 0:8], alb[:, 0:8], -1.0, ALU.mult, r=[alb], w=[alb])
            gb = aalloc('gb', nun * 56, F32, shape=("p (u j) -> p u j", dict(u=nun)))
            wab = wload(W[:, 4096:4112].rearrange("(k p) c -> p k c", p=128), "p (k c) -> p k c", k=KC)
            psab = psum()
            for u in range(nun):
                for k in range(KC):
                    mm(psab[0:UB, u * 16:(u + 1) * 16], hT[:, k, u * UB:(u + 1) * UB], wab[:, k, :], start=(k == 0),
                       stop=(k == KC - 1), r=[hT, wab], w=[psab])
            pv = psab.ap[0:UB, 0:nun * 16].rearrange("p (u j) -> p u j", u=nun)
            tg = aalloc('tg', nun * 16, F32, shape=("p (u j) -> p u j", dict(u=nun)))
            tt(tg[0:UB, :, 0:8], pv[:, :, 0:8], alb[0:UB, 8:16].unsqueeze(1).broadcast_to([UB, nun, 8]), ALU.add,
               r=[psab, alb], w=[tg])
            act(tg[0:UB, :, 0:8], tg[0:UB, :, 0:8], AF.Exp, r=[tg], w=[tg])
            act(tg[0:UB, :, 0:8], tg[0:UB, :, 0:8], AF.Ln, bias=cvec[0:UB, 1:2], r=[tg, cvec], w=[tg])
            tt(gb[0:UB, :, 0:8], tg[0:UB, :, 0:8], alb[0:UB, 0:8].unsqueeze(1).broadcast_to([UB, nun, 8]), ALU.mult,
               r=[tg, alb], w=[gb])
            act(tg[0:UB, :, 8:16], pv[:, :, 8:16], AF.Exp, scale=-1.0, r=[psab], w=[tg])
            ts(tg[0:UB, :, 8:16], tg[0:UB, :, 8:16], 1.0, ALU.add, r=[tg], w=[tg])
            recip(gb[0:UB, :, 48:56], tg[0:UB, :, 8:16], r=[tg], w=[gb])
            ts(gb[0:UB, :, 8:16], gb[0:UB, :, 48:56], -1.0, ALU.mult, r=[gb], w=[gb])
            psg = psum()
            psl = psum()
            for u in range(nun):
                mm(psg[0:UB, u * 8:(u + 1) * 8], tri.ap, gb[0:UB, u, 0:8], r=[tri, gb], w=[psg])
                mm(psl[0:UB, u * 8:(u + 1) * 8], blk.ap, gb[0:UB, u, 0:8], r=[blk, gb], w=[psl])
            pg = psg.ap[0:UB, 0:nun * 8].rearrange("p (u j) -> p u j", u=nun)
            pl = psl.ap[0:UB, 0:nun * 8].rearrange("p (u j) -> p u j", u=nun)
            cp(gb[0:UB, :, 16:24], pg, r=[psg], w=[gb])
            ts(gb[0:UB, :, 24:32], pg, -1.0, ALU.mult, r=[psg], w=[gb])
            tt(tg[0:UB, :, 0:8], pl, gb[0:UB, :, 16:24], ALU.subtract, r=[psl, gb], w=[tg])
            act(gb[0:UB, :, 32:40], tg[0:UB, :, 0:8], AF.Exp, r=[tg], w=[gb])
            act(tg[0:UB, :, 8:16], gb[0:UB, :, 16:24], AF.Exp, r=[gb], w=[tg])
            tt(gb[0:UB, :, 40:48], tg[0:UB, :, 8:16], gb[0:UB, :, 48:56], ALU.mult, r=[tg, gb], w=[gb])

            if DBG['gdn_stop'] == 1:
                phase_end()
                return
            obuf = aalloc('obuf', T, BF16)
            S = aalloc('S', 128, F32)
            Sb = aalloc('Sb', 128, BF16)
            cbuf = [aalloc('cb%d' % i, TT + 3, F32) for i in range(3)]
            acc = [aalloc('acc%d' % i, TT, F32) for i in range(2)]
            sqb = aalloc('gsq', TT, BF16)
            rnb = aalloc('grn', TT, F32)
            sqb2 = aalloc('gsq2', TT, BF16)
            rnb2 = aalloc('grn2', TT, F32)
            qTt = aalloc('qT', TT, BF16)
            kTt = aalloc('kT', TT, BF16)
            vTt = aalloc('vT', TT, BF16)
            oT = aalloc('oT', TT, F32)
            U4 = upt
            u3 = ("p (u c) -> p u c", dict(u=U4))
            kbg = aalloc('kbg', U4 * 128, BF16, shape=u3)
            vb = aalloc('vb', U4 * 128, BF16, shape=u3)
            trig = aalloc('trig', U4 * UB, F32, shape=u3)
            argT = aalloc('argT', U4 * UB, F32, shape=u3)
            argA = aalloc('argA', U4 * UB, F32, shape=u3)
            eg = aalloc('eg', U4 * UB, F32, shape=u3)
            NA = [aalloc('NA%d' % i, U4 * UB, F32, shape=u3) for i in range(2)]
            NT = [aalloc('NT%d' % i, U4 * UB, F32, shape=u3) for i in range(2)]
            TTm = [aalloc('TTm%d' % i, U4 * UB, F32, shape=u3) for i in range(2)]
            PTb = aalloc('PTb', U4 * UB, BF16, shape=u3)
            ub_ = aalloc('ub', 128, BF16)
            wob = aalloc('wob', 1024, BF16)
            HB = []
            for i in range(2):
                HB.append(dict(
                    QKT=aalloc('QKT%d' % i, U4 * UB, BF16, shape=u3),
                    qg=aalloc('qg%d' % i, U4 * UB, BF16, shape=u3),
                    wv=aalloc('wv%d' % i, U4 * 128, BF16, shape=u3),
                    wkT=aalloc('wkT%d' % i, U4 * UB, BF16, shape=u3),
                    kd=aalloc('kd%d' % i, U4 * 128, BF16, shape=u3),
                    zs=aalloc('zs%d' % i, TT, BF16),
                    egl=aalloc('egl%d' % i, U4 * cpu, F32),
                ))
            gout = (O['gconv_p'] if prompt else O['gconv_s'])[slot, li]
            Sout = (O['gS_p'] if prompt else O['gS_s'])[slot, li]
            Wo = I['gdn_w_o'][slot]
            hw_ = {}

            def _pc(c0):
                return W[:, c0:c0 + 128].rearrange("(k p) c -> p k c", p=128)

            def stageA(h, j, hb):
                B_ = HB[hb]
                QKT, qg, wv, wkT, kd, zs, egl = B_['QKT'], B_['qg'], B_['wv'], B_['wkT'], B_['kd'], B_['zs'], B_['egl']
                if j == 0:
                    hw_['wqk'] = wload_pieces(128, 2048, [(lambda v: v[:, :, 0, :], _pc(h * 128)),
                                                          (lambda v: v[:, :, 1, :], _pc(1024 + h * 128))],
                                              "p (k t c) -> p k t c", k=KC, t=2)
                    hw_['wvz'] = wload_pieces(128, 2048, [(lambda v: v[:, :, 0, :], _pc(2048 + h * 128)),
                                                          (lambda v: v[:, :, 1, :], _pc(3072 + h * 128))],
                                              "p (k t c) -> p k t c", k=KC, t=2)
                    if prompt:
                        for pi in range(3):
                            mset(cbuf[pi][:, 0:3], 0.0, w=[cbuf[pi]])
                    else:
                        for pi in range(3):
                            c0 = pi * 1024 + h * 128
                            ld(cbuf[pi][:, 0:3], I['st_gconv'][slot, li][:, c0:c0 + 128].rearrange("r c -> c r"),
                               w=[cbuf[pi]], r=['BAR'], slow=True)
                    yield
                wqk, wvz = hw_['wqk'], hw_['wvz']
                sl = slice(j * TT, (j + 1) * TT)
                for pi in range(3):
                    wt = wqk if pi < 2 else wvz
                    wi = pi if pi < 2 else 0
                    ps = psum()
                    for k in range(KC):
                        mm(ps[:, 0:TT], wt[:, k, wi, :], hT[:, k, sl], start=(k == 0), stop=(k == KC - 1),
                           r=[wt, hT], w=[ps])
                    yield
                    cb = cbuf[pi]
                    cp(cb[:, 3:3 + TT], ps[:, 0:TT], r=[ps], w=[cb], eng='act')
                    ch = pi * 8 + h
                    a_ = acc[pi % 2]
                    ts(a_.ap, cb[:, 0:TT], cw[:, ch, 0:1], ALU.mult, r=[cb, cw], w=[a_])
                    for jj in range(1, 4):
                        stt(a_.ap, cb[:, jj:jj + TT], cw[:, ch, jj:jj + 1], a_.ap, ALU.mult, ALU.add,
                            r=[cb, cw, a_], w=[a_])
                    yield
                    if j == nTT - 1:
                        c0 = pi * 1024 + h * 128
                        stq(gout[:, c0:c0 + 128].rearrange("r c -> c r"), cb[:, TT:TT + 3], r=[cb], w=['o_gconv'],
                            slow=True)
                    else:
                        cp(cb[:, 0:3], cb[:, TT:TT + 3], r=[cb], w=[cb], eng='act')
                    act(a_.ap, a_.ap, AF.Silu, r=[a_], w=[a_])
                    if pi < 2:
                        act(sqb.ap, a_.ap, AF.Square, r=[a_], w=[sqb])
                        ps2 = psum()
                        mm(ps2[:, 0:TT], onesB.ap, sqb.ap, r=[onesB, sqb], w=[ps2])
                        yield
                        rsqrt_from(rnb.ap, ps2[:, 0:TT], 1.0, [ps2], [rnb, rnb], rnb.ap)
                        if pi == 0:
                            stt(qTt.ap, a_.ap, 128.0 ** -0.5, rnb.ap, ALU.mult, ALU.mult, r=[a_, rnb], w=[qTt])
                        else:
                            tt(kTt.ap, a_.ap, rnb.ap, ALU.mult, r=[a_, rnb], w=[kTt])
                    else:
                        cp(vTt.ap, a_.ap, r=[a_], w=[vTt], eng='act')
                    yield
                ps = psum()
                for k in range(KC):
                    mm(ps[:, 0:TT], wvz[:, k, 1, :], hT[:, k, sl], start=(k == 0), stop=(k == KC - 1),
                       r=[wvz, hT], w=[ps])
                act(zs.ap, ps[:, 0:TT], AF.Silu, r=[ps], w=[zs])
                yield
                for uu in range(upt):
                    u = j * upt + uu
                    us = slice(uu * UB, (uu + 1) * UB)
                    pk = psum()
                    pkb = pk.ap.bitcast(BF16)
                    tr(pkb[0:UB, 0:128], kTt[:, us], identB.ap, r=[kTt, identB], w=[pk])
                    tr(pkb[0:UB, 128:256], vTt[:, us], identB.ap, r=[vTt, identB], w=[pk])
                    act(kbg[0:UB, uu, :], pkb[0:UB, 0:128], AF.Copy, scale=gb[0:UB, u, 40 + h:41 + h], r=[pk, gb],
                        w=[kbg])
                    act(kd[0:UB, uu, :], pkb[0:UB, 0:128], AF.Copy, scale=gb[0:UB, u, 32 + h:33 + h], r=[pk, gb],
                        w=[kd])
                    act(vb[0:UB, uu, :], pkb[0:UB, 128:256], AF.Copy, scale=gb[0:UB, u, 48 + h:49 + h],
                        r=[pk, gb], w=[vb])
                    ts(trig[0:UB, uu, :], tri.ap, gb[0:UB, u, h:h + 1], ALU.mult, r=[tri, gb], w=[trig])
                    yield
                    pgr = psum()
                    mm(pgr[:, 0:UB], onesF[0:UB, :], trig[0:UB, uu, :], r=[onesF, trig], w=[pgr])
                    tt(argT[0:UB, uu, :], pgr[0:UB, 0:UB], mT.ap, ALU.add, r=[pgr, mT], w=[argT])
                    tt(argA[0:UB, uu, :], mA.ap, pgr[0:UB, 0:UB], ALU.subtract, r=[pgr, mA], w=[argA])
                    act(eg[:, uu, :], pgr[:, 0:UB], AF.Exp, r=[pgr], w=[eg])
                    yield
                    act(argT[0:UB, uu, :], argT[0:UB, uu, :], AF.Exp, bias=gb[0:UB, u, 24 + h:25 + h],
                        r=[argT, gb], w=[argT])
                    act(argA[0:UB, uu, :], argA[0:UB, uu, :], AF.Exp, bias=gb[0:UB, u, 16 + h:17 + h],
                        r=[argA, gb], w=[argA])
                    for cc in range(cpu):
                        e_ = cc * C + C - 1
                        cp(egl[:, uu * cpu + cc:uu * cpu + cc + 1], eg[:, uu, e_:e_ + 1], r=[eg], w=[egl], eng='act')
                    pkk = psum()
                    mm(pkk[0:UB, 0:UB], kTt[:, us], kTt[:, us], r=[kTt], w=[pkk])
                    mm(pkk[0:UB, 128:128 + UB], kTt[:, us], qTt[:, us], r=[kTt, qTt], w=[pkk])
                    yield
                    stt(NA[0][0:UB, uu, :], pkk[0:UB, 0:UB], gb[0:UB, u, 8 + h:9 + h], argA[0:UB, uu, :], ALU.mult,
                        ALU.mult, r=[pkk, gb, argA], w=[NA[0]])
                    tt(QKT[0:UB, uu, :], pkk[0:UB, 128:128 + UB], argT[0:UB, uu, :], ALU.mult, r=[pkk, argT],
                       w=[QKT])
                    tt(qg[:, uu, :], qTt[:, us], eg[:, uu, :], ALU.mult, r=[qTt, eg], w=[qg])
                    pt_ = psum()
                    tr(pt_[0:UB, 0:UB], NA[0][0:UB, uu, :], identF[0:UB, 0:UB], r=[NA[0], identF], w=[pt_])
                    yield
                    cp(NT[0][0:UB, uu, :], pt_[0:UB, 0:UB], r=[pt_], w=[NT[0]], eng='act')
                    tt(TTm[0][0:UB, uu, :], pt_[0:UB, 0:UB], identF[0:UB, 0:UB], ALU.add, r=[pt_, identF],
                       w=[TTm[0]])
                    yield
                cur = 0
                for lev in range(5):
                    nxt = 1 - cur
                    last = (lev == 4)
                    for uu in range(upt):
                        p1 = psum()
                        mm(p1[0:UB, 0:UB], NT[cur][0:UB, uu, :], NA[cur][0:UB, uu, :], r=[NT[cur], NA[cur]], w=[p1])
                        if not last:
                            mm(p1[0:UB, 128:128 + UB], NA[cur][0:UB, uu, :], NT[cur][0:UB, uu, :],
                               r=[NT[cur], NA[cur]], w=[p1])
                        cp(NA[nxt][0:UB, uu, :], p1[0:UB, 0:UB], r=[p1], w=[NA[nxt]], eng='act')
                        if not last:
                            cp(NT[nxt][0:UB, uu, :], p1[0:UB, 128:128 + UB], r=[p1], w=[NT[nxt]], eng='dve')
                        yield
                    for uu in range(upt):
                        p2 = psum()
                        mm(p2[0:UB, 0:UB], identF[0:UB, 0:UB], TTm[cur][0:UB, uu, :], start=True, stop=False,
                           r=[identF, TTm[cur]], w=[p2])
                        mm(p2[0:UB, 0:UB], NA[nxt][0:UB, uu, :], TTm[cur][0:UB, uu, :], start=False, stop=True,
                           r=[NA[nxt], TTm[cur]], w=[p2])
                        if last:
                            cp(PTb[0:UB, uu, :], p2[0:UB, 0:UB], r=[p2], w=[PTb], eng='dve')
                        else:
                            cp(TTm[nxt][0:UB, uu, :], p2[0:UB, 0:UB], r=[p2], w=[TTm[nxt]], eng='dve')
                        yield
                    cur = nxt
                for uu in range(upt):
                    pw = psum()
                    mm(pw[0:UB, 0:128], PTb[0:UB, uu, :], vb[0:UB, uu, :], r=[PTb, vb], w=[pw])
                    mm(pw[:, 128:128 + UB], kbg[0:UB, uu, :], PTb[0:UB, uu, :], r=[PTb, kbg], w=[pw])
                    cp(wv[0:UB, uu, :], pw[0:UB, 0:128], r=[pw], w=[wv], eng='act')
                    cp(wkT[:, uu, :], pw[:, 128:128 + UB], r=[pw], w=[wkT], eng='dve')
                    yield

            def stageB(h, j, hb):
                B_ = HB[hb]
                QKT, qg, wv, wkT, kd, zs, egl = B_['QKT'], B_['qg'], B_['wv'], B_['wkT'], B_['kd'], B_['zs'], B_['egl']
                sl = slice(j * TT, (j + 1) * TT)
                if j == 0:
                    if prompt:
                        mset(S.ap, 0.0, w=[S])
                        mset(Sb.ap, 0.0, w=[Sb])
                    else:
                        ld(S.ap, I['st_gS'][slot, li, h], w=[S], r=['BAR'])
                        cp(Sb.ap, S.ap, r=[S], w=[Sb], eng='act')
                    yield
                for uu in range(upt):
                    for cc in range(cpu):
                        rs_ = slice(cc * C, (cc + 1) * C)
                        pu = psum()
                        mm(pu[0:UB, 0:128], wkT[:, uu, :], Sb.ap, r=[wkT, Sb], w=[pu])
                        po = psum()
                        mm(po[:, 0:C], Sb.ap, qg[:, uu, rs_], r=[Sb, qg], w=[po])
                        yield
                        tt(ub_[rs_, :], wv[rs_, uu, :], pu[rs_, 0:128], ALU.subtract, r=[wv, pu], w=[ub_])
                        t0 = uu * UB + cc * C
                        cp(oT[:, t0:t0 + C], po[:, 0:C], r=[po], w=[oT], eng='act')
                        yield
                        pS = psum()
                        mm(pS[:, 0:128], kd[rs_, uu, :], ub_[rs_, :], r=[kd, ub_], w=[pS])
                        po2 = psum()
                        mm(po2[:, 0:C], ub_[rs_, :], QKT[rs_, uu, rs_], r=[ub_, QKT], w=[po2])
                        yield
                        ei = uu * cpu + cc
                        stt(S.ap, S.ap, egl[:, ei:ei + 1], pS[:, 0:128], ALU.mult, ALU.add, r=[S, egl, pS], w=[S])
                        cp(Sb.ap, S.ap, r=[S], w=[Sb], eng='act')
                        tt(oT[:, t0:t0 + C], oT[:, t0:t0 + C], po2[:, 0:C], ALU.add, r=[oT, po2], w=[oT])
                        yield
                act(sqb2.ap, oT.ap, AF.Square, r=[oT], w=[sqb2])
                ps2 = psum()
                mm(ps2[:, 0:TT], onesB.ap, sqb2.ap, r=[onesB, sqb2], w=[ps2])
                yield
                rsqrt_from(rnb2.ap, ps2[:, 0:TT], 1.0 / 128, [ps2], [rnb2, rnb2], rnb2.ap)
                tt(oT.ap, oT.ap, rnb2.ap, ALU.mult, r=[oT, rnb2], w=[oT])
                stt(obuf[:, sl], oT.ap, onw[:, 0:1], zs.ap, ALU.mult, ALU.mult, r=[oT, onw, zs], w=[obuf])
                yield
                if j == nTT - 1:
                    stq(Sout[h], S.ap, r=[S], w=['o_gS'])
                    s_t = wst[wctr[0] % 2]
                    wctr[0] += 1
                    sv = s_t.ap[:, 0:1024]
                    ld(sv, Wo[h * 128:(h + 1) * 128, :], w=[s_t])
                    P.op('act', lambda: nc.scalar.copy(out=wob.ap, in_=sv), _keys([s_t]), _keys([wob]))
                    yield
                    for dc in range(KC):
                        for jt in range(nTT):
                            sl2 = slice(jt * TT, (jt + 1) * TT)
                            ps = psum()
                            mm(ps[:, 0:TT], wob[:, dc * 128:(dc + 1) * 128], obuf[:, sl2], r=[wob, obuf], w=[ps])
                            resid_add(ps, l, 2, dc, sl2)
                            yield

            def count_yields(mk):
                saved = (psctr[0], wctr[0], wctr[1], cast_ctr[0], dict(hw_))
                P.muted = True
                n = 0
                for _ in mk():
                    n += 1
                P.muted = False
                psctr[0], wctr[0], wctr[1], cast_ctr[0] = saved[0], saved[1], saved[2], saved[3]
                hw_.clear()
                hw_.update(saved[4])
                return n

            def drive(ga, na, gb_, nb):
                if ga is None or gb_ is None:
                    for g in (ga, gb_):
                        if g is not None:
                            for _ in g:
                                pass
                    return
                ia = ib = 0
                while ia < na or ib < nb:
                    if ib >= nb or (ia < na and ia * nb <= ib * na):
                        next(ga, None)
                        ia += 1
                    else:
                        next(gb_, None)
                        ib += 1
                for g in (ga, gb_):
                    for _ in g:
                        pass

            work = [(h, j) for h in range(8) for j in range(nTT)]
            prevB = None
            nprev = 0
            for idx, (h, j) in enumerate(work):
                na = count_yields(lambda: stageA(h, j, idx % 2))
                drive(stageA(h, j, idx % 2), na, prevB, nprev)
                nprev = count_yields(lambda: stageB(h, j, idx % 2))
                prevB = stageB(h, j, idx % 2)
            drive(None, 0, prevB, nprev)
            ARENA_USE['gdn'] = max(ARENA_USE.get('gdn', 0), apos[0])
            phase_end()

        def wload_pieces(parts, n, pieces, shape_str, **dims):
            assert n <= WCOLS
            s_t = wst[wctr[0] % 2]
            wctr[0] += 1
            b_t = wbf[wctr[1] % 3]
            wctr[1] += 1
            sv = s_t.ap[0:parts, 0:n]
            bv = b_t.ap[0:parts, 0:n]
            svv = sv.rearrange(shape_str, **dims)
            bvv = bv.rearrange(shape_str, **dims)
            for fn_, dap in pieces:
                ld(fn_(svv), dap, w=[s_t])
            wcast(bv, sv, s_t, b_t)
            return Tl(bvv, b_t.key)

        def out_tokmajor(src_fn, nfeat, dst_fn, stg, r):
            for b in range(nblk):
                ps = psum()
                tr(ps[0:tsz, 0:nfeat], src_fn(b), identF[0:nfeat, 0:nfeat], r=r + [identF], w=[ps])
                s_ = stg[b % 2]
                cp(s_[0:tsz, 0:nfeat], ps[0:tsz, 0:nfeat], r=[ps], w=[s_], eng='act')
                stq(dst_fn(b), s_[0:tsz, 0:nfeat], r=[s_], w=['o_misc'])

        def mla(l):
            psrot[0] = 4
            Win = I['mla_w_in'][0]
            Wq = I['mla_w_q_up'][0]
            Wkv = I['mla_w_kv_up'][0]
            Wo = I['mla_w_o'][0]
            NK = T if prompt else PAST + T
            koff = NK - T
            cosT = aalloc('cosT', T, F32, parts=32)
            sinT = aalloc('sinT', T, F32, parts=32)
            ld(cosT.ap, I['k_cos_' + nm], w=[cosT], r=['BAR'])
            ld(sinT.ap, I['k_sin_' + nm], w=[sinT], r=['BAR'])
            qnw = aalloc('qnw', 3, F32)
            kvw = aalloc('kvw', 2, F32)
            ld(qnw.ap, I['mla_q_norm'][0].rearrange("(k p) -> p k", p=128), w=[qnw], r=['BAR'], slow=True)
            ld(kvw.ap, I['mla_kv_norm'][0].rearrange("(k p) -> p k", p=128), w=[kvw], r=['BAR'], slow=True)
            cqT = aalloc('cqT', 3 * T, BF16, shape=("p (k t) -> p k t", dict(k=3)))
            ckvT = aalloc('ckvT', 2 * NK, BF16, shape=("p (k t) -> p k t", dict(k=2)))
            krT = aalloc('krT', NK, BF16, parts=32)
            amark = apos[0]
            pre = aalloc('pre', 3 * TT, F32, shape=("p (k t) -> p k t", dict(k=3)))
            sq = aalloc('msq', 3 * TT, BF16, shape=("p (k t) -> p k t", dict(k=3)))
            rs = aalloc('mrs', TT, F32)
            rtmp = aalloc('mrt', TT, F32)
            ckf = aalloc('ckf', 2 * TT, F32, shape=("p (k t) -> p k t", dict(k=2)))
            krf = aalloc('krf', TT, F32, parts=32)
            krt = aalloc('krt', TT, F32, parts=32)
            ostg = [aalloc('ostg%d' % i, 128, F32) for i in range(2)]
            lat_o = (O['lat_p'] if prompt else O['lat_s'])[li]
            kr_o = (O['kr_p'] if prompt else O['kr_s'])[li]
            if not prompt:
                cin = [aalloc('cin%d' % i, 256 + 32, F32) for i in range(2)]
                for b in range(PAST // 128):
                    ci = cin[b % 2]
                    ld(ci[:, 0:256], I['c_lat'][li][b * 128:(b + 1) * 128, :], w=[ci], r=['BAR'])
                    ld(ci[:, 256:288], I['c_kr'][li][b * 128:(b + 1) * 128, :], w=[ci], r=['BAR'])
                    ps = psum()
                    for k in range(2):
                        tr(ps[:, k * 128:(k + 1) * 128], ci[:, k * 128:(k + 1) * 128], identF.ap, r=[ci, identF], w=[ps])
                    tr(ps[0:32, 256:384], ci[:, 256:288], identF.ap, r=[ci, identF], w=[ps])
                    cp(ckvT[:, :, b * 128:(b + 1) * 128], ps[:, 0:256].rearrange("p (k t) -> p k t", k=2), r=[ps],
                       w=[ckvT], eng='dve')
                    cp(krT[:, b * 128:(b + 1) * 128], ps[0:32, 256:384], r=[ps], w=[krT], eng='act')
            for j in range(nTT):
                sl = slice(j * TT, (j + 1) * TT)
                ksl = slice(koff + j * TT, koff + (j + 1) * TT)
                wa = wload(Win[:, 0:256].rearrange("(k p) c -> p k c", p=128), "p (k c) -> p k c", k=KC)
                for c in range(2):
                    ps = psum()
                    for k in range(KC):
                        mm(ps[:, 0:TT], wa[:, k, c * 128:(c + 1) * 128], hT[:, k, sl], start=(k == 0),
                           stop=(k == KC - 1), r=[wa, hT], w=[ps])
                    cp(pre[:, c, :], ps[:, 0:TT], r=[ps], w=[pre], eng='act')
                wb_ = wload(Win[:, 256:512].rearrange("(k p) c -> p k c", p=128), "p (k c) -> p k c", k=KC)
                ps = psum()
                for k in range(KC):
                    mm(ps[:, 0:TT], wb_[:, k, 0:128], hT[:, k, sl], start=(k == 0), stop=(k == KC - 1), r=[wb_, hT],
                       w=[ps])
                cp(pre[:, 2, :], ps[:, 0:TT], r=[ps], w=[pre], eng='act')
                act(sq.ap, pre.ap, AF.Square, r=[pre], w=[sq])
                ps = psum()
                for c in range(3):
                    mm(ps[:, 0:TT], onesB.ap, sq[:, c, :], start=(c == 0), stop=(c == 2), r=[onesB, sq], w=[ps])
                rsqrt_from(rs.ap, ps[:, 0:TT], 1.0 / 384, [ps], [rs, rtmp], rtmp.ap)
                for c in range(3):
                    stt(cqT[:, c, sl], pre[:, c, :], qnw[:, c:c + 1], rs.ap, ALU.mult, ALU.mult, r=[pre, qnw, rs],
                        w=[cqT])
                wc = wload_pieces(128, KC * 192, [
                    (lambda v: v[:, :, 0:160], Win[:, 512:672].rearrange("(k p) c -> p k c", p=128)),
                    (lambda v: v[:, :, 160:176], Win[:, 656:672].rearrange("(k p) c -> p k c", p=128)),
                    (lambda v: v[:, :, 176:192], Win[:, 640:656].rearrange("(k p) c -> p k c", p=128)),
                ], "p (k c) -> p k c", k=KC)
                ps = psum()
                for k in range(KC):
                    mm(ps[:, 0:TT], wb_[:, k, 128:256], hT[:, k, sl], start=(k == 0), stop=(k == KC - 1), r=[wb_, hT],
                       w=[ps])
                cp(pre[:, 0, :], ps[:, 0:TT], r=[ps], w=[pre], eng='act')
                ps = psum()
                for k in range(KC):
                    mm(ps[:, 0:TT], wc[:, k, 0:128], hT[:, k, sl], start=(k == 0), stop=(k == KC - 1), r=[wc, hT],
                       w=[ps])
                cp(pre[:, 1, :], ps[:, 0:TT], r=[ps], w=[pre], eng='act')
                act(sq[:, 0:2, :], pre[:, 0:2, :], AF.Square, r=[pre], w=[sq])
                ps = psum()
                for c in range(2):
                    mm(ps[:, 0:TT], onesB.ap, sq[:, c, :], start=(c == 0), stop=(c == 1), r=[onesB, sq], w=[ps])
                rsqrt_from(rs.ap, ps[:, 0:TT], 1.0 / 256, [ps], [rs, rtmp], rtmp.ap)
                for c in range(2):
                    stt(ckf[:, c, :], pre[:, c, :], kvw[:, c:c + 1], rs.ap, ALU.mult, ALU.mult, r=[pre, kvw, rs],
                        w=[ckf])
                cp(ckvT[:, :, ksl], ckf.ap, r=[ckf], w=[ckvT], eng='act')
                for b in range(TT // tsz):
                    for c in range(2):
                        ps = psum()
                        tr(ps[0:tsz, 0:128], ckf[:, c, b * tsz:(b + 1) * tsz], identF.ap, r=[ckf, identF], w=[ps])
                        s_ = ostg[c]
                        cp(s_[0:tsz, :], ps[0:tsz, 0:128], r=[ps], w=[s_], eng='act')
                        t0 = j * TT + b * tsz
                        stq(lat_o[t0:t0 + tsz, c * 128:(c + 1) * 128], s_[0:tsz, :], r=[s_], w=['o_lat'])
                psA = psum()
                for k in range(KC):
                    mm(psA[0:32, 0:TT], wc[:, k, 128:160], hT[:, k, sl], start=(k == 0), stop=(k == KC - 1),
                       r=[wc, hT], w=[psA])
                psB = psum()
                for k in range(KC):
                    mm(psB[0:32, 0:TT], wc[:, k, 160:192], hT[:, k, sl], start=(k == 0), stop=(k == KC - 1),
                       r=[wc, hT], w=[psB])
                tt(krt.ap, psB[0:32, 0:TT], sinT[:, sl], ALU.mult, r=[psB, sinT], w=[krt])
                tt(krf.ap, psA[0:32, 0:TT], cosT[:, sl], ALU.mult, r=[psA, cosT], w=[krf])
                tt(krf.ap, krf.ap, krt.ap, ALU.add, r=[krf, krt], w=[krf])
                cp(krT[:, ksl], krf.ap, r=[krf], w=[krT], eng='act')
                for b in range(TT // tsz):
                    ps = psum()
                    tr(ps[0:tsz, 0:32], krf[:, b * tsz:(b + 1) * tsz], identF[0:32, 0:32], r=[krf, identF], w=[ps])
                    s_ = ostg[b % 2]
                    cp(s_[0:tsz, 0:32], ps[0:tsz, 0:32], r=[ps], w=[s_], eng='act')
                    t0 = j * TT + b * tsz
                    stq(kr_o[t0:t0 + tsz, :], s_[0:tsz, 0:32], r=[s_], w=['o_kr'])
            P.barrier()
            P.op('dve', lambda: nc.vector.memset(bar_t[:, 0:1], 0.0), writes=['BAR'])
            apos[0] = amark
            aphase[0] += 1
            HG = 4
            nkt = (NK + 127) // 128
            hoff = [0]

            def halias(name, cols, parts, shape=None):
                ap = hT_t[0:parts, hoff[0]:hoff[0] + cols]
                hoff[0] += cols
                assert hoff[0] <= KC * TP
                if shape is not None:
                    ap = ap.rearrange(shape[0], **shape[1])
                return Tl(ap, '%s@%d' % (name, aphase[0]))
            vtok = halias('vtok', nkt * HG * 64, 128, shape=("p (n c) -> p n c", dict(n=nkt)))
            abuf = halias('abuf', HG * T, 64, shape=("p (h t) -> p h t", dict(h=HG)))
            knT = halias('knT', NK, 64)
            qnT = halias('qnT', T, 64)
            qrT = aalloc('qrT', T, BF16, parts=32)
            qtmp = aalloc('qtmp', TT, F32, parts=32)
            qtm2 = aalloc('qtm2', TT, F32, parts=32)
            ptl = [aalloc('ptl%d' % i, TT, BF16) for i in range(3)]
            rD = aalloc('rD', TT, F32, parts=64)
            scale = 96.0 ** -0.5
            pctr = [0]
            accsel = [0]
            for g in range(16 // HG):
                wkv = wload(Wkv[:, g * HG * 128:(g + 1) * HG * 128].rearrange("(k p) c -> p k c", p=128),
                            "p (k c) -> p k c", k=2)
                wkv4 = wkv.ap.rearrange("p k (h c) -> p k h c", h=HG)
                for kt in range(nkt):
                    ks = min(128, NK - kt * 128)
                    ps = psum()
                    for c in range(2):
                        mm(ps[0:ks, 0:HG * 64].rearrange("p (h c) -> p h c", h=HG), ckvT[:, c, kt * 128:kt * 128 + ks],
                           wkv4[:, c, :, 64:128], start=(c == 0), stop=(c == 1), r=[ckvT, wkv], w=[ps])
                    cp(vtok[0:ks, kt, :], ps[0:ks, 0:HG * 64], r=[ps], w=[vtok], eng='act')
                for hh in range(HG):
                    h = g * HG + hh
                    for kb in range((NK + 511) // 512):
                        k0 = kb * 512
                        kn = min(512, NK - k0)
                        ps = psum()
                        for c in range(2):
                            mm(ps[0:64, 0:kn], wkv4[:, c, hh, 0:64], ckvT[:, c, k0:k0 + kn], start=(c == 0),
                               stop=(c == 1), r=[wkv, ckvT], w=[ps])
                        cp(knT[:, k0:k0 + kn], ps[0:64, 0:kn], r=[ps], w=[knT], eng='act')
                    c0 = h * 96
                    wq = wload_pieces(128, 3 * 128, [
                        (lambda v: v[:, :, 0:96], Wq[:, c0:c0 + 96].rearrange("(k p) c -> p k c", p=128)),
                        (lambda v: v[:, :, 96:112], Wq[:, c0 + 80:c0 + 96].rearrange("(k p) c -> p k c", p=128)),
                        (lambda v: v[:, :, 112:128], Wq[:, c0 + 64:c0 + 80].rearrange("(k p) c -> p k c", p=128)),
                    ], "p (k c) -> p k c", k=3)
                    for j in range(nTT):
                        sl = slice(j * TT, (j + 1) * TT)
                        ps = psum()
                        for c in range(3):
                            mm(ps[0:64, 0:TT], wq[:, c, 0:64], cqT[:, c, sl], start=(c == 0), stop=(c == 2),
                               r=[wq, cqT], w=[ps])
                        cp(qnT[:, sl], ps[0:64, 0:TT], r=[ps], w=[qnT], eng='act')
                        psA = psum()
                        for c in range(3):
                            mm(psA[0:32, 0:TT], wq[:, c, 64:96], cqT[:, c, sl], start=(c == 0), stop=(c == 2),
                               r=[wq, cqT], w=[psA])
                        psB = psum()
                        for c in range(3):
                            mm(psB[0:32, 0:TT], wq[:, c, 96:128], cqT[:, c, sl], start=(c == 0), stop=(c == 2),
                               r=[wq, cqT], w=[psB])
                        tt(qtmp.ap, psB[0:32, 0:TT], sinT[:, sl], ALU.mult, r=[psB, sinT], w=[qtmp])
                        tt(qtm2.ap, psA[0:32, 0:TT], cosT[:, sl], ALU.mult, r=[psA, cosT], w=[qtm2])
                        tt(qrT[:, sl], qtm2.ap, qtmp.ap, ALU.add, r=[qtm2, qtmp], w=[qrT])
                    for j in range(nTT):
                        sl = slice(j * TT, (j + 1) * TT)
                        if prompt:
                            kts = list(range(0, 4 * j + 4))
                        else:
                            kts = list(range(nkt))
                        A0, A1 = ACCP[accsel[0] % 2]
                        accsel[0] += 1

                        def blk_info(kt):
                            ks = min(128, NK - kt * 128)
                            partial = prompt and kt >= 4 * j
                            q0 = 128 * (kt - 4 * j) if partial else 0
                            return ks, partial, q0

                        def issue_scores(kt):
                            ks, partial, q0 = blk_info(kt)
                            qs = slice(j * TT + q0, (j + 1) * TT)
                            nq = TT - q0
                            ps = psum()
                            mm(ps[0:ks, 0:nq], knT[:, kt * 128:kt * 128 + ks], qnT[:, qs], start=True, stop=False,
                               r=[knT, qnT], w=[ps])
                            mm(ps[0:ks, 0:nq], krT[:, kt * 128:kt * 128 + ks], qrT[:, qs], start=False, stop=True,
                               r=[krT, qrT], w=[ps])
                            pt_ = ptl[pctr[0] % 3]
                            pctr[0] += 1
                            act(pt_[0:ks, 0:nq], ps[0:ks, 0:nq], AF.Exp, scale=scale, r=[ps], w=[pt_])
                            if partial:
                                mset(pt_[64:128, 0:64], 0.0, w=[pt_])
                            return pt_

                        def issue_pv(ix, kt, pt_):
                            ks, partial, q0 = blk_info(kt)
                            nq = TT - q0
                            first = (ix == 0)
                            lastk = (ix == len(kts) - 1)
                            mm(A0[0:64, q0:TT], vtok[0:ks, kt, hh * 64:(hh + 1) * 64], pt_[0:ks, 0:nq], start=first,
                               stop=lastk, r=[vtok, pt_], w=[A0])
                            mm(A1[0:64, q0:TT], onesB[0:ks, 0:64], pt_[0:ks, 0:nq], start=first, stop=lastk,
                               r=[onesB, pt_], w=[A1])

                        pend = None
                        for ix, kt in enumerate(kts):
                            pt_ = issue_scores(kt)
                            if pend is not None:
                                issue_pv(*pend)
                            pend = (ix, kt, pt_)
                        issue_pv(*pend)
                        recip(rD[:, 0:TT], A1[0:64, 0:TT], r=[A1], w=[rD])
                        tt(abuf[:, hh, sl], A0[0:64, 0:TT], rD[:, 0:TT], ALU.mult, r=[A0, rD], w=[abuf])
                for half in range(2):
                    wo = wload(Wo[g * HG * 64:(g + 1) * HG * 64, half * 512:(half + 1) * 512].rearrange(
                        "(h p) c -> p h c", p=64), "p (h c) -> p h c", h=HG)
                    for dq in range(4):
                        dc = half * 4 + dq
                        for jt in range(nTT):
                            sl2 = slice(jt * TT, (jt + 1) * TT)
                            ps = psum()
                            for hh in range(HG):
                                mm(ps[:, 0:TT], wo[:, hh, dq * 128:(dq + 1) * 128], abuf[:, hh, sl2], start=(hh == 0),
                                   stop=(hh == HG - 1), r=[wo, abuf], w=[ps])
                            resid_add(ps, l, 2, dc, sl2)
            phase_end()

        def swa(l):
            psrot[0] = 4
            Win = I['swa_w_in'][0]
            Wo = I['swa_w_o'][0]
            CQ = 64 if prompt else 32
            nch = T // CQ
            NQ = 4 * CQ
            pad = 128 if prompt else 0
            NK = T if prompt else 128 + T
            koff = 0 if prompt else 128
            kT = aalloc('skT', 4 * (pad + NK), BF16, parts=64, shape=("p (h t) -> p h t", dict(h=4)))
            nva = (T // 128) if prompt else 2
            vA = aalloc('vA', nva * 256, BF16, shape=("p (n c) -> p n c", dict(n=nva)))
            if prompt:
                vB = aalloc('vB', (nva + 1) * 256, BF16, shape=("p (n c) -> p n c", dict(n=nva + 1)))
            ba = aalloc('sba', 4 * NQ, F32, shape=("p (h c) -> p h c", dict(h=4)))
            ld(ba.ap, I['k_ba_' + nm], w=[ba], r=['BAR'])
            bsz = 64 if prompt else 32
            bb = aalloc('sbb', 4 * NQ, F32, parts=bsz, shape=("p (h c) -> p h c", dict(h=4)))
            ld(bb.ap, I['k_bb_' + nm], w=[bb], r=['BAR'])
            if prompt:
                ba1 = aalloc('sba1', NQ, F32)
            snk = aalloc('snk', 16, F32, parts=64)
            ld(snk.ap, I['swa_sinks'][0].partition_broadcast(64), w=[snk], r=['BAR'])
            act(snk.ap, snk.ap, AF.Exp, r=[snk], w=[snk])
            esk = aalloc('esk', 4 * NQ, F32, parts=64, shape=("p (h g q) -> p h g q", dict(h=4, g=4)))
            cp(esk.ap, snk.ap.rearrange("p (h g) -> p h g", h=4).unsqueeze(3).broadcast_to([64, 4, 4, CQ]), r=[snk],
               w=[esk])
            NH = 2 if prompt else 1
            TH = T // NH
            nchh = nch // NH
            abuf = aalloc('sabuf', 4 * TH, BF16, parts=64, shape=("p (g t) -> p g t", dict(g=4)))
            q4 = aalloc('sq4', 4 * TH, BF16, parts=64, shape=("p (c g q) -> p c g q", dict(c=nchh, g=4)))
            stg = [aalloc('sstg%d' % i, 256, F32) for i in range(2)]
            sarg = [aalloc('sarg%d' % i, NQ, F32) for i in range(2)]
            spa = [aalloc('spa%d' % i, NQ, BF16) for i in range(2)]
            spb = [aalloc('spb%d' % i, NQ, BF16, parts=64) for i in range(2)]
            sden = [aalloc('sden%d' % i, NQ, F32, parts=64) for i in range(2)]
            sk_o = (O['sk_p'] if prompt else O['sk_s'])[li]
            sv_o = (O['sv_p'] if prompt else O['sv_s'])[li]
            wk_ = wload(Win[:, 1024:1280].rearrange("(k p) c -> p k c", p=128), "p (k c) -> p k c", k=KC)
            wv_ = wload(Win[:, 1280:1536].rearrange("(k p) c -> p k c", p=128), "p (k c) -> p k c", k=KC)
            if prompt:
                for hh in range(4):
                    mset(kT[:, hh, 0:pad], 0.0, w=[kT])
                mset(vB[:, 0, :], 0.0, w=[vB])
            else:
                ck = aalloc('sck', 256, F32)
                cv = aalloc('scv', 256, F32)
                ld(ck.ap, I['c_sk'][li], w=[ck], r=['BAR'])
                ld(cv.ap, I['c_sv'][li], w=[cv], r=['BAR'])
                for hh in range(4):
                    ps = psum()
                    tr(ps[0:64, 0:128], ck[:, hh * 64:(hh + 1) * 64], identF.ap, r=[ck, identF], w=[ps])
                    cp(kT[:, hh, 0:128], ps[0:64, 0:128], r=[ps], w=[kT], eng='act')
                cp(vA[:, 0, :], cv.ap, r=[cv], w=[vA], eng='dve')
                stq(sk_o[0:96, :], ck[32:128, :], r=[ck], w=['o_sk'])
                stq(sv_o[0:96, :], cv[32:128, :], r=[cv], w=['o_sv'])
            for j in range(nTT):
                sl = slice(j * TT, (j + 1) * TT)
                for hh in range(4):
                    ps = psum()
                    for k in range(KC):
                        mm(ps[0:64, 0:TT], wk_[:, k, hh * 64:(hh + 1) * 64], hT[:, k, sl], start=(k == 0),
                           stop=(k == KC - 1), r=[wk_, hT], w=[ps])
                    cp(kT[:, hh, pad + koff + j * TT:pad + koff + (j + 1) * TT], ps[0:64, 0:TT], r=[ps], w=[kT],
                       eng='act')
            for b in range(nblk):
                ps = psum()
                for k in range(KC):
                    mm(ps[0:tsz, 0:256], hT[:, k, b * tsz:(b + 1) * tsz], wv_[:, k, :], start=(k == 0),
                       stop=(k == KC - 1), r=[hT, wv_], w=[ps])
                vi = b if prompt else 1
                cp(vA[0:tsz, vi, :], ps[0:tsz, 0:256], r=[ps], w=[vA], eng='act')
                if b == nblk - 1:
                    s_ = stg[0]
                    cp(s_[0:tsz, :], ps[0:tsz, 0:256], r=[ps], w=[s_], eng='dve')
                    stq(sv_o[128 - tsz:128, :], s_[0:tsz, :], r=[s_], w=['o_sv'])
                    ps2 = psum()
                    for k in range(KC):
                        mm(ps2[0:tsz, 0:256], hT[:, k, b * tsz:(b + 1) * tsz], wk_[:, k, :], start=(k == 0),
                           stop=(k == KC - 1), r=[hT, wk_], w=[ps2])
                    s2 = stg[1]
                    cp(s2[0:tsz, :], ps2[0:tsz, 0:256], r=[ps2], w=[s2], eng='dve')
                    stq(sk_o[128 - tsz:128, :], s2[0:tsz, :], r=[s2], w=['o_sk'])
            if prompt:
                for m in range(1, nva + 1):
                    t0 = 64 + 128 * (m - 1)
                    nt_ = min(128, T - t0)
                    ps = psum()
                    for k in range(KC):
                        mm(ps[0:nt_, 0:256], hT[:, k, t0:t0 + nt_], wv_[:, k, :], start=(k == 0), stop=(k == KC - 1),
                           r=[hT, wv_], w=[ps])
                    cp(vB[0:nt_, m, :], ps[0:nt_, 0:256], r=[ps], w=[vB], eng='act')
                ps = psum()
                for k in range(KC):
                    mm(ps[:, 0:256], hT[:, k, 0:128], wv_[:, k, :], start=(k == 0), stop=(k == KC - 1), r=[hT, wv_],
                       w=[ps])
                ps3 = psum()
                cp(stg[0][:, :], ps[:, 0:256], r=[ps], w=[stg[0]], eng='dve')
                shm = aalloc('shm', 128, BF16)
                mset(shm.ap, 0.0, w=[shm])
                cp(shm[0:64, 64:128], identB[0:64, 0:64], r=[identB], w=[shm], eng='dve')
                vtmp = aalloc('vtmp', 256, BF16)
                cp(vtmp.ap, stg[0].ap, r=[stg[0]], w=[vtmp], eng='dve')
                mm(ps3[:, 0:256], shm.ap, vtmp.ap, r=[shm, vtmp], w=[ps3])
                cp(vB[:, 0, :], ps3[:, 0:256], r=[ps3], w=[vB], eng='act')
            scale = 64.0 ** -0.5
            for hh in range(4):
                wq_ = wload(Win[:, hh * 256:(hh + 1) * 256].rearrange("(k p) c -> p k c", p=128), "p (k c) -> p k c",
                            k=KC)
                if prompt:
                    ld(ba1.ap, I['k_ba1_p'][:, hh, :], w=[ba1], r=['BAR'])
                tph = nTT // NH if prompt else 1
                for hf in range(NH):
                    for g in range(4):
                        for jj in range(tph):
                            j = hf * tph + jj
                            sl = slice(j * TT, (j + 1) * TT)
                            ps = psum()
                            for k in range(KC):
                                mm(ps[0:64, 0:TT], wq_[:, k, g * 64:(g + 1) * 64], hT[:, k, sl], start=(k == 0),
                                   stop=(k == KC - 1), r=[wq_, hT], w=[ps])
                            cpt = TT // CQ
                            cp(q4[:, jj * cpt:(jj + 1) * cpt, g, :], ps[0:64, 0:TT].rearrange("p (c q) -> p c q", c=cpt),
                               r=[ps], w=[q4], eng='act')
                    def sw_scores(cl):
                        c = hf * nchh + cl
                        blocks = []
                        if prompt:
                            if c >= 1:
                                k0 = pad + 64 * (c - 2)
                                vt = (vA, (c - 2) // 2) if c % 2 == 0 else (vB, (c - 1) // 2)
                                blocks.append((128, k0, vt, (ba[:, hh, :] if c >= 2 else ba1.ap), (ba if c >= 2 else ba1)))
                            vt = (vA, c // 2) if c % 2 == 0 else (vB, (c + 1) // 2)
                            blocks.append((64, pad + 64 * c, vt, bb[:, hh, :], bb))
                        else:
                            blocks.append((128, 0, (vA, 0), ba[:, hh, :], ba))
                            blocks.append((32, 128, (vA, 1), bb[:, hh, :], bb))
                        qv = q4[:, cl, :, :].rearrange("p g q -> p (g q)")
                        res = []
                        for bi, (ks, k0, (vt_, vi), btap, bt) in enumerate(blocks):
                            ps = psum()
                            mm(ps[0:ks, 0:NQ], kT[:, hh, k0:k0 + ks], qv, r=[kT, q4], w=[ps])
                            ar = sarg[bi % 2]
                            stt(ar[0:ks, :], ps[0:ks, 0:NQ], scale, btap[0:ks, :], ALU.mult, ALU.add, r=[ps, bt], w=[ar])
                            pp = (spa if ks == 128 else spb)[c % 2]
                            act(pp[0:ks, :], ar[0:ks, :], AF.Exp, r=[ar], w=[pp])
                            res.append((ks, vt_, vi, pp))
                        return res

                    def sw_pv(cl, res):
                        A0, A1 = ACCP[cl % 2]
                        for bi, (ks, vt_, vi, pp) in enumerate(res):
                            first = (bi == 0)
                            lastb = (bi == len(res) - 1)
                            mm(A0[0:64, 0:NQ], vt_[0:ks, vi, hh * 64:(hh + 1) * 64], pp[0:ks, :], start=first,
                               stop=lastb, r=[vt_, pp], w=[A0])
                            mm(A1[0:64, 0:NQ], onesB[0:ks, 0:64], pp[0:ks, :], start=first, stop=lastb,
                               r=[onesB, pp], w=[A1])
                        sd = sden[cl % 2]
                        tt(sd.ap, A1[0:64, 0:NQ], esk[:, hh, :, :].rearrange("p g q -> p (g q)"), ALU.add,
                           r=[A1, esk], w=[sd])
                        recip(sd.ap, sd.ap, r=[sd], w=[sd])
                        tt(abuf[:, :, cl * CQ:(cl + 1) * CQ], A0[0:64, 0:NQ].rearrange("p (g q) -> p g q", g=4),
                           sd.ap.rearrange("p (g q) -> p g q", g=4), ALU.mult, r=[A0, sd], w=[abuf])

                    pend = None
                    for cl in range(nchh):
                        res = sw_scores(cl)
                        if pend is not None:
                            sw_pv(*pend)
                        pend = (cl, res)
                    sw_pv(*pend)
                    for half in range(2):
                        wo = wload(Wo[hh * 256:(hh + 1) * 256, half * 512:(half + 1) * 512].rearrange(
                            "(h p) c -> p h c", p=64), "p (h c) -> p h c", h=4)
                        for dq in range(4):
                            dc = half * 4 + dq
                            for jj in range(tph):
                                jt = hf * tph + jj
                                sl2 = slice(jt * TT, (jt + 1) * TT)
                                sl3 = slice(jj * TT, (jj + 1) * TT)
                                ps = psum()
                                for g in range(4):
                                    mm(ps[:, 0:TT], wo[:, g, dq * 128:(dq + 1) * 128], abuf[:, g, sl3], start=(g == 0),
                                       stop=(g == 3), r=[wo, abuf], w=[ps])
                                resid_add(ps, l, 2, dc, sl2)
            phase_end()

        def ffn(l):
            psrot[0] = 8
            Win = I['ffn_w_in'][l]
            Wout = I['ffn_w_out'][l]
            G = 11
            fcw = aalloc('fcw', NFC * 3, F32, shape=("p (c j) -> p c j", dict(c=NFC)))
            for jj in range(3):
                ld(fcw[:, :, jj], I['ffn_conv_w'][l][jj].rearrange("(c p) -> p c", p=128), w=[fcw], r=['BAR'], slow=True)
            fcb = aalloc('fcb', NFC, F32)
            ld(fcb.ap, I['ffn_conv_b'][l].rearrange("(c p) -> p c", p=128), w=[fcb], r=['BAR'], slow=True)
            actb = aalloc('actb', G * T, BF16, shape=("p (g t) -> p g t", dict(g=G)))
            cb = aalloc('fcbuf', TT + 2, F32)
            acc = [aalloc('facc%d' % i, TT, F32) for i in range(2)]
            fout = (O['fconv_p'] if prompt else O['fconv_s'])[l, li]
            Win4 = Win.rearrange("(k p) (two f) -> p k two f", p=128, two=2)
            for g0 in range(0, NFC, G):
                gn = min(G, NFC - g0)
                for gi in range(gn):
                    cc = g0 + gi
                    wt = wload_pieces(128, 2048, [
                        (lambda v: v[:, :, 0, :], Win[:, cc * 128:(cc + 1) * 128].rearrange("(k p) c -> p k c", p=128)),
                        (lambda v: v[:, :, 1, :], Win[:, DFF + cc * 128:DFF + (cc + 1) * 128].rearrange("(k p) c -> p k c", p=128)),
                    ], "p (k two f) -> p k two f", k=KC, two=2)
                    if prompt:
                        mset(cb[:, 0:2], 0.0, w=[cb])
                    else:
                        ld(cb[:, 0:2], I['st_fconv'][l, li][:, cc * 128:(cc + 1) * 128].rearrange("r c -> c r"), w=[cb],
                           r=['BAR'], slow=True)
                    for j in range(nTT):
                        sl = slice(j * TT, (j + 1) * TT)
                        pg = psum()
                        for k in range(KC):
                            mm(pg[:, 0:TT], wt[:, k, 0, :], hT[:, k, sl], start=(k == 0), stop=(k == KC - 1),
                               r=[wt, hT], w=[pg])
                        pu = psum()
                        for k in range(KC):
                            mm(pu[:, 0:TT], wt[:, k, 1, :], hT[:, k, sl], start=(k == 0), stop=(k == KC - 1),
                               r=[wt, hT], w=[pu])
                        cp(cb[:, 2:2 + TT], pg[:, 0:TT], r=[pg], w=[cb], eng='act')
                        a_ = acc[j % 2]
                        ts(a_.ap, cb[:, 0:TT], fcw[:, cc, 0:1], ALU.mult, fcb[:, cc:cc + 1], ALU.add, r=[cb, fcw, fcb],
                           w=[a_])
                        for jj in range(1, 3):
                            stt(a_.ap, cb[:, jj:jj + TT], fcw[:, cc, jj:jj + 1], a_.ap, ALU.mult, ALU.add,
                                r=[cb, fcw, a_], w=[a_])
                        if j == nTT - 1:
                            stq(fout[:, cc * 128:(cc + 1) * 128].rearrange("r c -> c r"), cb[:, TT:TT + 2], r=[cb],
                                w=['o_fconv'], slow=True)
                        else:
                            cp(cb[:, 0:2], cb[:, TT:TT + 2], r=[cb], w=[cb], eng='act')
                        act(a_.ap, a_.ap, AF.Silu, r=[a_], w=[a_])
                        tt(actb[:, gi, sl], a_.ap, pu[:, 0:TT], ALU.mult, r=[a_, pu], w=[actb])
                for dc in range(KC):
                    wo = wload(Wout[g0 * 128:(g0 + gn) * 128, dc * 128:(dc + 1) * 128].rearrange(
                        "(g p) c -> p g c", p=128), "p (g c) -> p g c", g=gn)
                    for jt in range(nTT):
                        sl2 = slice(jt * TT, (jt + 1) * TT)
                        ps = psum()
                        for gi in range(gn):
                            mm(ps[:, 0:TT], wo[:, gi, :], actb[:, gi, sl2], start=(gi == 0),
                               stop=(gi == gn - 1), r=[wo, actb], w=[ps])
                        resid_add(ps, l, 5, dc, sl2)
            phase_end()

        def final():
            psrot[0] = 8
            sq = aalloc('fsq', KC * TT, BF16, shape=("p (k t) -> p k t", dict(k=KC)))
            rs = aalloc('frs', TT, F32)
            tm = aalloc('ftm', TT, F32)
            yT = aalloc('yT', KC * TT, F32, shape=("p (k t) -> p k t", dict(k=KC)))
            ystg = [aalloc('ystg%d' % i, D, F32) for i in range(2)]
            y_o = (O['y_p'] if prompt else O['y_s'])[li]
            for j in range(nTT):
                sl = slice(j * TT, (j + 1) * TT)
                act(sq.ap, xT[:, :, sl], AF.Square, r=[xT], w=[sq])
                ps = psum()
                for k in range(KC):
                    mm(ps[:, 0:TT], onesB.ap, sq[:, k, :], start=(k == 0), stop=(k == KC - 1), r=[onesB, sq], w=[ps])
                rsqrt_from(rs.ap, ps[:, 0:TT], 1.0 / D, [ps], [rs, tm], tm.ap)
                for k in range(KC):
                    stt(yT[:, k, :], xT[:, k, sl], nrmw[:, 8, k:k + 1], rs.ap, ALU.mult, ALU.mult, r=[xT, nrmw, rs],
                        w=[yT])
                for b in range(TT // tsz):
                    ys = ystg[b % 2]
                    for half in range(2):
                        ps = psum()
                        for q in range(4):
                            k = half * 4 + q
                            tr(ps[0:tsz, q * 128:(q + 1) * 128], yT[:, k, b * tsz:(b + 1) * tsz], identF.ap,
                               r=[yT, identF], w=[ps])
                        cp(ys[0:tsz, half * 512:(half + 1) * 512], ps[0:tsz, 0:512], r=[ps], w=[ys],
                           eng=('dve' if half == 0 else 'act'))
                    t0 = j * TT + b * tsz
                    stq(y_o[t0:t0 + tsz, :], ys[0:tsz, :], r=[ys], w=['o_y'])
            phase_end()

        for l in range(DEPTH):
            if l >= NLAYERS_DBG[0]:
                break
            kind, slot = l % 3, l // 3
            norm_mod(l, 0)
            if DBG['mixer']:
                if kind == 0:
                    gdn(l, slot)
                elif kind == 1:
                    mla(l)
                else:
                    swa(l)
            norm_mod(l, 1)
            if DBG['ffn']:
                ffn(l)
        final()

    for si in range(NSEQ):
        if si in SEQS_DBG:
            run_seq(si)
    P.barrier()
    P.emit()
    es.close()
    return nc


_NC_CACHE = {}


def _consts(rel_bias):
    c = {}
    c['k_ident'] = np.eye(128, dtype=np.float32)
    for nm, ub, ch in (('p', 128, 64), ('s', 32, 32)):
        tri, blk, mA, mT = _gdn_masks(ub, ch)
        c['k_tri_' + nm], c['k_blk_' + nm], c['k_mA_' + nm], c['k_mT_' + nm] = tri, blk, mA, mT
    c['k_cos_p'], c['k_sin_p'] = _rope_tables(np.arange(TP))
    c['k_cos_s'], c['k_sin_s'] = _rope_tables(np.arange(TS) + PAST)
    rb = np.asarray(rel_bias, dtype=np.float32)
    i = np.arange(64)
    kr = np.arange(192)
    bk = _t5_bucket((128 + i)[None, :] - kr[:, None])
    t = rb[bk]
    t = t.reshape(192, 64, 4, 4).transpose(0, 2, 3, 1).reshape(192, 4, 256)
    c['k_ba_p'] = np.ascontiguousarray(t[0:128])
    ba1 = t[0:128].copy()
    ba1[0:64] = NEG
    c['k_ba1_p'] = ba1
    c['k_bb_p'] = np.ascontiguousarray(t[128:192])
    qpos = PAST + np.arange(TS)
    kpos = np.concatenate([PAST - 128 + np.arange(128), PAST + np.arange(TS)])
    bk = _t5_bucket(qpos[None, :] - kpos[:, None])
    t = rb[bk].reshape(160, TS, 4, 4).transpose(0, 2, 3, 1).reshape(160, 4, 4 * TS)
    c['k_ba_s'] = np.ascontiguousarray(t[0:128])
    c['k_bb_s'] = np.ascontiguousarray(t[128:160])
    return c


def kernel(**inputs):
    f = lambda k: np.ascontiguousarray(np.asarray(inputs[k], dtype=np.float32))
    if 'nc' not in _NC_CACHE:
        _NC_CACHE['nc'] = build_program()
    nc = _NC_CACHE['nc']
    consts = _consts(f('rel_bias'))
    shared = {}
    for k in ('ada_w', 'ada_b', 'norm1', 'norm2', 'final_norm', 'gdn_w_in', 'gdn_conv_w', 'gdn_a_log', 'gdn_dt_bias',
              'gdn_o_norm', 'gdn_w_o', 'mla_w_in', 'mla_q_norm', 'mla_kv_norm', 'mla_w_q_up', 'mla_w_kv_up', 'mla_w_o',
              'swa_w_in', 'swa_sinks', 'swa_w_o', 'ffn_w_in', 'ffn_conv_w', 'ffn_conv_b', 'ffn_w_out'):
        shared[k] = f(k)
    shared.update(consts)
    xp, xs, cpr, csm = f('x_prompt'), f('x_sample'), f('c_prompt'), f('c_sample')
    gconv, gS = f('state_gdn_conv'), f('state_gdn_S')
    clat, ckr = f('cache_mla_latent'), f('cache_mla_krope')
    csk, csv = f('cache_swa_k'), f('cache_swa_v')
    fconv = f('state_ffn_conv')
    in_maps = []
    for c in range(NCORES):
        ps = slice(c * NPS, (c + 1) * NPS)
        ss = slice(c * NSS, (c + 1) * NSS)
        m = dict(shared)
        m['xp'] = np.ascontiguousarray(xp[ps])
        m['xs'] = np.ascontiguousarray(xs[ss])
        m['c6'] = np.ascontiguousarray(np.concatenate([cpr[ps], csm[ss]], 0))
        m['st_gconv'] = np.ascontiguousarray(gconv[:, ss])
        m['st_gS'] = np.ascontiguousarray(gS[:, ss])
        m['c_lat'] = np.ascontiguousarray(clat[0, ss])
        m['c_kr'] = np.ascontiguousarray(ckr[0, ss])
        m['c_sk'] = np.ascontiguousarray(csk[0, ss].reshape(NSS, 128, 256))
        m['c_sv'] = np.ascontiguousarray(csv[0, ss].reshape(NSS, 128, 256))
        m['st_fconv'] = np.ascontiguousarray(fconv[:, ss])
        in_maps.append(m)
    res = run_bass_kernel_spmd(nc, in_maps, core_ids=list(range(NCORES)))
    R = res.results

    def cat(name, axis):
        return np.concatenate([np.asarray(R[c][name], dtype=np.float32) for c in range(NCORES)], axis=axis)

    y_p = cat('y_p', 0)
    y_s = cat('y_s', 0)
    outs = (
        y_p, y_s,
        cat('gconv_p', 1), cat('gconv_s', 1),
        cat('gS_p', 1), cat('gS_s', 1),
        cat('lat_p', 0)[None], cat('lat_s', 0)[None],
        cat('kr_p', 0)[None], cat('kr_s', 0)[None],
        cat('sk_p', 0).reshape(1, NCORES * NPS, 128, 4, 64), cat('sk_s', 0).reshape(1, NCORES * NSS, 128, 4, 64),
        cat('sv_p', 0).reshape(1, NCORES * NPS, 128, 4, 64), cat('sv_s', 0).reshape(1, NCORES * NSS, 128, 4, 64),
        cat('fconv_p', 1), cat('fconv_s', 1),
    )
    return tuple(np.ascontiguousarray(o) for o in outs)
```

```python
import math
import numpy as np
import concourse.bass as bass
import concourse.mybir as mybir
from concourse.bass_utils import run_bass_kernel_spmd

F32 = mybir.dt.float32
BF16 = mybir.dt.bfloat16
AF = mybir.ActivationFunctionType
ALU = mybir.AluOpType

NCORES = 8
D = 1024
KC = 8
DEPTH = 4
TP = 2048
TS = 32
PAST = 1024
NPS = 4
NSS = 2
NSEQ = NPS + NSS
EPS = 1e-6
DFF = 2816
NFC = 22
NEG = -30000.0
NLAYERS_DBG = [4]
ARENA_USE = {}
DBG = {'mixer': True, 'ffn': True, 'gdn_stop': 99}
SEQS_DBG = [0, 1, 2, 3, 4, 5]

COMPUTE = ('pe', 'act', 'dve', 'pool')
DMAQ = ('sp', 'gq')
NDSEM = 8


class Op:
    __slots__ = ('stream', 'idx', 'fn', 'is_dma', 'q', 'deps', 'signal', 'sig', 'dseq')

    def __init__(self):
        self.signal = False
        self.sig = None
        self.deps = []


class Prog:
    def __init__(self, nc):
        self.nc = nc
        self.streams = {s: [] for s in ('pe', 'act', 'dve', 'pool', 'sp')}
        self.last_w = {}
        self.readers = {}
        self.known = {s: {} for s in self.streams}
        self.known_dma = {s: set() for s in self.streams}
        self.dma_count = {q: 0 for q in DMAQ}
        self.dma_ops = {q: [] for q in DMAQ}
        self.ps_pending = {}

    @staticmethod
    def _stream_of(eng):
        return 'pool' if eng == 'gq' else eng

    muted = False

    def op(self, eng, fn, reads=(), writes=()):
        if self.muted:
            return None
        for key in reads:
            if key in self.ps_pending:
                self.ps_pending[key] -= 1
        o = Op()
        o.is_dma = eng in DMAQ
        o.q = eng if o.is_dma else None
        o.stream = self._stream_of(eng)
        o.fn = fn
        st = self.streams[o.stream]
        o.idx = len(st)
        cand = []
        comp = not o.is_dma
        for key in reads:
            w = self.last_w.get(key)
            if w is not None:
                if comp and (not w.is_dma) and w.stream == 'pe' and o.stream == 'pe':
                    continue
                cand.append(w)
            if isinstance(key, str) and key.startswith('ps') and key[2:].isdigit():
                for r in self.readers.get(key, ()):
                    if r.stream != o.stream:
                        cand.append(r)
        for key in writes:
            w = self.last_w.get(key)
            if w is not None:
                if not (comp and (not w.is_dma) and w.stream == o.stream and o.stream == 'pe'):
                    cand.append(w)
            for r in self.readers.get(key, ()):
                if comp and (not r.is_dma) and r.stream == o.stream and o.stream == 'pe':
                    continue
                cand.append(r)
        if o.is_dma:
            n = self.dma_count[o.q]
            o.dseq = n
            if n >= NDSEM:
                cand.append(self.dma_ops[o.q][n - NDSEM])
            self.dma_count[o.q] = n + 1
            self.dma_ops[o.q].append(o)
            o.signal = True
        best = {}
        dl = []
        kd = self.known_dma[o.stream]
        kn = self.known[o.stream]
        for d in cand:
            if d is o:
                continue
            if d.is_dma:
                if id(d) not in kd:
                    kd.add(id(d))
                    dl.append(d)
            else:
                b = best.get(d.stream)
                if b is None or d.idx > b.idx:
                    best[d.stream] = d
        for s, d in best.items():
            if kn.get(s, -1) < d.idx:
                kn[s] = d.idx
                d.signal = True
                dl.append(d)
        o.deps = dl
        for key in reads:
            self.readers.setdefault(key, []).append(o)
        for key in writes:
            self.last_w[key] = o
            self.readers[key] = []
        st.append(o)
        return o

    def barrier(self):
        streams = COMPUTE
        lasts = []
        for s in streams:
            for o in reversed(self.streams[s]):
                if (not o.is_dma) and o.fn is not None:
                    lasts.append(o)
                    break
        dmas = list(self.dma_ops['gq'][-NDSEM:])
        for s in streams:
            o = Op()
            o.is_dma = False
            o.q = None
            o.stream = s
            o.fn = None
            o.idx = len(self.streams[s])
            for l in lasts:
                if l.stream == s and s == 'pe':
                    continue
                if self.known[s].get(l.stream, -1) < l.idx:
                    self.known[s][l.stream] = l.idx
                    l.signal = True
                    o.deps.append(l)
            for d in dmas:
                if id(d) not in self.known_dma[s]:
                    self.known_dma[s].add(id(d))
                    o.deps.append(d)
            self.streams[s].append(o)

    def emit(self):
        nc = self.nc
        ctxs = [nc.semaphore('sem_' + s) for s in COMPUTE]
        for q in DMAQ:
            ctxs += [nc.semaphore('dsem_%s_%d' % (q, i)) for i in range(NDSEM)]
        handles = [c.__enter__() for c in ctxs]
        hs = {s: handles[i] for i, s in enumerate(COMPUTE)}
        dh = {}
        i = len(COMPUTE)
        for q in DMAQ:
            dh[q] = handles[i:i + NDSEM]
            i += NDSEM
        for s in COMPUTE:
            cnt = 0
            for o in self.streams[s]:
                if o.is_dma or o.fn is None:
                    continue
                if o.signal:
                    cnt += 1
                    o.sig = (hs[s], cnt, 1)
        for q in DMAQ:
            for o in self.dma_ops[q]:
                o.sig = (dh[q][o.dseq % NDSEM], 16 * (o.dseq // NDSEM + 1), 16)
        with nc.Block() as block:
            def run(s, e):
                for o in self.streams[s]:
                    for d in o.deps:
                        e.wait_ge(d.sig[0], d.sig[1])
                    if o.fn is None:
                        continue
                    ins = o.fn()
                    if o.signal:
                        ins.then_inc(o.sig[0], o.sig[2])

            @block.tensor
            def _(e):
                run('pe', e)

            @block.scalar
            def _(e):
                run('act', e)

            @block.vector
            def _(e):
                run('dve', e)

            @block.gpsimd
            def _(e):
                run('pool', e)

            @block.sync
            def _(e):
                run('sp', e)
        for c in reversed(ctxs):
            c.__exit__(None, None, None)


class Tl:
    __slots__ = ('ap', 'key')

    def __init__(self, ap, key):
        self.ap = ap
        self.key = key

    def __getitem__(self, idx):
        return self.ap[idx]


def _keys(lst):
    out = []
    for x in lst:
        if x is None:
            continue
        out.append(x.key if isinstance(x, Tl) else x)
    return out


def _t5_bucket(n):
    n = np.asarray(n, dtype=np.int64)
    half, exact = 16, 8
    side = np.where(n < 0, half, 0)
    na = np.abs(n)
    val = (np.log(np.maximum(na, 1).astype(np.float64) / exact) / math.log(128 / exact) * (half - exact))
    r = np.round(val)
    val = np.where(np.abs(val - r) < 1e-5, r, val)
    log_b = exact + np.floor(val).astype(np.int64)
    log_b = np.where(val < 0, exact + np.ceil(val).astype(np.int64), log_b)
    return side + np.where(na < exact, na, np.minimum(log_b, half - 1))


def _rope_tables(pos):
    half = 16
    inv = (10000.0 ** (-(np.arange(half, dtype=np.float32)) / np.float32(half))).astype(np.float32)
    ang = (pos.astype(np.float32)[None, :] * inv[:, None]).astype(np.float32)
    c = np.cos(ang.astype(np.float64)).astype(np.float32)
    s = np.sin(ang.astype(np.float64)).astype(np.float32)
    cos = np.concatenate([c, c], 0)
    sin = np.concatenate([-s, s], 0)
    return np.ascontiguousarray(cos), np.ascontiguousarray(sin)


def _gdn_masks(ub, c):
    idx = np.arange(ub)
    same = (idx[:, None] // c) == (idx[None, :] // c)
    tri = (same & (idx[:, None] <= idx[None, :])).astype(np.float32)
    blk = same.astype(np.float32)
    mA = np.where(same & (idx[:, None] > idx[None, :]), 0.0, NEG).astype(np.float32)
    mT = np.where(same & (idx[None, :] >= idx[:, None]), 0.0, NEG).astype(np.float32)
    return tri, blk, mA, mT


def build_program():
    nc = bass.Bass("TRN2", target_bir_lowering=False)
    P = Prog(nc)

    def din(name, shape):
        return nc.dram_tensor(name, list(shape), F32, kind="ExternalInput").ap()

    def dout(name, shape):
        return nc.dram_tensor(name, list(shape), F32, kind="ExternalOutput").ap()

    I = {}
    I['xp'] = din('xp', [NPS, TP, D])
    I['xs'] = din('xs', [NSS, TS, D])
    I['c6'] = din('c6', [NSEQ, D])
    I['st_gconv'] = din('st_gconv', [2, NSS, 3, 3072])
    I['st_gS'] = din('st_gS', [2, NSS, 8, 128, 128])
    I['c_lat'] = din('c_lat', [NSS, PAST, 256])
    I['c_kr'] = din('c_kr', [NSS, PAST, 32])
    I['c_sk'] = din('c_sk', [NSS, 128, 256])
    I['c_sv'] = din('c_sv', [NSS, 128, 256])
    I['st_fconv'] = din('st_fconv', [DEPTH, NSS, 2, DFF])
    I['ada_w'] = din('ada_w', [DEPTH, D, 6 * D])
    I['ada_b'] = din('ada_b', [DEPTH, 6 * D])
    I['norm1'] = din('norm1', [DEPTH, D])
    I['norm2'] = din('norm2', [DEPTH, D])
    I['final_norm'] = din('final_norm', [D])
    I['gdn_w_in'] = din('gdn_w_in', [2, D, 4112])
    I['gdn_conv_w'] = din('gdn_conv_w', [2, 4, 3072])
    I['gdn_a_log'] = din('gdn_a_log', [2, 8])
    I['gdn_dt_bias'] = din('gdn_dt_bias', [2, 8])
    I['gdn_o_norm'] = din('gdn_o_norm', [2, 128])
    I['gdn_w_o'] = din('gdn_w_o', [2, D, D])
    I['mla_w_in'] = din('mla_w_in', [1, D, 672])
    I['mla_q_norm'] = din('mla_q_norm', [1, 384])
    I['mla_kv_norm'] = din('mla_kv_norm', [1, 256])
    I['mla_w_q_up'] = din('mla_w_q_up', [1, 384, 1536])
    I['mla_w_kv_up'] = din('mla_w_kv_up', [1, 256, 2048])
    I['mla_w_o'] = din('mla_w_o', [1, D, D])
    I['swa_w_in'] = din('swa_w_in', [1, D, 1536])
    I['swa_sinks'] = din('swa_sinks', [1, 16])
    I['swa_w_o'] = din('swa_w_o', [1, D, D])
    I['ffn_w_in'] = din('ffn_w_in', [DEPTH, D, 2 * DFF])
    I['ffn_conv_w'] = din('ffn_conv_w', [DEPTH, 3, DFF])
    I['ffn_conv_b'] = din('ffn_conv_b', [DEPTH, DFF])
    I['ffn_w_out'] = din('ffn_w_out', [DEPTH, DFF, D])
    I['k_ident'] = din('k_ident', [128, 128])
    for nm, ub in (('p', 128), ('s', 32)):
        for t in ('tri', 'blk', 'mA', 'mT'):
            I['k_%s_%s' % (t, nm)] = din('k_%s_%s' % (t, nm), [ub, ub])
    I['k_cos_p'] = din('k_cos_p', [32, TP])
    I['k_sin_p'] = din('k_sin_p', [32, TP])
    I['k_cos_s'] = din('k_cos_s', [32, TS])
    I['k_sin_s'] = din('k_sin_s', [32, TS])
    I['k_ba_p'] = din('k_ba_p', [128, 4, 256])
    I['k_ba1_p'] = din('k_ba1_p', [128, 4, 256])
    I['k_bb_p'] = din('k_bb_p', [64, 4, 256])
    I['k_ba_s'] = din('k_ba_s', [128, 4, 128])
    I['k_bb_s'] = din('k_bb_s', [32, 4, 128])

    O = {}
    O['y_p'] = dout('y_p', [NPS, TP, D])
    O['y_s'] = dout('y_s', [NSS, TS, D])
    O['gconv_p'] = dout('gconv_p', [2, NPS, 3, 3072])
    O['gconv_s'] = dout('gconv_s', [2, NSS, 3, 3072])
    O['gS_p'] = dout('gS_p', [2, NPS, 8, 128, 128])
    O['gS_s'] = dout('gS_s', [2, NSS, 8, 128, 128])
    O['lat_p'] = dout('lat_p', [NPS, TP, 256])
    O['lat_s'] = dout('lat_s', [NSS, TS, 256])
    O['kr_p'] = dout('kr_p', [NPS, TP, 32])
    O['kr_s'] = dout('kr_s', [NSS, TS, 32])
    O['sk_p'] = dout('sk_p', [NPS, 128, 256])
    O['sk_s'] = dout('sk_s', [NSS, 128, 256])
    O['sv_p'] = dout('sv_p', [NPS, 128, 256])
    O['sv_s'] = dout('sv_s', [NSS, 128, 256])
    O['fconv_p'] = dout('fconv_p', [DEPTH, NPS, 2, DFF])
    O['fconv_s'] = dout('fconv_s', [DEPTH, NSS, 2, DFF])

    import contextlib
    es = contextlib.ExitStack()
    uid = [0]

    def sb(name, cols, dt=F32, parts=128):
        t = es.enter_context(nc.sbuf_tensor(name, [parts, cols], dt))
        return t

    xT_t = sb('xT', KC * TP, F32)
    xT = Tl(xT_t[:, :].rearrange("p (k t) -> p k t", k=KC), 'xT')
    hT_t = sb('hT', KC * TP, BF16)
    hT = Tl(hT_t[:, :].rearrange("p (k t) -> p k t", k=KC), 'hT')
    WCOLS = 2048
    wst = [Tl(sb('wst%d' % i, WCOLS, F32)[:, :], 'wst%d' % i) for i in range(2)]
    wbf = [Tl(sb('wbf%d' % i, WCOLS, BF16)[:, :], 'wbf%d' % i) for i in range(3)]
    wctr = [0, 0]
    identF = Tl(sb('identF', 128, F32)[:, :], 'identF')
    identB = Tl(sb('identB', 128, BF16)[:, :], 'identB')
    onesB = Tl(sb('onesB', 128, BF16)[:, :], 'onesB')
    onesF = Tl(sb('onesF', 128, F32)[:, :], 'onesF')
    cvec = Tl(sb('cvec', 8, F32)[:, :], 'cvec')
    modT_t = sb('modT', DEPTH * 48 * NSEQ, F32)
    modT = Tl(modT_t[:, :].rearrange("p (l j s) -> p l j s", l=DEPTH, j=48), 'modT')
    am_t = sb('amul', 2 * DEPTH * KC * NSEQ, F32)
    amul = Tl(am_t[:, :].rearrange("p (w l k s) -> p w l k s", w=2, l=DEPTH, k=KC), 'amul')
    nrm_t = sb('nrmw', (2 * DEPTH + 1) * KC, F32)
    nrmw = Tl(nrm_t[:, :].rearrange("p (w k) -> p w k", k=KC), 'nrmw')
    bar_t = Tl(sb('bar', 4, F32)[:, :], 'BAR')
    ARENA = 19000
    arena_t = sb('arena', ARENA, F32)
    apos = [0]
    aphase = [0]

    def aalloc(name, cols, dt=F32, parts=128, shape=None):
        nf = cols if dt == F32 else (cols + 1) // 2
        nf = (nf + 7) // 8 * 8
        assert apos[0] + nf <= ARENA, (name, apos[0], nf)
        ap = arena_t[0:parts, apos[0]:apos[0] + nf]
        apos[0] += nf
        if dt != F32:
            ap = ap.bitcast(dt)
        ap = ap[:, 0:cols]
        if shape is not None:
            ap = ap.rearrange(shape[0], **shape[1])
        return Tl(ap, '%s@%d' % (name, aphase[0]))

    def phase_end():
        P.barrier()
        P.op('dve', lambda: nc.vector.memset(bar_t[:, 0:1], 0.0), writes=['BAR'])
        apos[0] = 0
        aphase[0] += 1

    psb = [Tl(es.enter_context(nc.psum_tensor('ps%d' % i, [128, 512], F32))[:, :], 'ps%d' % i) for i in range(8)]
    psctr = [0]

    psrot = [6]

    def psum(nreads=1):
        n = psrot[0]
        for _ in range(n):
            t = psb[psctr[0] % n]
            psctr[0] += 1
            if P.ps_pending.get(t.key, 0) <= 0:
                break
        if not P.muted:
            P.ps_pending[t.key] = nreads
        return t

    ACC0, ACC1 = psb[6], psb[7]
    ACCP = [(psb[4], psb[5]), (psb[6], psb[7])]
    cast_engs = ['act']
    cast_ctr = [0]

    def wcast(bv, sv, s_t, b_t):
        e = cast_engs[cast_ctr[0] % len(cast_engs)]
        cast_ctr[0] += 1
        if e == 'pool':
            P.op('pool', lambda: nc.gpsimd.tensor_copy(out=bv, in_=sv), _keys([s_t]), _keys([b_t]))
        elif e == 'dve':
            P.op('dve', lambda: nc.vector.tensor_copy(out=bv, in_=sv), _keys([s_t]), _keys([b_t]))
        else:
            P.op('act', lambda: nc.scalar.copy(out=bv, in_=sv), _keys([s_t]), _keys([b_t]))

    def mm(out, lhsT, rhs, start=True, stop=True, r=(), w=()):
        return P.op('pe', lambda: nc.tensor.matmul(out, lhsT, rhs, start=start, stop=stop), _keys(r), _keys(w))

    def tr(out, in_, ident, r=(), w=()):
        return P.op('pe', lambda: nc.tensor.transpose(out, in_, ident), _keys(r), _keys(w))

    def act(out, in_, func, bias=None, scale=1.0, r=(), w=()):
        if bias is None:
            return P.op('act', lambda: nc.scalar.activation(out=out, in_=in_, func=func, scale=scale),
                        _keys(r), _keys(w))
        return P.op('act', lambda: nc.scalar.activation(out=out, in_=in_, func=func, bias=bias, scale=scale),
                    _keys(r), _keys(w))

    def tt(out, in0, in1, op, r=(), w=(), eng='dve'):
        e = nc.vector if eng == 'dve' else nc.gpsimd
        return P.op(eng, lambda: e.tensor_tensor(out=out, in0=in0, in1=in1, op=op), _keys(r), _keys(w))

    def ts(out, in0, s1, op0, s2=None, op1=None, r=(), w=(), eng='dve'):
        e = nc.vector if eng == 'dve' else nc.gpsimd
        if op1 is None:
            return P.op(eng, lambda: e.tensor_scalar(out=out, in0=in0, scalar1=s1, scalar2=None, op0=op0),
                        _keys(r), _keys(w))
        return P.op(eng, lambda: e.tensor_scalar(out=out, in0=in0, scalar1=s1, scalar2=s2, op0=op0, op1=op1),
                    _keys(r), _keys(w))

    def stt(out, in0, scalar, in1, op0, op1, r=(), w=()):
        return P.op('dve', lambda: nc.vector.scalar_tensor_tensor(out=out, in0=in0, scalar=scalar, in1=in1,
                                                                  op0=op0, op1=op1), _keys(r), _keys(w))

    def cp(out, in_, r=(), w=(), eng='dve'):
        if eng == 'act':
            return P.op('act', lambda: nc.scalar.copy(out=out, in_=in_), _keys(r), _keys(w))
        e = nc.vector if eng == 'dve' else nc.gpsimd
        return P.op(eng, lambda: e.tensor_copy(out=out, in_=in_), _keys(r), _keys(w))

    def recip(out, in_, r=(), w=()):
        return P.op('dve', lambda: nc.vector.reciprocal(out=out, in_=in_), _keys(r), _keys(w))

    def mset(ap, val, w=(), eng='dve'):
        e = nc.vector if eng == 'dve' else nc.gpsimd
        return P.op(eng, lambda: e.memset(ap, val), (), _keys(w))

    def ld(out, in_, w=(), r=(), slow=False):
        return P.op('sp', lambda: nc.sync.dma_start(out=out, in_=in_, allow_slow_non_contiguous=slow),
                    _keys(r), _keys(w))

    def stq(out, in_, r=(), w=(), slow=False):
        return P.op('gq', lambda: nc.gpsimd.dma_start(out=out, in_=in_, allow_slow_non_contiguous=slow),
                    _keys(r), _keys(w))

    def wload(dram_ap, shape_str, **dims):
        n = 1
        for s in dram_ap.shape[1:]:
            n *= s
        assert n <= WCOLS, dram_ap.shape
        s_t = wst[wctr[0] % 2]
        wctr[0] += 1
        b_t = wbf[wctr[1] % 3]
        wctr[1] += 1
        parts = dram_ap.shape[0]
        sv = s_t.ap[0:parts, 0:n]
        bv = b_t.ap[0:parts, 0:n]
        if shape_str is not None:
            svv = sv.rearrange(shape_str, **dims)
            bvv = bv.rearrange(shape_str, **dims)
        else:
            svv, bvv = sv, bv
        ld(svv, dram_ap, w=[s_t])
        wcast(bv, sv, s_t, b_t)
        return Tl(bvv, b_t.key)

    def rsqrt_from(out_sb, in_ps, mean_scale, rk, wk, tmp):
        act(tmp, in_ps, AF.Ln, bias=cvec[0:in_ps.shape[0], 0:1], scale=mean_scale, r=rk + [cvec], w=wk[1:])
        act(out_sb, tmp, AF.Exp, scale=-0.5, r=wk[1:], w=wk[0:1])

    ld(identF.ap, I['k_ident'], w=[identF])
    cp(identB.ap, identF.ap, r=[identF], w=[identB])
    mset(onesB.ap, 1.0, w=[onesB])
    mset(onesF.ap, 1.0, w=[onesF])
    mset(cvec[:, 0:1], EPS, w=[cvec])
    mset(cvec[:, 1:2], 1.0, w=[cvec])
    mset(cvec[:, 2:3], 0.0, w=[cvec])
    mset(bar_t[:, 0:1], 0.0, w=['BAR'])
    ld(nrmw[:, 0:4, :], I['norm1'].rearrange("l (k p) -> p l k", p=128), w=[nrmw], slow=True)
    ld(nrmw[:, 4:8, :], I['norm2'].rearrange("l (k p) -> p l k", p=128), w=[nrmw], slow=True)
    ld(nrmw[:, 8, :], I['final_norm'].rearrange("(k p) -> p k", p=128), w=[nrmw], slow=True)
    csT = aalloc('csT', KC * NSEQ, F32, shape=("p (k s) -> p k s", dict(k=KC)))
    adab = aalloc('adab', DEPTH * 48, F32, shape=("p (l j) -> p l j", dict(l=DEPTH)))
    for k in range(KC):
        ld(csT[:, k, :], I['c6'][:, k * 128:(k + 1) * 128].rearrange("s p -> p s"), w=[csT], slow=True)
    for l in range(DEPTH):
        ld(adab[:, l, :], I['ada_b'][l].rearrange("(j p) -> p j", p=128), w=[adab], slow=True)
    act(csT.ap, csT.ap, AF.Silu, r=[csT], w=[csT])
    for l in range(DEPTH):
        for ct in range(24):
            s_t = wst[wctr[0] % 2]
            wctr[0] += 1
            sv = s_t.ap[:, 0:2048].rearrange("p (k c) -> p k c", k=KC)
            ld(sv, I['ada_w'][l][:, ct * 256:(ct + 1) * 256].rearrange("(k p) c -> p k c", p=128), w=[s_t])
            for fc in range(2):
                ps = psum()
                for k in range(KC):
                    mm(ps[:, 0:NSEQ], sv[:, k, fc * 128:(fc + 1) * 128], csT[:, k, :], start=(k == 0),
                       stop=(k == KC - 1), r=[s_t, csT], w=[ps])
                j = ct * 2 + fc
                ts(modT[:, l, j, :], ps[:, 0:NSEQ], adab[:, l, j:j + 1], ALU.add, r=[ps, adab], w=[modT])
    for l in range(DEPTH):
        for wi, j0 in ((0, 8), (1, 32)):
            for k in range(KC):
                ts(amul[:, wi, l, k, :], modT[:, l, j0 + k, :], 1.0, ALU.add, nrmw[:, wi * 4 + l, k:k + 1], ALU.mult,
                   r=[modT, nrmw], w=[amul])
    phase_end()

    def run_seq(si):
        prompt = si < NPS
        li = si if prompt else si - NPS
        T = TP if prompt else TS
        TT = min(512, T)
        nTT = T // TT
        nm = 'p' if prompt else 's'
        xin_d = I['xp'][li] if prompt else I['xs'][li]
        tsz = min(128, T)
        cast_engs[:] = ['act'] if prompt else ['act', 'dve']
        nblk = T // tsz

        def mod(l, j, k):
            return modT[:, l, j * 8 + k, si:si + 1]

        xin = [aalloc('xin%d' % i, D, F32) for i in range(2)]
        for b in range(nblk):
            xi = xin[b % 2]
            ld(xi[0:tsz, :], xin_d[b * tsz:(b + 1) * tsz, :], w=[xi], r=['BAR'])
            for half in range(2):
                ps = psum()
                for q in range(4):
                    k = half * 4 + q
                    tr(ps[:, q * 128:q * 128 + tsz], xi[0:tsz, k * 128:(k + 1) * 128], identF[0:tsz, 0:tsz],
                       r=[xi, identF], w=[ps])
                cp(xT[:, half * 4:half * 4 + 4, b * tsz:(b + 1) * tsz],
                   ps.ap.rearrange("p (q t) -> p q t", q=4)[:, :, 0:tsz], r=[ps], w=[xT],
                   eng=('dve' if half == 0 else 'act'))
        phase_end()

        def norm_mod(l, which):
            psrot[0] = 8
            sq = [aalloc('nsq%d' % i, KC * TT, BF16, shape=("p (k t) -> p k t", dict(k=KC))) for i in range(2)]
            rs = [aalloc('nrs%d' % i, TT, F32) for i in range(2)]
            tmp = [aalloc('ntmp%d' % i, TT, F32) for i in range(2)]
            t2 = [aalloc('nt2%d' % i, TT, F32) for i in range(3)]
            c = 0
            for j in range(nTT):
                sl = slice(j * TT, (j + 1) * TT)
                s_ = sq[j % 2]
                sk_ = [Tl(s_[:, k, :], (s_.key, k)) for k in range(KC)]
                for k in range(KC):
                    if k % 2 == 0:
                        act(sk_[k].ap, xT[:, k, sl], AF.Square, r=[xT], w=[sk_[k]])
                    else:
                        tt(sk_[k].ap, xT[:, k, sl], xT[:, k, sl], ALU.mult, r=[xT], w=[sk_[k]])
                ps = psum()
                for k in range(KC):
                    mm(ps[:, 0:TT], onesB.ap, sk_[k].ap, start=(k == 0), stop=(k == KC - 1), r=[onesB, sk_[k]], w=[ps])
                r_ = rs[j % 2]
                tm = tmp[j % 2]
                rsqrt_from(r_.ap, ps[:, 0:TT], 1.0 / D, [ps], [r_, tm], tm.ap)
                for k in range(KC):
                    u = t2[c % 3]
                    c += 1
                    stt(u.ap, xT[:, k, sl], amul[:, which, l, k, si:si + 1], r_.ap, ALU.mult, ALU.mult,
                        r=[xT, amul, r_], w=[u])
                    shift = mod(l, 0 if which == 0 else 3, k)
                    act(hT[:, k, sl], u.ap, AF.Identity, bias=shift, scale=1.0, r=[u, modT], w=[hT])
            phase_end()

        def resid_add(ps, l, gj, k, sl):
            stt(xT[:, k, sl], ps[:, 0:TT], mod(l, gj, k), xT[:, k, sl], ALU.mult, ALU.add, r=[ps, modT, xT], w=[xT])

        def gdn(l, slot):
            psrot[0] = 8
            UB = min(128, T)
            C = 64 if prompt else 32
            cpu = UB // C
            nun = T // UB
            upt = TT // UB
            W = I['gdn_w_in'][slot]
            tri = aalloc('tri', UB, F32, parts=UB)
            blk = aalloc('blk', UB, F32, parts=UB)
            mA = aalloc('mA', UB, F32, parts=UB)
            mT = aalloc('mT', UB, F32, parts=UB)
            for t_, n_ in ((tri, 'tri'), (blk, 'blk'), (mA, 'mA'), (mT, 'mT')):
                ld(t_.ap, I['k_%s_%s' % (n_, nm)], w=[t_], r=['BAR'])
            cw = aalloc('gcw', 24 * 4, F32, shape=("p (c j) -> p c j", dict(c=24)))
            for j in range(4):
                ld(cw[:, :, j], I['gdn_conv_w'][slot][j].rearrange("(c p) -> p c", p=128), w=[cw], r=['BAR'], slow=True)
            onw = aalloc('onw', 1, F32)
            ld(onw.ap, I['gdn_o_norm'][slot].rearrange("(p o) -> p o", o=1), w=[onw], r=['BAR'], slow=True)
            alb = aalloc('alb', 16, F32)
            ld(alb[:, 0:8], I['gdn_a_log'][slot].partition_broadcast(128), w=[alb], r=['BAR'])
            ld(alb[:, 8:16], I['gdn_dt_bias'][slot].partition_broadcast(128), w=[alb], r=['BAR'])
            act(alb[:, 0:8], alb[:, 0:8], AF.Exp, r=[alb], w=[alb])
            ts(alb[:, 0:8], alb[:, 0:8], -1.0, ALU.mult, r=[alb], w=[alb])
            gb = aalloc('gb', nun * 56, F32, shape=("p (u j) -> p u j", dict(u=nun)))
            wab = wload(W[:, 4096:4112].rearrange("(k p) c -> p k c", p=128), "p (k c) -> p k c", k=KC)
            psab = psum(2)
            for u in range(nun):
                for k in range(KC):
                    mm(psab[0:UB, u * 16:(u + 1) * 16], hT[:, k, u * UB:(u + 1) * UB], wab[:, k, :], start=(k == 0),
                       stop=(k == KC - 1), r=[hT, wab], w=[psab])
            pv = psab.ap[0:UB, 0:nun * 16].rearrange("p (u j) -> p u j", u=nun)
            tgmark = apos[0]
            tg = aalloc('tg', nun * 16, F32, shape=("p (u j) -> p u j", dict(u=nun)))
            tt(tg[0:UB, :, 0:8], pv[:, :, 0:8], alb[0:UB, 8:16].unsqueeze(1).broadcast_to([UB, nun, 8]), ALU.add,
               r=[psab, alb], w=[tg])
            act(tg[0:UB, :, 0:8], tg[0:UB, :, 0:8], AF.Exp, r=[tg], w=[tg])
            act(tg[0:UB, :, 0:8], tg[0:UB, :, 0:8], AF.Ln, bias=cvec[0:UB, 1:2], r=[tg, cvec], w=[tg])
            tt(gb[0:UB, :, 0:8], tg[0:UB, :, 0:8], alb[0:UB, 0:8].unsqueeze(1).broadcast_to([UB, nun, 8]), ALU.mult,
               r=[tg, alb], w=[gb])
            act(tg[0:UB, :, 8:16], pv[:, :, 8:16], AF.Exp, scale=-1.0, r=[psab], w=[tg])
            ts(tg[0:UB, :, 8:16], tg[0:UB, :, 8:16], 1.0, ALU.add, r=[tg], w=[tg])
            recip(gb[0:UB, :, 48:56], tg[0:UB, :, 8:16], r=[tg], w=[gb])
            ts(gb[0:UB, :, 8:16], gb[0:UB, :, 48:56], -1.0, ALU.mult, r=[gb], w=[gb])
            psg = psum(2)
            psl = psum(1)
            for u in range(nun):
                mm(psg[0:UB, u * 8:(u + 1) * 8], tri.ap, gb[0:UB, u, 0:8], r=[tri, gb], w=[psg])
                mm(psl[0:UB, u * 8:(u + 1) * 8], blk.ap, gb[0:UB, u, 0:8], r=[blk, gb], w=[psl])
            pg = psg.ap[0:UB, 0:nun * 8].rearrange("p (u j) -> p u j", u=nun)
            pl = psl.ap[0:UB, 0:nun * 8].rearrange("p (u j) -> p u j", u=nun)
            cp(gb[0:UB, :, 16:24], pg, r=[psg], w=[gb])
            ts(gb[0:UB, :, 24:32], pg, -1.0, ALU.mult, r=[psg], w=[gb])
            tt(tg[0:UB, :, 0:8], pl, gb[0:UB, :, 16:24], ALU.subtract, r=[psl, gb], w=[tg])
            act(gb[0:UB, :, 32:40], tg[0:UB, :, 0:8], AF.Exp, r=[tg], w=[gb])
            act(tg[0:UB, :, 8:16], gb[0:UB, :, 16:24], AF.Exp, r=[gb], w=[tg])
            tt(gb[0:UB, :, 40:48], tg[0:UB, :, 8:16], gb[0:UB, :, 48:56], ALU.mult, r=[tg, gb], w=[gb])

            P.barrier()
            apos[0] = tgmark
            if DBG['gdn_stop'] == 1:
                phase_end()
                return
            obuf = aalloc('obuf', T, BF16)
            S = aalloc('S', 128, F32)
            Sb = aalloc('Sb', 128, BF16)
            cbuf = [aalloc('cb%d' % i, TT + 3, F32) for i in range(3)]
            acc3 = [aalloc('acc%d' % i, TT, F32) for i in range(3)]
            sqb = aalloc('gsq', TT, BF16)
            rnb = aalloc('grn', TT, F32)
            sqbk = aalloc('gsqk', TT, BF16)
            rnbk = aalloc('grnk', TT, F32)
            sqb2 = aalloc('gsq2', TT, BF16)
            rnb2 = aalloc('grn2', TT, F32)
            qTt = aalloc('qT', TT, BF16)
            kTt = aalloc('kT', TT, BF16)
            vTt = aalloc('vT', TT, BF16)
            oT = aalloc('oT', TT, F32)
            U4 = upt
            u3 = ("p (u c) -> p u c", dict(u=U4))
            kbg = aalloc('kbg', U4 * 128, BF16, shape=u3)
            vb = aalloc('vb', U4 * 128, BF16, shape=u3)
            trig = aalloc('trig', U4 * UB, F32, shape=u3)
            argT = aalloc('argT', U4 * UB, F32, shape=u3)
            argA = aalloc('argA', U4 * UB, F32, shape=u3)
            eg = aalloc('eg', U4 * UB, F32, shape=u3)
            NA = [aalloc('NA%d' % i, U4 * UB, F32, shape=u3) for i in range(2)]
            NT = [aalloc('NT%d' % i, U4 * UB, F32, shape=u3) for i in range(2)]
            TTm = [aalloc('TTm%d' % i, U4 * UB, F32, shape=u3) for i in range(2)]
            PTb = aalloc('PTb', U4 * UB, BF16, shape=u3)
            ub_ = aalloc('ub', 128, BF16)
            wob = aalloc('wob', 1024, BF16)
            HB = []
            for i in range(2):
                HB.append(dict(
                    QKT=aalloc('QKT%d' % i, U4 * UB, BF16, shape=u3),
                    qg=aalloc('qg%d' % i, U4 * UB, BF16, shape=u3),
                    wv=aalloc('wv%d' % i, U4 * 128, BF16, shape=u3),
                    wkT=aalloc('wkT%d' % i, U4 * UB, BF16, shape=u3),
                    kd=aalloc('kd%d' % i, U4 * 128, BF16, shape=u3),
                    zs=aalloc('zs%d' % i, TT, BF16),
                    egl=aalloc('egl%d' % i, U4 * cpu, F32),
                ))
            gout = (O['gconv_p'] if prompt else O['gconv_s'])[slot, li]
            Sout = (O['gS_p'] if prompt else O['gS_s'])[slot, li]
            Wo = I['gdn_w_o'][slot]
            hw_ = {}

            def _pc(c0):
                return W[:, c0:c0 + 128].rearrange("(k p) c -> p k c", p=128)

            def stageA(h, j, hb):
                B_ = HB[hb]
                QKT, qg, wv, wkT, kd, zs, egl = B_['QKT'], B_['qg'], B_['wv'], B_['wkT'], B_['kd'], B_['zs'], B_['egl']
                if j == 0:
                    hw_['wqk'] = wload_pieces(128, 2048, [(lambda v: v[:, :, 0, :], _pc(h * 128)),
                                                          (lambda v: v[:, :, 1, :], _pc(1024 + h * 128))],
                                              "p (k t c) -> p k t c", k=KC, t=2)
                    hw_['wvz'] = wload_pieces(128, 2048, [(lambda v: v[:, :, 0, :], _pc(2048 + h * 128)),
                                                          (lambda v: v[:, :, 1, :], _pc(3072 + h * 128))],
                                              "p (k t c) -> p k t c", k=KC, t=2)
                    if prompt:
                        for pi in range(3):
                            mset(cbuf[pi][:, 0:3], 0.0, w=[cbuf[pi]])
                    else:
                        for pi in range(3):
                            c0 = pi * 1024 + h * 128
                            ld(cbuf[pi][:, 0:3], I['st_gconv'][slot, li][:, c0:c0 + 128].rearrange("r c -> c r"),
                               w=[cbuf[pi]], r=['BAR'], slow=True)
                    yield
                wqk, wvz = hw_['wqk'], hw_['wvz']
                sl = slice(j * TT, (j + 1) * TT)
                pps = []
                for pi in range(4):
                    wt = wqk if pi < 2 else wvz
                    wi = pi if pi < 2 else pi - 2
                    ps = psum()
                    for k in range(KC):
                        mm(ps[:, 0:TT], wt[:, k, wi, :], hT[:, k, sl], start=(k == 0), stop=(k == KC - 1),
                           r=[wt, hT], w=[ps])
                    pps.append(ps)
                    yield
                accs = []
                for pi in range(3):
                    ps = pps[pi]
                    cb = cbuf[pi]
                    cp(cb[:, 3:3 + TT], ps[:, 0:TT], r=[ps], w=[cb], eng='act')
                    ch = pi * 8 + h
                    a_ = acc3[pi]
                    ts(a_.ap, cb[:, 0:TT], cw[:, ch, 0:1], ALU.mult, r=[cb, cw], w=[a_])
                    for jj in range(1, 4):
                        stt(a_.ap, cb[:, jj:jj + TT], cw[:, ch, jj:jj + 1], a_.ap, ALU.mult, ALU.add,
                            r=[cb, cw, a_], w=[a_])
                    yield
                    if j == nTT - 1:
                        c0 = pi * 1024 + h * 128
                        stq(gout[:, c0:c0 + 128].rearrange("r c -> c r"), cb[:, TT:TT + 3], r=[cb], w=['o_gconv'],
                            slow=True)
                    else:
                        cp(cb[:, 0:3], cb[:, TT:TT + 3], r=[cb], w=[cb], eng='act')
                    act(a_.ap, a_.ap, AF.Silu, r=[a_], w=[a_])
                    if pi < 2:
                        sq_ = sqb if pi == 0 else sqbk
                        act(sq_.ap, a_.ap, AF.Square, r=[a_], w=[sq_])
                    else:
                        cp(vTt.ap, a_.ap, r=[a_], w=[vTt], eng='act')
                    yield
                act(zs.ap, pps[3][:, 0:TT], AF.Silu, r=[pps[3]], w=[zs])
                for pi in range(2):
                    a_ = acc3[pi]
                    sq_ = sqb if pi == 0 else sqbk
                    rn_ = rnb if pi == 0 else rnbk
                    ps2 = psum()
                    mm(ps2[:, 0:TT], onesB.ap, sq_.ap, r=[onesB, sq_], w=[ps2])
                    rsqrt_from(rn_.ap, ps2[:, 0:TT], 1.0, [ps2], [rn_, rn_], rn_.ap)
                    if pi == 0:
                        stt(qTt.ap, a_.ap, 128.0 ** -0.5, rn_.ap, ALU.mult, ALU.mult, r=[a_, rn_], w=[qTt])
                    else:
                        tt(kTt.ap, a_.ap, rn_.ap, ALU.mult, r=[a_, rn_], w=[kTt])
                    yield
                U = list(range(upt))
                pk = psum(3 * upt)
                pkb = pk.ap.bitcast(BF16)
                for uu in U:
                    us = slice(uu * UB, (uu + 1) * UB)
                    tr(pkb[0:UB, uu * 256:uu * 256 + 128], kTt[:, us], identB.ap, r=[kTt, identB], w=[pk])
                    tr(pkb[0:UB, uu * 256 + 128:uu * 256 + 256], vTt[:, us], identB.ap, r=[vTt, identB], w=[pk])
                yield
                for uu in U:
                    u = j * upt + uu
                    act(kbg[0:UB, uu, :], pkb[0:UB, uu * 256:uu * 256 + 128], AF.Copy, scale=gb[0:UB, u, 40 + h:41 + h],
                        r=[pk, gb], w=[kbg])
                    act(kd[0:UB, uu, :], pkb[0:UB, uu * 256:uu * 256 + 128], AF.Copy, scale=gb[0:UB, u, 32 + h:33 + h],
                        r=[pk, gb], w=[kd])
                    act(vb[0:UB, uu, :], pkb[0:UB, uu * 256 + 128:uu * 256 + 256], AF.Copy,
                        scale=gb[0:UB, u, 48 + h:49 + h], r=[pk, gb], w=[vb])
                    ts(trig[0:UB, uu, :], tri.ap, gb[0:UB, u, h:h + 1], ALU.mult, r=[tri, gb], w=[trig])
                    yield
                pgr = psum(3 * upt)
                for uu in U:
                    mm(pgr[:, uu * UB:(uu + 1) * UB], onesF[0:UB, :], trig[0:UB, uu, :], r=[onesF, trig], w=[pgr])
                pkk = [psum(2 * min(2, upt)), psum(2 * max(0, upt - 2))]
                for uu in U:
                    us = slice(uu * UB, (uu + 1) * UB)
                    pq = pkk[uu // 2]
                    o_ = (uu % 2) * 256
                    mm(pq[0:UB, o_:o_ + UB], kTt[:, us], kTt[:, us], r=[kTt], w=[pq])
                    mm(pq[0:UB, o_ + 128:o_ + 128 + UB], kTt[:, us], qTt[:, us], r=[kTt, qTt], w=[pq])
                yield
                for uu in U:
                    u = j * upt + uu
                    g_ = pgr[0:UB, uu * UB:(uu + 1) * UB]
                    tt(argT[0:UB, uu, :], g_, mT.ap, ALU.add, r=[pgr, mT], w=[argT])
                    tt(argA[0:UB, uu, :], mA.ap, g_, ALU.subtract, r=[pgr, mA], w=[argA])
                    act(eg[:, uu, :], pgr[:, uu * UB:(uu + 1) * UB], AF.Exp, r=[pgr], w=[eg])
                    yield
                for uu in U:
                    u = j * upt + uu
                    act(argT[0:UB, uu, :], argT[0:UB, uu, :], AF.Exp, bias=gb[0:UB, u, 24 + h:25 + h],
                        r=[argT, gb], w=[argT])
                    act(argA[0:UB, uu, :], argA[0:UB, uu, :], AF.Exp, bias=gb[0:UB, u, 16 + h:17 + h],
                        r=[argA, gb], w=[argA])
                    for cc in range(cpu):
                        e_ = cc * C + C - 1
                        cp(egl[:, uu * cpu + cc:uu * cpu + cc + 1], eg[:, uu, e_:e_ + 1], r=[eg], w=[egl], eng='act')
                    yield
                for uu in U:
                    u = j * upt + uu
                    us = slice(uu * UB, (uu + 1) * UB)
                    pq = pkk[uu // 2]
                    o_ = (uu % 2) * 256
                    stt(NA[0][0:UB, uu, :], pq[0:UB, o_:o_ + UB], gb[0:UB, u, 8 + h:9 + h], argA[0:UB, uu, :], ALU.mult,
                        ALU.mult, r=[pq, gb, argA], w=[NA[0]])
                    tt(QKT[0:UB, uu, :], pq[0:UB, o_ + 128:o_ + 128 + UB], argT[0:UB, uu, :], ALU.mult, r=[pq, argT],
                       w=[QKT])
                    tt(qg[:, uu, :], qTt[:, us], eg[:, uu, :], ALU.mult, r=[qTt, eg], w=[qg])
                    yield
                pt_ = psum(2 * upt)
                for uu in U:
                    tr(pt_[0:UB, uu * UB:(uu + 1) * UB], NA[0][0:UB, uu, :], identF[0:UB, 0:UB], r=[NA[0], identF],
                       w=[pt_])
                yield
                for uu in U:
                    cp(NT[0][0:UB, uu, :], pt_[0:UB, uu * UB:(uu + 1) * UB], r=[pt_], w=[NT[0]], eng='act')
                    tt(TTm[0][0:UB, uu, :], pt_[0:UB, uu * UB:(uu + 1) * UB], identF[0:UB, 0:UB], ALU.add,
                       r=[pt_, identF], w=[TTm[0]])
                    yield
                cur = 0
                for lev in range(5):
                    nxt = 1 - cur
                    last = (lev == 4)
                    for uu in range(upt):
                        p1 = psum(1 if last else 2)
                        mm(p1[0:UB, 0:UB], NT[cur][0:UB, uu, :], NA[cur][0:UB, uu, :], r=[NT[cur], NA[cur]], w=[p1])
                        if not last:
                            mm(p1[0:UB, 128:128 + UB], NA[cur][0:UB, uu, :], NT[cur][0:UB, uu, :],
                               r=[NT[cur], NA[cur]], w=[p1])
                        cp(NA[nxt][0:UB, uu, :], p1[0:UB, 0:UB], r=[p1], w=[NA[nxt]], eng='act')
                        if not last:
                            cp(NT[nxt][0:UB, uu, :], p1[0:UB, 128:128 + UB], r=[p1], w=[NT[nxt]], eng='dve')
                        yield
                    for uu in range(upt):
                        p2 = psum()
                        mm(p2[0:UB, 0:UB], identF[0:UB, 0:UB], TTm[cur][0:UB, uu, :], start=True, stop=False,
                           r=[identF, TTm[cur]], w=[p2])
                        mm(p2[0:UB, 0:UB], NA[nxt][0:UB, uu, :], TTm[cur][0:UB, uu, :], start=False, stop=True,
                           r=[NA[nxt], TTm[cur]], w=[p2])
                        if last:
                            cp(PTb[0:UB, uu, :], p2[0:UB, 0:UB], r=[p2], w=[PTb], eng='dve')
                        else:
                            cp(TTm[nxt][0:UB, uu, :], p2[0:UB, 0:UB], r=[p2], w=[TTm[nxt]], eng='dve')
                        yield
                    cur = nxt
                for uu in range(upt):
                    pw = psum(2)
                    mm(pw[0:UB, 0:128], PTb[0:UB, uu, :], vb[0:UB, uu, :], r=[PTb, vb], w=[pw])
                    mm(pw[:, 128:128 + UB], kbg[0:UB, uu, :], PTb[0:UB, uu, :], r=[PTb, kbg], w=[pw])
                    cp(wv[0:UB, uu, :], pw[0:UB, 0:128], r=[pw], w=[wv], eng='act')
                    cp(wkT[:, uu, :], pw[:, 128:128 + UB], r=[pw], w=[wkT], eng='dve')
                    yield

            def stageB(h, j, hb):
                B_ = HB[hb]
                QKT, qg, wv, wkT, kd, zs, egl = B_['QKT'], B_['qg'], B_['wv'], B_['wkT'], B_['kd'], B_['zs'], B_['egl']
                sl = slice(j * TT, (j + 1) * TT)
                if j == 0:
                    if prompt:
                        mset(S.ap, 0.0, w=[S])
                        mset(Sb.ap, 0.0, w=[Sb])
                    else:
                        ld(S.ap, I['st_gS'][slot, li, h], w=[S], r=['BAR'])
                        cp(Sb.ap, S.ap, r=[S], w=[Sb], eng='act')
                    yield
                for uu in range(upt):
                    for cc in range(cpu):
                        rs_ = slice(cc * C, (cc + 1) * C)
                        pu = psum()
                        mm(pu[0:UB, 0:128], wkT[:, uu, :], Sb.ap, r=[wkT, Sb], w=[pu])
                        po = psum()
                        mm(po[:, 0:C], Sb.ap, qg[:, uu, rs_], r=[Sb, qg], w=[po])
                        yield
                        tt(ub_[rs_, :], wv[rs_, uu, :], pu[rs_, 0:128], ALU.subtract, r=[wv, pu], w=[ub_])
                        t0 = uu * UB + cc * C
                        cp(oT[:, t0:t0 + C], po[:, 0:C], r=[po], w=[oT], eng='act')
                        yield
                        pS = psum()
                        mm(pS[:, 0:128], kd[rs_, uu, :], ub_[rs_, :], r=[kd, ub_], w=[pS])
                        po2 = psum()
                        mm(po2[:, 0:C], ub_[rs_, :], QKT[rs_, uu, rs_], r=[ub_, QKT], w=[po2])
                        yield
                        ei = uu * cpu + cc
                        stt(S.ap, S.ap, egl[:, ei:ei + 1], pS[:, 0:128], ALU.mult, ALU.add, r=[S, egl, pS], w=[S])
                        cp(Sb.ap, S.ap, r=[S], w=[Sb], eng='act')
                        tt(oT[:, t0:t0 + C], oT[:, t0:t0 + C], po2[:, 0:C], ALU.add, r=[oT, po2], w=[oT])
                        yield
                act(sqb2.ap, oT.ap, AF.Square, r=[oT], w=[sqb2])
                ps2 = psum()
                mm(ps2[:, 0:TT], onesB.ap, sqb2.ap, r=[onesB, sqb2], w=[ps2])
                yield
                rsqrt_from(rnb2.ap, ps2[:, 0:TT], 1.0 / 128, [ps2], [rnb2, rnb2], rnb2.ap)
                tt(oT.ap, oT.ap, rnb2.ap, ALU.mult, r=[oT, rnb2], w=[oT])
                stt(obuf[:, sl], oT.ap, onw[:, 0:1], zs.ap, ALU.mult, ALU.mult, r=[oT, onw, zs], w=[obuf])
                yield
                if j == nTT - 1:
                    stq(Sout[h], S.ap, r=[S], w=['o_gS'])
                    s_t = wst[wctr[0] % 2]
                    wctr[0] += 1
                    sv = s_t.ap[:, 0:1024]
                    ld(sv, Wo[h * 128:(h + 1) * 128, :], w=[s_t])
                    P.op('act', lambda: nc.scalar.copy(out=wob.ap, in_=sv), _keys([s_t]), _keys([wob]))
                    yield
                    for dc in range(KC):
                        for jt in range(nTT):
                            sl2 = slice(jt * TT, (jt + 1) * TT)
                            ps = psum()
                            mm(ps[:, 0:TT], wob[:, dc * 128:(dc + 1) * 128], obuf[:, sl2], r=[wob, obuf], w=[ps])
                            resid_add(ps, l, 2, dc, sl2)
                            yield

            def count_yields(mk):
                saved = (psctr[0], wctr[0], wctr[1], cast_ctr[0], dict(hw_))
                P.muted = True
                n = 0
                for _ in mk():
                    n += 1
                P.muted = False
                psctr[0], wctr[0], wctr[1], cast_ctr[0] = saved[0], saved[1], saved[2], saved[3]
                hw_.clear()
                hw_.update(saved[4])
                return n

            def drive(ga, na, gb_, nb):
                if ga is None or gb_ is None:
                    for g in (ga, gb_):
                        if g is not None:
                            for _ in g:
                                pass
                    return
                ia = ib = 0
                while ia < na or ib < nb:
                    if ib >= nb or (ia < na and ia * nb <= ib * na):
                        next(ga, None)
                        ia += 1
                    else:
                        next(gb_, None)
                        ib += 1
                for g in (ga, gb_):
                    for _ in g:
                        pass

            work = [(h, j) for h in range(8) for j in range(nTT)]
            prevB = None
            nprev = 0
            for idx, (h, j) in enumerate(work):
                na = count_yields(lambda: stageA(h, j, idx % 2))
                drive(stageA(h, j, idx % 2), na, prevB, nprev)
                nprev = count_yields(lambda: stageB(h, j, idx % 2))
                prevB = stageB(h, j, idx % 2)
            drive(None, 0, prevB, nprev)
            ARENA_USE['gdn'] = max(ARENA_USE.get('gdn', 0), apos[0])
            phase_end()

        def wload_pieces(parts, n, pieces, shape_str, **dims):
            assert n <= WCOLS
            s_t = wst[wctr[0] % 2]
            wctr[0] += 1
            b_t = wbf[wctr[1] % 3]
            wctr[1] += 1
            sv = s_t.ap[0:parts, 0:n]
            bv = b_t.ap[0:parts, 0:n]
            svv = sv.rearrange(shape_str, **dims)
            bvv = bv.rearrange(shape_str, **dims)
            for fn_, dap in pieces:
                ld(fn_(svv), dap, w=[s_t])
            wcast(bv, sv, s_t, b_t)
            return Tl(bvv, b_t.key)

        def out_tokmajor(src_fn, nfeat, dst_fn, stg, r):
            for b in range(nblk):
                ps = psum()
                tr(ps[0:tsz, 0:nfeat], src_fn(b), identF[0:nfeat, 0:nfeat], r=r + [identF], w=[ps])
                s_ = stg[b % 2]
                cp(s_[0:tsz, 0:nfeat], ps[0:tsz, 0:nfeat], r=[ps], w=[s_], eng='act')
                stq(dst_fn(b), s_[0:tsz, 0:nfeat], r=[s_], w=['o_misc'])

        def mla(l):
            psrot[0] = 4
            Win = I['mla_w_in'][0]
            Wq = I['mla_w_q_up'][0]
            Wkv = I['mla_w_kv_up'][0]
            Wo = I['mla_w_o'][0]
            NK = T if prompt else PAST + T
            koff = NK - T
            cosT = aalloc('cosT', T, F32, parts=32)
            sinT = aalloc('sinT', T, F32, parts=32)
            ld(cosT.ap, I['k_cos_' + nm], w=[cosT], r=['BAR'])
            ld(sinT.ap, I['k_sin_' + nm], w=[sinT], r=['BAR'])
            qnw = aalloc('qnw', 3, F32)
            kvw = aalloc('kvw', 2, F32)
            ld(qnw.ap, I['mla_q_norm'][0].rearrange("(k p) -> p k", p=128), w=[qnw], r=['BAR'], slow=True)
            ld(kvw.ap, I['mla_kv_norm'][0].rearrange("(k p) -> p k", p=128), w=[kvw], r=['BAR'], slow=True)
            cqT = aalloc('cqT', 3 * T, BF16, shape=("p (k t) -> p k t", dict(k=3)))
            ckvT = aalloc('ckvT', 2 * NK, BF16, shape=("p (k t) -> p k t", dict(k=2)))
            krT = aalloc('krT', NK, BF16, parts=32)
            amark = apos[0]
            pre = aalloc('pre', 3 * TT, F32, shape=("p (k t) -> p k t", dict(k=3)))
            sq = aalloc('msq', 3 * TT, BF16, shape=("p (k t) -> p k t", dict(k=3)))
            rs = aalloc('mrs', TT, F32)
            rtmp = aalloc('mrt', TT, F32)
            ckf = aalloc('ckf', 2 * TT, F32, shape=("p (k t) -> p k t", dict(k=2)))
            krf = aalloc('krf', TT, F32, parts=32)
            krt = aalloc('krt', TT, F32, parts=32)
            ostg = [aalloc('ostg%d' % i, 128, F32) for i in range(2)]
            lat_o = (O['lat_p'] if prompt else O['lat_s'])[li]
            kr_o = (O['kr_p'] if prompt else O['kr_s'])[li]
            if not prompt:
                cin = [aalloc('cin%d' % i, 256 + 32, F32) for i in range(2)]
                for b in range(PAST // 128):
                    ci = cin[b % 2]
                    ld(ci[:, 0:256], I['c_lat'][li][b * 128:(b + 1) * 128, :], w=[ci], r=['BAR'])
                    ld(ci[:, 256:288], I['c_kr'][li][b * 128:(b + 1) * 128, :], w=[ci], r=['BAR'])
                    ps = psum()
                    for k in range(2):
                        tr(ps[:, k * 128:(k + 1) * 128], ci[:, k * 128:(k + 1) * 128], identF.ap, r=[ci, identF], w=[ps])
                    tr(ps[0:32, 256:384], ci[:, 256:288], identF.ap, r=[ci, identF], w=[ps])
                    cp(ckvT[:, :, b * 128:(b + 1) * 128], ps[:, 0:256].rearrange("p (k t) -> p k t", k=2), r=[ps],
                       w=[ckvT], eng='dve')
                    cp(krT[:, b * 128:(b + 1) * 128], ps[0:32, 256:384], r=[ps], w=[krT], eng='act')
            for j in range(nTT):
                sl = slice(j * TT, (j + 1) * TT)
                ksl = slice(koff + j * TT, koff + (j + 1) * TT)
                wa = wload(Win[:, 0:256].rearrange("(k p) c -> p k c", p=128), "p (k c) -> p k c", k=KC)
                for c in range(2):
                    ps = psum()
                    for k in range(KC):
                        mm(ps[:, 0:TT], wa[:, k, c * 128:(c + 1) * 128], hT[:, k, sl], start=(k == 0),
                           stop=(k == KC - 1), r=[wa, hT], w=[ps])
                    cp(pre[:, c, :], ps[:, 0:TT], r=[ps], w=[pre], eng='act')
                wb_ = wload(Win[:, 256:512].rearrange("(k p) c -> p k c", p=128), "p (k c) -> p k c", k=KC)
                ps = psum()
                for k in range(KC):
                    mm(ps[:, 0:TT], wb_[:, k, 0:128], hT[:, k, sl], start=(k == 0), stop=(k == KC - 1), r=[wb_, hT],
                       w=[ps])
                cp(pre[:, 2, :], ps[:, 0:TT], r=[ps], w=[pre], eng='act')
                act(sq.ap, pre.ap, AF.Square, r=[pre], w=[sq])
                ps = psum()
                for c in range(3):
                    mm(ps[:, 0:TT], onesB.ap, sq[:, c, :], start=(c == 0), stop=(c == 2), r=[onesB, sq], w=[ps])
                rsqrt_from(rs.ap, ps[:, 0:TT], 1.0 / 384, [ps], [rs, rtmp], rtmp.ap)
                for c in range(3):
                    stt(cqT[:, c, sl], pre[:, c, :], qnw[:, c:c + 1], rs.ap, ALU.mult, ALU.mult, r=[pre, qnw, rs],
                        w=[cqT])
                wc = wload_pieces(128, KC * 192, [
                    (lambda v: v[:, :, 0:160], Win[:, 512:672].rearrange("(k p) c -> p k c", p=128)),
                    (lambda v: v[:, :, 160:176], Win[:, 656:672].rearrange("(k p) c -> p k c", p=128)),
                    (lambda v: v[:, :, 176:192], Win[:, 640:656].rearrange("(k p) c -> p k c", p=128)),
                ], "p (k c) -> p k c", k=KC)
                ps = psum()
                for k in range(KC):
                    mm(ps[:, 0:TT], wb_[:, k, 128:256], hT[:, k, sl], start=(k == 0), stop=(k == KC - 1), r=[wb_, hT],
                       w=[ps])
                cp(pre[:, 0, :], ps[:, 0:TT], r=[ps], w=[pre], eng='act')
                ps = psum()
                for k in range(KC):
                    mm(ps[:, 0:TT], wc[:, k, 0:128], hT[:, k, sl], start=(k == 0), stop=(k == KC - 1), r=[wc, hT],
                       w=[ps])
                cp(pre[:, 1, :], ps[:, 0:TT], r=[ps], w=[pre], eng='act')
                act(sq[:, 0:2, :], pre[:, 0:2, :], AF.Square, r=[pre], w=[sq])
                ps = psum()
                for c in range(2):
                    mm(ps[:, 0:TT], onesB.ap, sq[:, c, :], start=(c == 0), stop=(c == 1), r=[onesB, sq], w=[ps])
                rsqrt_from(rs.ap, ps[:, 0:TT], 1.0 / 256, [ps], [rs, rtmp], rtmp.ap)
                for c in range(2):
                    stt(ckf[:, c, :], pre[:, c, :], kvw[:, c:c + 1], rs.ap, ALU.mult, ALU.mult, r=[pre, kvw, rs],
                        w=[ckf])
                cp(ckvT[:, :, ksl], ckf.ap, r=[ckf], w=[ckvT], eng='act')
                for b in range(TT // tsz):
                    for c in range(2):
                        ps = psum()
                        tr(ps[0:tsz, 0:128], ckf[:, c, b * tsz:(b + 1) * tsz], identF.ap, r=[ckf, identF], w=[ps])
                        s_ = ostg[c]
                        cp(s_[0:tsz, :], ps[0:tsz, 0:128], r=[ps], w=[s_], eng='act')
                        t0 = j * TT + b * tsz
                        stq(lat_o[t0:t0 + tsz, c * 128:(c + 1) * 128], s_[0:tsz, :], r=[s_], w=['o_lat'])
                psA = psum()
                for k in range(KC):
                    mm(psA[0:32, 0:TT], wc[:, k, 128:160], hT[:, k, sl], start=(k == 0), stop=(k == KC - 1),
                       r=[wc, hT], w=[psA])
                psB = psum()
                for k in range(KC):
                    mm(psB[0:32, 0:TT], wc[:, k, 160:192], hT[:, k, sl], start=(k == 0), stop=(k == KC - 1),
                       r=[wc, hT], w=[psB])
                tt(krt.ap, psB[0:32, 0:TT], sinT[:, sl], ALU.mult, r=[psB, sinT], w=[krt])
                tt(krf.ap, psA[0:32, 0:TT], cosT[:, sl], ALU.mult, r=[psA, cosT], w=[krf])
                tt(krf.ap, krf.ap, krt.ap, ALU.add, r=[krf, krt], w=[krf])
                cp(krT[:, ksl], krf.ap, r=[krf], w=[krT], eng='act')
                for b in range(TT // tsz):
                    ps = psum()
                    tr(ps[0:tsz, 0:32], krf[:, b * tsz:(b + 1) * tsz], identF[0:32, 0:32], r=[krf, identF], w=[ps])
                    s_ = ostg[b % 2]
                    cp(s_[0:tsz, 0:32], ps[0:tsz, 0:32], r=[ps], w=[s_], eng='act')
                    t0 = j * TT + b * tsz
                    stq(kr_o[t0:t0 + tsz, :], s_[0:tsz, 0:32], r=[s_], w=['o_kr'])
            P.barrier()
            P.op('dve', lambda: nc.vector.memset(bar_t[:, 0:1], 0.0), writes=['BAR'])
            apos[0] = amark
            aphase[0] += 1
            HG = 4
            nkt = (NK + 127) // 128
            hoff = [0]

            def halias(name, cols, parts, shape=None):
                ap = hT_t[0:parts, hoff[0]:hoff[0] + cols]
                hoff[0] += cols
                assert hoff[0] <= KC * TP
                if shape is not None:
                    ap = ap.rearrange(shape[0], **shape[1])
                return Tl(ap, '%s@%d' % (name, aphase[0]))
            vtok = halias('vtok', nkt * HG * 64, 128, shape=("p (n c) -> p n c", dict(n=nkt)))
            abuf = halias('abuf', HG * T, 64, shape=("p (h t) -> p h t", dict(h=HG)))
            knT = halias('knT', NK, 64)
            qnT = halias('qnT', T, 64)
            qrT = aalloc('qrT', T, BF16, parts=32)
            qtmp = aalloc('qtmp', TT, F32, parts=32)
            qtm2 = aalloc('qtm2', TT, F32, parts=32)
            ptl = [aalloc('ptl%d' % i, TT, BF16) for i in range(3)]
            rD = aalloc('rD', TT, F32, parts=64)
            scale = 96.0 ** -0.5
            pctr = [0]
            accsel = [0]
            for g in range(16 // HG):
                wkv = wload(Wkv[:, g * HG * 128:(g + 1) * HG * 128].rearrange("(k p) c -> p k c", p=128),
                            "p (k c) -> p k c", k=2)
                wkv4 = wkv.ap.rearrange("p k (h c) -> p k h c", h=HG)
                for kt in range(nkt):
                    ks = min(128, NK - kt * 128)
                    ps = psum()
                    for c in range(2):
                        mm(ps[0:ks, 0:HG * 64].rearrange("p (h c) -> p h c", h=HG), ckvT[:, c, kt * 128:kt * 128 + ks],
                           wkv4[:, c, :, 64:128], start=(c == 0), stop=(c == 1), r=[ckvT, wkv], w=[ps])
                    cp(vtok[0:ks, kt, :], ps[0:ks, 0:HG * 64], r=[ps], w=[vtok], eng='act')
                for hh in range(HG):
                    h = g * HG + hh
                    for kb in range((NK + 511) // 512):
                        k0 = kb * 512
                        kn = min(512, NK - k0)
                        ps = psum()
                        for c in range(2):
                            mm(ps[0:64, 0:kn], wkv4[:, c, hh, 0:64], ckvT[:, c, k0:k0 + kn], start=(c == 0),
                               stop=(c == 1), r=[wkv, ckvT], w=[ps])
                        cp(knT[:, k0:k0 + kn], ps[0:64, 0:kn], r=[ps], w=[knT], eng='act')
                    c0 = h * 96
                    wq = wload_pieces(128, 3 * 128, [
                        (lambda v: v[:, :, 0:96], Wq[:, c0:c0 + 96].rearrange("(k p) c -> p k c", p=128)),
                        (lambda v: v[:, :, 96:112], Wq[:, c0 + 80:c0 + 96].rearrange("(k p) c -> p k c", p=128)),
                        (lambda v: v[:, :, 112:128], Wq[:, c0 + 64:c0 + 80].rearrange("(k p) c -> p k c", p=128)),
                    ], "p (k c) -> p k c", k=3)
                    for j in range(nTT):
                        sl = slice(j * TT, (j + 1) * TT)
                        ps = psum()
                        for c in range(3):
                            mm(ps[0:64, 0:TT], wq[:, c, 0:64], cqT[:, c, sl], start=(c == 0), stop=(c == 2),
                               r=[wq, cqT], w=[ps])
                        cp(qnT[:, sl], ps[0:64, 0:TT], r=[ps], w=[qnT], eng='act')
                        psA = psum()
                        for c in range(3):
                            mm(psA[0:32, 0:TT], wq[:, c, 64:96], cqT[:, c, sl], start=(c == 0), stop=(c == 2),
                               r=[wq, cqT], w=[psA])
                        psB = psum()
                        for c in range(3):
                            mm(psB[0:32, 0:TT], wq[:, c, 96:128], cqT[:, c, sl], start=(c == 0), stop=(c == 2),
                               r=[wq, cqT], w=[psB])
                        tt(qtmp.ap, psB[0:32, 0:TT], sinT[:, sl], ALU.mult, r=[psB, sinT], w=[qtmp])
                        tt(qtm2.ap, psA[0:32, 0:TT], cosT[:, sl], ALU.mult, r=[psA, cosT], w=[qtm2])
                        tt(qrT[:, sl], qtm2.ap, qtmp.ap, ALU.add, r=[qtm2, qtmp], w=[qrT])
                    for j in range(nTT):
                        sl = slice(j * TT, (j + 1) * TT)
                        if prompt:
                            kts = list(range(0, 4 * j + 4))
                        else:
                            kts = list(range(nkt))
                        A0, A1 = ACCP[accsel[0] % 2]
                        accsel[0] += 1

                        def blk_info(kt):
                            ks = min(128, NK - kt * 128)
                            partial = prompt and kt >= 4 * j
                            q0 = 128 * (kt - 4 * j) if partial else 0
                            return ks, partial, q0

                        def issue_scores(kt):
                            ks, partial, q0 = blk_info(kt)
                            qs = slice(j * TT + q0, (j + 1) * TT)
                            nq = TT - q0
                            ps = psum()
                            mm(ps[0:ks, 0:nq], knT[:, kt * 128:kt * 128 + ks], qnT[:, qs], start=True, stop=False,
                               r=[knT, qnT], w=[ps])
                            mm(ps[0:ks, 0:nq], krT[:, kt * 128:kt * 128 + ks], qrT[:, qs], start=False, stop=True,
                               r=[krT, qrT], w=[ps])
                            pt_ = ptl[pctr[0] % 3]
                            pctr[0] += 1
                            act(pt_[0:ks, 0:nq], ps[0:ks, 0:nq], AF.Exp, scale=scale, r=[ps], w=[pt_])
                            if partial:
                                mset(pt_[64:128, 0:64], 0.0, w=[pt_])
                            return pt_

                        def issue_pv(ix, kt, pt_):
                            ks, partial, q0 = blk_info(kt)
                            nq = TT - q0
                            first = (ix == 0)
                            lastk = (ix == len(kts) - 1)
                            mm(A0[0:64, q0:TT], vtok[0:ks, kt, hh * 64:(hh + 1) * 64], pt_[0:ks, 0:nq], start=first,
                               stop=lastk, r=[vtok, pt_], w=[A0])
                            mm(A1[0:64, q0:TT], onesB[0:ks, 0:64], pt_[0:ks, 0:nq], start=first, stop=lastk,
                               r=[onesB, pt_], w=[A1])

                        pend = None
                        for ix, kt in enumerate(kts):
                            pt_ = issue_scores(kt)
                            if pend is not None:
                                issue_pv(*pend)
                            pend = (ix, kt, pt_)
                        issue_pv(*pend)
                        recip(rD[:, 0:TT], A1[0:64, 0:TT], r=[A1], w=[rD])
                        tt(abuf[:, hh, sl], A0[0:64, 0:TT], rD[:, 0:TT], ALU.mult, r=[A0, rD], w=[abuf])
                for half in range(2):
                    wo = wload(Wo[g * HG * 64:(g + 1) * HG * 64, half * 512:(half + 1) * 512].rearrange(
                        "(h p) c -> p h c", p=64), "p (h c) -> p h c", h=HG)
                    for dq in range(4):
                        dc = half * 4 + dq
                        for jt in range(nTT):
                            sl2 = slice(jt * TT, (jt + 1) * TT)
                            ps = psum()
                            for hh in range(HG):
                                mm(ps[:, 0:TT], wo[:, hh, dq * 128:(dq + 1) * 128], abuf[:, hh, sl2], start=(hh == 0),
                                   stop=(hh == HG - 1), r=[wo, abuf], w=[ps])
                            resid_add(ps, l, 2, dc, sl2)
            phase_end()

        def swa(l):
            psrot[0] = 4
            Win = I['swa_w_in'][0]
            Wo = I['swa_w_o'][0]
            CQ = 64 if prompt else 32
            nch = T // CQ
            NQ = 4 * CQ
            pad = 128 if prompt else 0
            NK = T if prompt else 128 + T
            koff = 0 if prompt else 128
            kT = aalloc('skT', 4 * (pad + NK), BF16, parts=64, shape=("p (h t) -> p h t", dict(h=4)))
            nva = (T // 128) if prompt else 2
            vA = aalloc('vA', nva * 256, BF16, shape=("p (n c) -> p n c", dict(n=nva)))
            if prompt:
                vB = aalloc('vB', (nva + 1) * 256, BF16, shape=("p (n c) -> p n c", dict(n=nva + 1)))
            ba = aalloc('sba', 4 * NQ, F32, shape=("p (h c) -> p h c", dict(h=4)))
            ld(ba.ap, I['k_ba_' + nm], w=[ba], r=['BAR'])
            bsz = 64 if prompt else 32
            bb = aalloc('sbb', 4 * NQ, F32, parts=bsz, shape=("p (h c) -> p h c", dict(h=4)))
            ld(bb.ap, I['k_bb_' + nm], w=[bb], r=['BAR'])
            if prompt:
                ba1 = aalloc('sba1', NQ, F32)
            snk = aalloc('snk', 16, F32, parts=64)
            ld(snk.ap, I['swa_sinks'][0].partition_broadcast(64), w=[snk], r=['BAR'])
            act(snk.ap, snk.ap, AF.Exp, r=[snk], w=[snk])
            esk = aalloc('esk', 4 * NQ, F32, parts=64, shape=("p (h g q) -> p h g q", dict(h=4, g=4)))
            cp(esk.ap, snk.ap.rearrange("p (h g) -> p h g", h=4).unsqueeze(3).broadcast_to([64, 4, 4, CQ]), r=[snk],
               w=[esk])
            NH = 2 if prompt else 1
            TH = T // NH
            nchh = nch // NH
            abuf = aalloc('sabuf', 4 * TH, BF16, parts=64, shape=("p (g t) -> p g t", dict(g=4)))
            q4 = aalloc('sq4', 4 * TH, BF16, parts=64, shape=("p (c g q) -> p c g q", dict(c=nchh, g=4)))
            stg = [aalloc('sstg%d' % i, 256, F32) for i in range(2)]
            sarg = [aalloc('sarg%d' % i, NQ, F32) for i in range(2)]
            spa = [aalloc('spa%d' % i, NQ, BF16) for i in range(2)]
            spb = [aalloc('spb%d' % i, NQ, BF16, parts=64) for i in range(2)]
            sden = [aalloc('sden%d' % i, NQ, F32, parts=64) for i in range(2)]
            sk_o = (O['sk_p'] if prompt else O['sk_s'])[li]
            sv_o = (O['sv_p'] if prompt else O['sv_s'])[li]
            wk_ = wload(Win[:, 1024:1280].rearrange("(k p) c -> p k c", p=128), "p (k c) -> p k c", k=KC)
            wv_ = wload(Win[:, 1280:1536].rearrange("(k p) c -> p k c", p=128), "p (k c) -> p k c", k=KC)
            if prompt:
                for hh in range(4):
                    mset(kT[:, hh, 0:pad], 0.0, w=[kT])
                mset(vB[:, 0, :], 0.0, w=[vB])
            else:
                ck = aalloc('sck', 256, F32)
                cv = aalloc('scv', 256, F32)
                ld(ck.ap, I['c_sk'][li], w=[ck], r=['BAR'])
                ld(cv.ap, I['c_sv'][li], w=[cv], r=['BAR'])
                for hh in range(4):
                    ps = psum()
                    tr(ps[0:64, 0:128], ck[:, hh * 64:(hh + 1) * 64], identF.ap, r=[ck, identF], w=[ps])
                    cp(kT[:, hh, 0:128], ps[0:64, 0:128], r=[ps], w=[kT], eng='act')
                cp(vA[:, 0, :], cv.ap, r=[cv], w=[vA], eng='dve')
                stq(sk_o[0:96, :], ck[32:128, :], r=[ck], w=['o_sk'])
                stq(sv_o[0:96, :], cv[32:128, :], r=[cv], w=['o_sv'])
            for j in range(nTT):
                sl = slice(j * TT, (j + 1) * TT)
                for hh in range(4):
                    ps = psum()
                    for k in range(KC):
                        mm(ps[0:64, 0:TT], wk_[:, k, hh * 64:(hh + 1) * 64], hT[:, k, sl], start=(k == 0),
                           stop=(k == KC - 1), r=[wk_, hT], w=[ps])
                    cp(kT[:, hh, pad + koff + j * TT:pad + koff + (j + 1) * TT], ps[0:64, 0:TT], r=[ps], w=[kT],
                       eng='act')
            for b in range(nblk):
                ps = psum()
                for k in range(KC):
                    mm(ps[0:tsz, 0:256], hT[:, k, b * tsz:(b + 1) * tsz], wv_[:, k, :], start=(k == 0),
                       stop=(k == KC - 1), r=[hT, wv_], w=[ps])
                vi = b if prompt else 1
                cp(vA[0:tsz, vi, :], ps[0:tsz, 0:256], r=[ps], w=[vA], eng='act')
                if b == nblk - 1:
                    s_ = stg[0]
                    cp(s_[0:tsz, :], ps[0:tsz, 0:256], r=[ps], w=[s_], eng='dve')
                    stq(sv_o[128 - tsz:128, :], s_[0:tsz, :], r=[s_], w=['o_sv'])
                    ps2 = psum()
                    for k in range(KC):
                        mm(ps2[0:tsz, 0:256], hT[:, k, b * tsz:(b + 1) * tsz], wk_[:, k, :], start=(k == 0),
                           stop=(k == KC - 1), r=[hT, wk_], w=[ps2])
                    s2 = stg[1]
                    cp(s2[0:tsz, :], ps2[0:tsz, 0:256], r=[ps2], w=[s2], eng='dve')
                    stq(sk_o[128 - tsz:128, :], s2[0:tsz, :], r=[s2], w=['o_sk'])
            if prompt:
                for m in range(1, nva + 1):
                    t0 = 64 + 128 * (m - 1)
                    nt_ = min(128, T - t0)
                    ps = psum()
                    for k in range(KC):
                        mm(ps[0:nt_, 0:256], hT[:, k, t0:t0 + nt_], wv_[:, k, :], start=(k == 0), stop=(k == KC - 1),
                           r=[hT, wv_], w=[ps])
                    cp(vB[0:nt_, m, :], ps[0:nt_, 0:256], r=[ps], w=[vB], eng='act')
                ps = psum()
                for k in range(KC):
                    mm(ps[:, 0:256], hT[:, k, 0:128], wv_[:, k, :], start=(k == 0), stop=(k == KC - 1), r=[hT, wv_],
                       w=[ps])
                ps3 = psum()
                cp(stg[0][:, :], ps[:, 0:256], r=[ps], w=[stg[0]], eng='dve')
                shm = aalloc('shm', 128, BF16)
                mset(shm.ap, 0.0, w=[shm])
                cp(shm[0:64, 64:128], identB[0:64, 0:64], r=[identB], w=[shm], eng='dve')
                vtmp = aalloc('vtmp', 256, BF16)
                cp(vtmp.ap, stg[0].ap, r=[stg[0]], w=[vtmp], eng='dve')
                mm(ps3[:, 0:256], shm.ap, vtmp.ap, r=[shm, vtmp], w=[ps3])
                cp(vB[:, 0, :], ps3[:, 0:256], r=[ps3], w=[vB], eng='act')
            scale = 64.0 ** -0.5
            for hh in range(4):
                wq_ = wload(Win[:, hh * 256:(hh + 1) * 256].rearrange("(k p) c -> p k c", p=128), "p (k c) -> p k c",
                            k=KC)
                if prompt:
                    ld(ba1.ap, I['k_ba1_p'][:, hh, :], w=[ba1], r=['BAR'])
                tph = nTT // NH if prompt else 1
                for hf in range(NH):
                    for g in range(4):
                        for jj in range(tph):
                            j = hf * tph + jj
                            sl = slice(j * TT, (j + 1) * TT)
                            ps = psum()
                            for k in range(KC):
                                mm(ps[0:64, 0:TT], wq_[:, k, g * 64:(g + 1) * 64], hT[:, k, sl], start=(k == 0),
                                   stop=(k == KC - 1), r=[wq_, hT], w=[ps])
                            cpt = TT // CQ
                            cp(q4[:, jj * cpt:(jj + 1) * cpt, g, :], ps[0:64, 0:TT].rearrange("p (c q) -> p c q", c=cpt),
                               r=[ps], w=[q4], eng='act')
                    def sw_scores(cl):
                        c = hf * nchh + cl
                        blocks = []
                        if prompt:
                            if c >= 1:
                                k0 = pad + 64 * (c - 2)
                                vt = (vA, (c - 2) // 2) if c % 2 == 0 else (vB, (c - 1) // 2)
                                blocks.append((128, k0, vt, (ba[:, hh, :] if c >= 2 else ba1.ap), (ba if c >= 2 else ba1)))
                            vt = (vA, c // 2) if c % 2 == 0 else (vB, (c + 1) // 2)
                            blocks.append((64, pad + 64 * c, vt, bb[:, hh, :], bb))
                        else:
                            blocks.append((128, 0, (vA, 0), ba[:, hh, :], ba))
                            blocks.append((32, 128, (vA, 1), bb[:, hh, :], bb))
                        qv = q4[:, cl, :, :].rearrange("p g q -> p (g q)")
                        res = []
                        for bi, (ks, k0, (vt_, vi), btap, bt) in enumerate(blocks):
                            ps = psum()
                            mm(ps[0:ks, 0:NQ], kT[:, hh, k0:k0 + ks], qv, r=[kT, q4], w=[ps])
                            ar = sarg[bi % 2]
                            stt(ar[0:ks, :], ps[0:ks, 0:NQ], scale, btap[0:ks, :], ALU.mult, ALU.add, r=[ps, bt], w=[ar])
                            pp = (spa if ks == 128 else spb)[c % 2]
                            act(pp[0:ks, :], ar[0:ks, :], AF.Exp, r=[ar], w=[pp])
                            res.append((ks, vt_, vi, pp))
                        return res

                    def sw_pv(cl, res):
                        A0, A1 = ACCP[cl % 2]
                        for bi, (ks, vt_, vi, pp) in enumerate(res):
                            first = (bi == 0)
                            lastb = (bi == len(res) - 1)
                            mm(A0[0:64, 0:NQ], vt_[0:ks, vi, hh * 64:(hh + 1) * 64], pp[0:ks, :], start=first,
                               stop=lastb, r=[vt_, pp], w=[A0])
                            mm(A1[0:64, 0:NQ], onesB[0:ks, 0:64], pp[0:ks, :], start=first, stop=lastb,
                               r=[onesB, pp], w=[A1])
                        sd = sden[cl % 2]
                        tt(sd.ap, A1[0:64, 0:NQ], esk[:, hh, :, :].rearrange("p g q -> p (g q)"), ALU.add,
                           r=[A1, esk], w=[sd])
                        recip(sd.ap, sd.ap, r=[sd], w=[sd])
                        tt(abuf[:, :, cl * CQ:(cl + 1) * CQ], A0[0:64, 0:NQ].rearrange("p (g q) -> p g q", g=4),
                           sd.ap.rearrange("p (g q) -> p g q", g=4), ALU.mult, r=[A0, sd], w=[abuf])

                    pend = None
                    for cl in range(nchh):
                        res = sw_scores(cl)
                        if pend is not None:
                            sw_pv(*pend)
                        pend = (cl, res)
                    sw_pv(*pend)
                    for half in range(2):
                        wo = wload(Wo[hh * 256:(hh + 1) * 256, half * 512:(half + 1) * 512].rearrange(
                            "(h p) c -> p h c", p=64), "p (h c) -> p h c", h=4)
                        for dq in range(4):
                            dc = half * 4 + dq
                            for jj in range(tph):
                                jt = hf * tph + jj
                                sl2 = slice(jt * TT, (jt + 1) * TT)
                                sl3 = slice(jj * TT, (jj + 1) * TT)
                                ps = psum()
                                for g in range(4):
                                    mm(ps[:, 0:TT], wo[:, g, dq * 128:(dq + 1) * 128], abuf[:, g, sl3], start=(g == 0),
                                       stop=(g == 3), r=[wo, abuf], w=[ps])
                                resid_add(ps, l, 2, dc, sl2)
            phase_end()

        def ffn(l):
            psrot[0] = 8
            Win = I['ffn_w_in'][l]
            Wout = I['ffn_w_out'][l]
            G = 11
            fcw = aalloc('fcw', NFC * 3, F32, shape=("p (c j) -> p c j", dict(c=NFC)))
            for jj in range(3):
                ld(fcw[:, :, jj], I['ffn_conv_w'][l][jj].rearrange("(c p) -> p c", p=128), w=[fcw], r=['BAR'], slow=True)
            fcb = aalloc('fcb', NFC, F32)
            ld(fcb.ap, I['ffn_conv_b'][l].rearrange("(c p) -> p c", p=128), w=[fcb], r=['BAR'], slow=True)
            actb = aalloc('actb', G * T, BF16, shape=("p (g t) -> p g t", dict(g=G)))
            cb = aalloc('fcbuf', TT + 2, F32)
            acc = [aalloc('facc%d' % i, TT, F32) for i in range(2)]
            fout = (O['fconv_p'] if prompt else O['fconv_s'])[l, li]
            Win4 = Win.rearrange("(k p) (two f) -> p k two f", p=128, two=2)
            for g0 in range(0, NFC, G):
                gn = min(G, NFC - g0)
                for gi in range(gn):
                    cc = g0 + gi
                    wt = wload_pieces(128, 2048, [
                        (lambda v: v[:, :, 0, :], Win[:, cc * 128:(cc + 1) * 128].rearrange("(k p) c -> p k c", p=128)),
                        (lambda v: v[:, :, 1, :], Win[:, DFF + cc * 128:DFF + (cc + 1) * 128].rearrange("(k p) c -> p k c", p=128)),
                    ], "p (k two f) -> p k two f", k=KC, two=2)
                    if prompt:
                        mset(cb[:, 0:2], 0.0, w=[cb])
                    else:
                        ld(cb[:, 0:2], I['st_fconv'][l, li][:, cc * 128:(cc + 1) * 128].rearrange("r c -> c r"), w=[cb],
                           r=['BAR'], slow=True)
                    for j in range(nTT):
                        sl = slice(j * TT, (j + 1) * TT)
                        pg = psum()
                        for k in range(KC):
                            mm(pg[:, 0:TT], wt[:, k, 0, :], hT[:, k, sl], start=(k == 0), stop=(k == KC - 1),
                               r=[wt, hT], w=[pg])
                        pu = psum()
                        for k in range(KC):
                            mm(pu[:, 0:TT], wt[:, k, 1, :], hT[:, k, sl], start=(k == 0), stop=(k == KC - 1),
                               r=[wt, hT], w=[pu])
                        cp(cb[:, 2:2 + TT], pg[:, 0:TT], r=[pg], w=[cb], eng='act')
                        a_ = acc[j % 2]
                        ts(a_.ap, cb[:, 0:TT], fcw[:, cc, 0:1], ALU.mult, fcb[:, cc:cc + 1], ALU.add, r=[cb, fcw, fcb],
                           w=[a_])
                        for jj in range(1, 3):
                            stt(a_.ap, cb[:, jj:jj + TT], fcw[:, cc, jj:jj + 1], a_.ap, ALU.mult, ALU.add,
                                r=[cb, fcw, a_], w=[a_])
                        if j == nTT - 1:
                            stq(fout[:, cc * 128:(cc + 1) * 128].rearrange("r c -> c r"), cb[:, TT:TT + 2], r=[cb],
                                w=['o_fconv'], slow=True)
                        else:
                            cp(cb[:, 0:2], cb[:, TT:TT + 2], r=[cb], w=[cb], eng='act')
                        act(a_.ap, a_.ap, AF.Silu, r=[a_], w=[a_])
                        tt(actb[:, gi, sl], a_.ap, pu[:, 0:TT], ALU.mult, r=[a_, pu], w=[actb])
                for dc in range(KC):
                    wo = wload(Wout[g0 * 128:(g0 + gn) * 128, dc * 128:(dc + 1) * 128].rearrange(
                        "(g p) c -> p g c", p=128), "p (g c) -> p g c", g=gn)
                    for jt in range(nTT):
                        sl2 = slice(jt * TT, (jt + 1) * TT)
                        ps = psum()
                        for gi in range(gn):
                            mm(ps[:, 0:TT], wo[:, gi, :], actb[:, gi, sl2], start=(gi == 0),
                               stop=(gi == gn - 1), r=[wo, actb], w=[ps])
                        resid_add(ps, l, 5, dc, sl2)
            phase_end()

        def final():
            psrot[0] = 8
            sq = aalloc('fsq', KC * TT, BF16, shape=("p (k t) -> p k t", dict(k=KC)))
            rs = aalloc('frs', TT, F32)
            tm = aalloc('ftm', TT, F32)
            yT = aalloc('yT', KC * TT, F32, shape=("p (k t) -> p k t", dict(k=KC)))
            ystg = [aalloc('ystg%d' % i, D, F32) for i in range(2)]
            y_o = (O['y_p'] if prompt else O['y_s'])[li]
            for j in range(nTT):
                sl = slice(j * TT, (j + 1) * TT)
                act(sq.ap, xT[:, :, sl], AF.Square, r=[xT], w=[sq])
                ps = psum()
                for k in range(KC):
                    mm(ps[:, 0:TT], onesB.ap, sq[:, k, :], start=(k == 0), stop=(k == KC - 1), r=[onesB, sq], w=[ps])
                rsqrt_from(rs.ap, ps[:, 0:TT], 1.0 / D, [ps], [rs, tm], tm.ap)
                for k in range(KC):
                    stt(yT[:, k, :], xT[:, k, sl], nrmw[:, 8, k:k + 1], rs.ap, ALU.mult, ALU.mult, r=[xT, nrmw, rs],
                        w=[yT])
                for b in range(TT // tsz):
                    ys = ystg[b % 2]
                    for half in range(2):
                        ps = psum()
                        for q in range(4):
                            k = half * 4 + q
                            tr(ps[0:tsz, q * 128:(q + 1) * 128], yT[:, k, b * tsz:(b + 1) * tsz], identF.ap,
                               r=[yT, identF], w=[ps])
                        cp(ys[0:tsz, half * 512:(half + 1) * 512], ps[0:tsz, 0:512], r=[ps], w=[ys],
                           eng=('dve' if half == 0 else 'act'))
                    t0 = j * TT + b * tsz
                    stq(y_o[t0:t0 + tsz, :], ys[0:tsz, :], r=[ys], w=['o_y'])
            phase_end()

        for l in range(DEPTH):
            if l >= NLAYERS_DBG[0]:
                break
            kind, slot = l % 3, l // 3
            norm_mod(l, 0)
            if DBG['mixer']:
                if kind == 0:
                    gdn(l, slot)
                elif kind == 1:
                    mla(l)
                else:
                    swa(l)
            norm_mod(l, 1)
            if DBG['ffn']:
                ffn(l)
        final()

    for si in range(NSEQ):
        if si in SEQS_DBG:
            run_seq(si)
    P.barrier()
    P.emit()
    es.close()
    return nc


_NC_CACHE = {}


def _consts(rel_bias):
    c = {}
    c['k_ident'] = np.eye(128, dtype=np.float32)
    for nm, ub, ch in (('p', 128, 64), ('s', 32, 32)):
        tri, blk, mA, mT = _gdn_masks(ub, ch)
        c['k_tri_' + nm], c['k_blk_' + nm], c['k_mA_' + nm], c['k_mT_' + nm] = tri, blk, mA, mT
    c['k_cos_p'], c['k_sin_p'] = _rope_tables(np.arange(TP))
    c['k_cos_s'], c['k_sin_s'] = _rope_tables(np.arange(TS) + PAST)
    rb = np.asarray(rel_bias, dtype=np.float32)
    i = np.arange(64)
    kr = np.arange(192)
    bk = _t5_bucket((128 + i)[None, :] - kr[:, None])
    t = rb[bk]
    t = t.reshape(192, 64, 4, 4).transpose(0, 2, 3, 1).reshape(192, 4, 256)
    c['k_ba_p'] = np.ascontiguousarray(t[0:128])
    ba1 = t[0:128].copy()
    ba1[0:64] = NEG
    c['k_ba1_p'] = ba1
    c['k_bb_p'] = np.ascontiguousarray(t[128:192])
    qpos = PAST + np.arange(TS)
    kpos = np.concatenate([PAST - 128 + np.arange(128), PAST + np.arange(TS)])
    bk = _t5_bucket(qpos[None, :] - kpos[:, None])
    t = rb[bk].reshape(160, TS, 4, 4).transpose(0, 2, 3, 1).reshape(160, 4, 4 * TS)
    c['k_ba_s'] = np.ascontiguousarray(t[0:128])
    c['k_bb_s'] = np.ascontiguousarray(t[128:160])
    return c


def kernel(**inputs):
    f = lambda k: np.ascontiguousarray(np.asarray(inputs[k], dtype=np.float32))
    if 'nc' not in _NC_CACHE:
        _NC_CACHE['nc'] = build_program()
    nc = _NC_CACHE['nc']
    consts = _consts(f('rel_bias'))
    shared = {}
    for k in ('ada_w', 'ada_b', 'norm1', 'norm2', 'final_norm', 'gdn_w_in', 'gdn_conv_w', 'gdn_a_log', 'gdn_dt_bias',
              'gdn_o_norm', 'gdn_w_o', 'mla_w_in', 'mla_q_norm', 'mla_kv_norm', 'mla_w_q_up', 'mla_w_kv_up', 'mla_w_o',
              'swa_w_in', 'swa_sinks', 'swa_w_o', 'ffn_w_in', 'ffn_conv_w', 'ffn_conv_b', 'ffn_w_out'):
        shared[k] = f(k)
    shared.update(consts)
    xp, xs, cpr, csm = f('x_prompt'), f('x_sample'), f('c_prompt'), f('c_sample')
    gconv, gS = f('state_gdn_conv'), f('state_gdn_S')
    clat, ckr = f('cache_mla_latent'), f('cache_mla_krope')
    csk, csv = f('cache_swa_k'), f('cache_swa_v')
    fconv = f('state_ffn_conv')
    in_maps = []
    for c in range(NCORES):
        ps = slice(c * NPS, (c + 1) * NPS)
        ss = slice(c * NSS, (c + 1) * NSS)
        m = dict(shared)
        m['xp'] = np.ascontiguousarray(xp[ps])
        m['xs'] = np.ascontiguousarray(xs[ss])
        m['c6'] = np.ascontiguousarray(np.concatenate([cpr[ps], csm[ss]], 0))
        m['st_gconv'] = np.ascontiguousarray(gconv[:, ss])
        m['st_gS'] = np.ascontiguousarray(gS[:, ss])
        m['c_lat'] = np.ascontiguousarray(clat[0, ss])
        m['c_kr'] = np.ascontiguousarray(ckr[0, ss])
        m['c_sk'] = np.ascontiguousarray(csk[0, ss].reshape(NSS, 128, 256))
        m['c_sv'] = np.ascontiguousarray(csv[0, ss].reshape(NSS, 128, 256))
        m['st_fconv'] = np.ascontiguousarray(fconv[:, ss])
        in_maps.append(m)
    res = run_bass_kernel_spmd(nc, in_maps, core_ids=list(range(NCORES)))
    R = res.results

    def cat(name, axis):
        return np.concatenate([np.asarray(R[c][name], dtype=np.float32) for c in range(NCORES)], axis=axis)

    y_p = cat('y_p', 0)
    y_s = cat('y_s', 0)
    outs = (
        y_p, y_s,
        cat('gconv_p', 1), cat('gconv_s', 1),
        cat('gS_p', 1), cat('gS_s', 1),
        cat('lat_p', 0)[None], cat('lat_s', 0)[None],
        cat('kr_p', 0)[None], cat('kr_s', 0)[None],
        cat('sk_p', 0).reshape(1, NCORES * NPS, 128, 4, 64), cat('sk_s', 0).reshape(1, NCORES * NSS, 128, 4, 64),
        cat('sv_p', 0).reshape(1, NCORES * NPS, 128, 4, 64), cat('sv_s', 0).reshape(1, NCORES * NSS, 128, 4, 64),
        cat('fconv_p', 1), cat('fconv_s', 1),
    )
    return tuple(np.ascontiguousarray(o) for o in outs)
```
